# Optimizing a Trainium2 kernel written in Bass

```python
import jax, jax.numpy as jnp
from jax import lax
import numpy as np

D_MODEL = 1024
BATCH = 8
SEQ = 4096
DEPTH = 2

GRID_W = 64
CTX_LEN = 256
CHUNK = 128
A_HEADS = 4
A_HEAD_DIM = 128
A_W = A_HEADS * A_HEAD_DIM
B_GROUPS = 4
B_GROUP_DIM = 128
B_W = B_GROUPS * B_GROUP_DIM
AB_IN = 3 * A_W + 2 * B_W
AB_OUT = A_W + B_W
C_HEADS = 16
C_KV_HEADS = 4
C_GROUP = C_HEADS // C_KV_HEADS
C_HEAD_DIM = 64
C_Q_W = C_HEADS * C_HEAD_DIM
C_KV_W = C_KV_HEADS * C_HEAD_DIM
C_IN = 2 * C_Q_W + 2 * C_KV_W
WINDOW = 128
Q_BLOCK = 128
ROPE_BASE = 10000.0
NORM_EPS = 1e-6
NEG_INF = -1e30
N_EVEN = (DEPTH + 1) // 2
N_ODD = DEPTH // 2

kernel_name = 'hybrid_gmlp_fnet_swa_prefix_trunk'


def rms_norm(x, g):
    xf = x.astype(jnp.float32)
    y = xf * lax.rsqrt(jnp.mean(xf * xf, axis=-1, keepdims=True) + NORM_EPS)
    return (y * g.astype(jnp.float32)).astype(x.dtype)


def layer_norm(x, g):
    xf = x.astype(jnp.float32)
    mu = jnp.mean(xf, axis=-1, keepdims=True)
    xc = xf - mu
    y = xc * lax.rsqrt(jnp.mean(xc * xc, axis=-1, keepdims=True) + NORM_EPS)
    return (y * g.astype(jnp.float32)).astype(x.dtype)


def adaln(cond, w, b):
    m = jax.nn.silu(cond) @ w + b
    return jnp.split(m, 3, axis=-1)


def rope_1d(x, pos):
    nf = x.shape[-1] // 2
    inv = ROPE_BASE ** (-jnp.arange(nf, dtype=jnp.float32) / nf)
    ang = pos.astype(jnp.float32)[:, None] * inv[None, :]
    cos = jnp.cos(ang)[None, :, None, :].astype(x.dtype)
    sin = jnp.sin(ang)[None, :, None, :].astype(x.dtype)
    x1, x2 = x[..., :nf], x[..., nf:]
    return jnp.concatenate([x1 * cos - x2 * sin, x2 * cos + x1 * sin], axis=-1)


def rope_2d(x, row, col):
    half = x.shape[-1] // 2
    return jnp.concatenate([rope_1d(x[..., :half], row), rope_1d(x[..., half:], col)], axis=-1)


def sink_softmax(scores, sink):
    s = jnp.broadcast_to(sink, scores.shape[:-1] + (1,))
    p = jax.nn.softmax(jnp.concatenate([s, scores], axis=-1), axis=-1)
    return p[..., 1:]


def mixer_ab(h, w_in, v_g, s_w, s_b, w_out):
    bsz, L, _ = h.shape
    z = h @ w_in
    u, v, g_a, x_b, g_b = jnp.split(z, [A_W, 2 * A_W, 3 * A_W, 3 * A_W + B_W], axis=-1)
    v = layer_norm(v, v_g).reshape(bsz, L // CHUNK, CHUNK, A_HEADS, A_HEAD_DIM)
    sv = jnp.einsum('bnphc,hqp->bnqhc', v, s_w) + s_b.T[:, :, None]
    y_a = u * sv.reshape(bsz, L, A_W) * jax.nn.silu(g_a)
    xb = x_b.reshape(bsz, L, B_GROUPS, B_GROUP_DIM).astype(jnp.float32)
    f = jnp.real(jnp.fft.fft2(xb, axes=(1, 3), norm='ortho')).astype(h.dtype).reshape(bsz, L, B_W)
    y_b = f * jax.nn.silu(g_b)
    return jnp.concatenate([y_a, y_b], axis=-1) @ w_out


def split_c(z):
    return jnp.split(z, [C_Q_W, C_Q_W + C_KV_W, C_Q_W + 2 * C_KV_W], axis=-1)


def window_attention(q, k, v, k_ctx, v_ctx, sink):
    bsz, S, _, hd = q.shape
    qg = q.reshape(bsz, S, C_KV_HEADS, C_GROUP, hd)
    pad = ((0, 0), (WINDOW, WINDOW), (0, 0), (0, 0))
    kp, vp = jnp.pad(k, pad), jnp.pad(v, pad)
    kb_len = Q_BLOCK + 2 * WINDOW
    scale = hd ** -0.5
    sink_b = sink.reshape(C_KV_HEADS, C_GROUP).astype(jnp.float32)[None, :, :, None, None]

    def one_block(n):
        start = n * Q_BLOCK
        qb = lax.dynamic_slice_in_dim(qg, start, Q_BLOCK, axis=1)
        kb = lax.dynamic_slice_in_dim(kp, start, kb_len, axis=1)
        vb = lax.dynamic_slice_in_dim(vp, start, kb_len, axis=1)
        qpos = start + jnp.arange(Q_BLOCK)
        kpos = start - WINDOW + jnp.arange(kb_len)
        valid = (jnp.abs(qpos[:, None] - kpos[None, :]) <= WINDOW) & (kpos[None, :] >= 0) & (kpos[None, :] < S)
        s_win = jnp.einsum('bqkgd,bjkd->bkgqj', qb, kb).astype(jnp.float32) * scale
        s_win = jnp.where(valid, s_win, NEG_INF)
        s_ctx = jnp.einsum('bqkgd,bjkd->bkgqj', qb, k_ctx).astype(jnp.float32) * scale
        p = sink_softmax(jnp.concatenate([s_ctx, s_win], axis=-1), sink_b)
        vals = jnp.concatenate([v_ctx, vb], axis=1)
        return jnp.einsum('bkgqj,bjkd->bqkgd', p.astype(v.dtype), vals)

    out = lax.map(one_block, jnp.arange(S // Q_BLOCK))
    return jnp.moveaxis(out, 0, 1).reshape(bsz, S, C_Q_W)


def context_attention(q, k, v, sink):
    bsz, L, _, hd = q.shape
    qg = q.reshape(bsz, L, C_KV_HEADS, C_GROUP, hd)
    s = jnp.einsum('bqkgd,bjkd->bkgqj', qg, k).astype(jnp.float32) * hd ** -0.5
    sink_b = sink.reshape(C_KV_HEADS, C_GROUP).astype(jnp.float32)[None, :, :, None, None]
    p = sink_softmax(s, sink_b)
    o = jnp.einsum('bkgqj,bjkd->bqkgd', p.astype(v.dtype), v)
    return o.reshape(bsz, L, C_Q_W)


def setup_inputs(seed: int = 0) -> dict:
    key = jax.random.key(seed)
    ks = jax.random.split(key, 16)

    def nrm(k, shape, s):
        return jax.random.normal(k, shape, jnp.float32) * s

    return {
        'x': nrm(ks[0], (BATCH, SEQ, D_MODEL), 1.0),
        'c': nrm(ks[1], (BATCH, D_MODEL), 1.0),
        'ctx': nrm(ks[2], (BATCH, CTX_LEN, D_MODEL), 1.0),
        'c_ctx': nrm(ks[3], (D_MODEL,), 1.0),
        'norm_g': 1.0 + nrm(ks[4], (DEPTH, D_MODEL), 0.02),
        'ada_w': nrm(ks[5], (DEPTH, D_MODEL, 3 * D_MODEL), 0.5 * D_MODEL ** -0.5),
        'ada_b': nrm(ks[6], (DEPTH, 3 * D_MODEL), 0.02),
        'w_in_ab': nrm(ks[7], (N_EVEN, D_MODEL, AB_IN), D_MODEL ** -0.5),
        'v_norm_g': 1.0 + nrm(ks[8], (N_EVEN, A_W), 0.02),
        'spatial_w': nrm(ks[9], (N_EVEN, A_HEADS, CHUNK, CHUNK), CHUNK ** -0.5),
        'spatial_b': 1.0 + nrm(ks[10], (N_EVEN, A_HEADS, CHUNK), 0.02),
        'w_out_ab': nrm(ks[11], (N_EVEN, AB_OUT, D_MODEL), AB_OUT ** -0.5),
        'w_in_c': nrm(ks[12], (N_ODD, D_MODEL, C_IN), D_MODEL ** -0.5),
        'sink_logit': nrm(ks[13], (N_ODD, C_HEADS), 0.5),
        'w_out_c': nrm(ks[14], (N_ODD, C_Q_W, D_MODEL), C_Q_W ** -0.5),
        'final_g': 1.0 + nrm(ks[15], (D_MODEL,), 0.02),
    }


def reference(x, c, ctx, c_ctx, norm_g, ada_w, ada_b, w_in_ab, v_norm_g, spatial_w, spatial_b,
              w_out_ab, w_in_c, sink_logit, w_out_c, final_g):
    bsz, S, _ = x.shape
    rows = S // GRID_W
    row = jnp.repeat(jnp.arange(rows), GRID_W)
    col = jnp.tile(jnp.arange(GRID_W), rows)

    for layer in range(DEPTH):
        need_ctx_out = layer < DEPTH - 1
        shift, scale, gate = adaln(c[:, None, :], ada_w[layer], ada_b[layer])
        shift_c, scale_c, gate_c = adaln(c_ctx, ada_w[layer], ada_b[layer])
        h = rms_norm(x, norm_g[layer]) * (1.0 + scale) + shift
        if layer % 2 == 0:
            i = layer // 2
            y = mixer_ab(h, w_in_ab[i], v_norm_g[i], spatial_w[i], spatial_b[i], w_out_ab[i])
            if need_ctx_out:
                hc = rms_norm(ctx, norm_g[layer]) * (1.0 + scale_c) + shift_c
                yc = mixer_ab(hc, w_in_ab[i], v_norm_g[i], spatial_w[i], spatial_b[i], w_out_ab[i])
                ctx = ctx + gate_c * yc
            x = x + gate * y
        else:
            i = layer // 2
            w_in = w_in_c[i]
            hc = rms_norm(ctx, norm_g[layer]) * (1.0 + scale_c) + shift_c
            q, k, v, g = split_c(h @ w_in)
            q = rope_2d(q.reshape(bsz, S, C_HEADS, C_HEAD_DIM), row, col)
            k = rope_2d(k.reshape(bsz, S, C_KV_HEADS, C_HEAD_DIM), row, col)
            v = v.reshape(bsz, S, C_KV_HEADS, C_HEAD_DIM)
            Lc = ctx.shape[1]
            if need_ctx_out:
                q_c, k_c, v_c, g_c = split_c(hc @ w_in)
            else:
                k_c, v_c = jnp.split(hc @ w_in[:, C_Q_W:C_Q_W + 2 * C_KV_W], 2, axis=-1)
            k_c = k_c.reshape(bsz, Lc, C_KV_HEADS, C_HEAD_DIM)
            v_c = v_c.reshape(bsz, Lc, C_KV_HEADS, C_HEAD_DIM)
            o = window_attention(q, k, v, k_c, v_c, sink_logit[i])
            y = (o * jax.nn.silu(g)) @ w_out_c[i]
            if need_ctx_out:
                o_c = context_attention(q_c.reshape(bsz, Lc, C_HEADS, C_HEAD_DIM), k_c, v_c, sink_logit[i])
                ctx = ctx + gate_c * ((o_c * jax.nn.silu(g_c)) @ w_out_c[i])
            x = x + gate * y

    return rms_norm(x, final_g)
```

```python
import numpy as np
import concourse.bass as bass
import concourse.mybir as mybir
from concourse.bass_utils import run_bass_kernel_spmd
from contextlib import ExitStack

F32 = mybir.dt.float32
BF16 = mybir.dt.bfloat16
AF = mybir.ActivationFunctionType
ALU = mybir.AluOpType

COMPUTE = ("pe", "act", "dve", "pool")
ALL_ENG = ("pe", "act", "dve", "pool", "sp")

D = 1024
S = 4096
LC = 256
NT = S // 128
EPS = 1e-6


class Buf:
    __slots__ = ("name", "last_w", "readers", "excl")

    def __init__(self, name):
        self.name = name
        self.last_w = None
        self.readers = []
        self.excl = False


class Op:
    __slots__ = ("eng", "fn", "deps", "signal", "sig_val", "is_dma", "dma_slot", "dma_val",
                 "pre_dma_wait")

    def __init__(self, eng, fn, is_dma=False):
        self.eng = eng
        self.fn = fn
        self.deps = []
        self.signal = False
        self.sig_val = None
        self.is_dma = is_dma
        self.dma_slot = None
        self.dma_val = None
        self.pre_dma_wait = None


class Prog:
    N_DMA_SEMS = 24
    STRICT_SAME_ENGINE = True

    def __init__(self, nc):
        self.nc = nc
        self.ops = {e: [] for e in ALL_ENG}
        self.dma_count = {e: 0 for e in ALL_ENG}
        self.stack = ExitStack()

    def sb(self, name, shape, dtype=F32):
        return self.stack.enter_context(self.nc.sbuf_tensor(name, list(shape), dtype))

    def ps(self, name, shape, dtype=F32):
        return self.stack.enter_context(self.nc.psum_tensor(name, list(shape), dtype))

    def _dep(self, op, w):
        if w is None or w is op:
            return
        w.signal = True
        op.deps.append(w)

    def add(self, eng, fn, reads=(), writes=(), is_dma=False):
        op = Op(eng, fn, is_dma)
        for b in reads:
            w = b.last_w
            if w is not None:
                self._dep(op, w)
            if b.excl:
                for r in b.readers:
                    if r.eng != eng:
                        self._dep(op, r)
        strict = self.STRICT_SAME_ENGINE and eng != "pe"
        for b in writes:
            w = b.last_w
            if w is not None and (w.eng != eng or w.is_dma or is_dma or strict):
                self._dep(op, w)
            for r in b.readers:
                if r.eng != eng or r.is_dma or is_dma or strict:
                    self._dep(op, r)
        for b in reads:
            if not is_dma:
                b.readers = [r for r in b.readers if r.eng != eng or r.is_dma]
            b.readers.append(op)
        for b in writes:
            b.last_w = op
            b.readers = []
        if is_dma:
            j = self.dma_count[eng]
            self.dma_count[eng] = j + 1
            op.dma_slot = j % self.N_DMA_SEMS
            op.dma_val = 16 * (j // self.N_DMA_SEMS + 1)
            op.pre_dma_wait = 16 * (j // self.N_DMA_SEMS)
        self.ops[eng].append(op)
        return op

    def dma(self, eng, out, in_, reads=(), writes=(), **kw):
        return self.add(eng, lambda e: e.dma_start(out=out, in_=in_, **kw), reads, writes, is_dma=True)

    def emit(self):
        nc = self.nc
        st = self.stack
        esem = {e: st.enter_context(nc.semaphore("s_" + e)) for e in COMPUTE}
        dsem = {}
        for e in ALL_ENG:
            if self.dma_count[e] > 0:
                dsem[e] = [st.enter_context(nc.semaphore("d_%s_%d" % (e, i)))
                           for i in range(min(self.N_DMA_SEMS, self.dma_count[e]))]
        for e in ALL_ENG:
            c = 0
            for op in self.ops[e]:
                if op.is_dma:
                    continue
                if op.signal:
                    c += 1
                    op.sig_val = c
        stats = {e: [0, 0] for e in ALL_ENG}

        def run(ename):
            def body(e):
                waited = {}
                for op in self.ops[ename]:
                    need = {}
                    for w in op.deps:
                        if w.is_dma:
                            key = ("d", w.eng, w.dma_slot)
                            val = w.dma_val
                        else:
                            key = ("e", w.eng)
                            val = w.sig_val
                        if need.get(key, 0) < val:
                            need[key] = val
                    if op.is_dma and op.pre_dma_wait > 0:
                        key = ("d", ename, op.dma_slot)
                        if need.get(key, 0) < op.pre_dma_wait:
                            need[key] = op.pre_dma_wait
                    for key, val in need.items():
                        if waited.get(key, 0) >= val:
                            continue
                        waited[key] = val
                        sem = esem[key[1]] if key[0] == "e" else dsem[key[1]][key[2]]
                        e.wait_ge(sem, val)
                        stats[ename][1] += 1
                    inst = op.fn(e)
                    stats[ename][0] += 1
                    if op.is_dma:
                        inst.then_inc(dsem[ename][op.dma_slot], 16)
                    elif op.signal:
                        inst.then_inc(esem[ename], 1)
                if ename in dsem:
                    nd = self.dma_count[ename]
                    for s in range(len(dsem[ename])):
                        cnt = (nd - 1 - s) // self.N_DMA_SEMS + 1 if nd > s else 0
                        if cnt > 0 and waited.get(("d", ename, s), 0) < 16 * cnt:
                            e.wait_ge(dsem[ename][s], 16 * cnt)
            return body

        with nc.Block() as block:
            if self.ops["sp"]:
                block.sync(run("sp"))
            if self.ops["pe"]:
                block.tensor(run("pe"))
            if self.ops["act"]:
                block.scalar(run("act"))
            if self.ops["dve"]:
                block.vector(run("dve"))
            if self.ops["pool"]:
                block.gpsimd(run("pool"))
        self.stats = stats
        st.close()


class T:
    def __init__(self, t, name, buf=None, base=0, shape=None):
        self.t = t
        self.name = name
        self.b = buf if buf is not None else Buf(name)
        self.rowfull = int(np.prod(list(t.shape)[1:]))
        self.base = base
        self.shape = list(shape) if shape is not None else list(t.shape)
        self.size = int(np.prod(self.shape[1:]))

    def ap(self, p0, npart, off, dims):
        return bass.AP(self.t, p0 * self.rowfull + self.base + off, [[self.rowfull, npart]] + [list(d) for d in dims])

    def view(self):
        if len(self.t.shape) != 2:
            assert self.base == 0
            return self.t
        v = self.t[:, self.base:self.base + self.size]
        if len(self.shape) == 3:
            v = v.rearrange("p (a b) -> p a b", a=self.shape[1], b=self.shape[2])
        elif len(self.shape) == 4:
            v = v.rearrange("p (a b c) -> p a b c", a=self.shape[1], b=self.shape[2], c=self.shape[3])
        return v

    def __getitem__(self, idx):
        return self.view()[idx]


class Arena:
    def __init__(self, prog, name, kib, parent=None):
        self.cap = int(kib * 1024)
        if parent is None:
            self.h = prog.sb(name, [128, self.cap // 2], BF16)
            self.views = {BF16: self.h, F32: self.h.bitcast(F32)}
            self.org = 0
        else:
            parent.off = (parent.off + 31) // 32 * 32
            assert parent.off + self.cap <= parent.cap, (name, parent.off, self.cap, parent.cap)
            self.views = parent.views
            self.org = parent.org + parent.off
            parent.off += self.cap
        self.parent = parent
        self.off = 0
        self.live = []
        self.children = []
        if parent is not None:
            parent.children.append(self)

    def all_live(self):
        out = list(self.live)
        for c in self.children:
            out += c.all_live()
        return out

    def reset(self):
        old = self.all_live()
        self.off = 0
        self.live = []
        self.children = []
        return old

    def alloc(self, name, shape, dt=F32):
        esz = 2 if dt == BF16 else 4
        n = int(np.prod(list(shape)[1:]))
        self.off = (self.off + 31) // 32 * 32
        assert self.off + n * esz <= self.cap, (name, self.off, n * esz, self.cap)
        t = T(self.views[dt], name, None, (self.org + self.off) // esz, shape)
        self.off += n * esz
        self.live.append(t)
        return t


def _tables():
    tb = {}
    tb["ident"] = np.eye(128, dtype=np.float32)
    r2 = np.arange(2)[:, None, None, None]
    a = np.arange(64)[None, :, None, None]
    j = np.arange(32)[None, None, :, None]
    k1 = np.arange(64)[None, None, None, :]
    n = 64 * a + 2 * j + r2
    th = 2 * np.pi * ((k1 * n) % 4096) / 4096.0
    t1 = np.concatenate([np.cos(th), -np.sin(th)], axis=-1) / 8.0
    tb["tab1"] = np.ascontiguousarray(t1.reshape(128, 32 * 128)).astype(np.float32)
    r = np.arange(64)[:, None]
    k2 = np.arange(64)[None, :]
    ph = 2 * np.pi * ((r * k2) % 64) / 64.0
    t2 = np.concatenate([np.cos(ph), np.sin(ph)], axis=1) / 8.0
    tb["tab2"] = np.concatenate([t2, t2], axis=0).astype(np.float32)
    c = np.arange(128)[:, None]
    c2 = np.arange(128)[None, :]
    al = 2 * np.pi * ((c * c2) % 128) / 128.0
    Cc, Sc = np.cos(al) / np.sqrt(128.0), np.sin(al) / np.sqrt(128.0)
    tb["fc"] = np.concatenate([Cc, -Sc, Sc, Cc], axis=1).astype(np.float32)
    nn = np.arange(256)[:, None]
    kk = np.arange(256)[None, :]
    be = 2 * np.pi * ((nn * kk) % 256) / 256.0
    tc = np.concatenate([np.cos(be), -np.sin(be)], axis=1) / 16.0
    tb["tabc"] = np.ascontiguousarray(tc.reshape(2, 128, 512).transpose(1, 0, 2).reshape(128, 1024)).astype(np.float32)
    tok = np.arange(S)
    row, col = tok // 64, tok % 64
    inv = 10000.0 ** (-np.arange(16) / 16.0)
    dd = np.arange(64)
    pos = np.where(dd[None, :] < 32, row[:, None], col[:, None]).astype(np.float64)
    ang = (pos.astype(np.float32) * inv[dd % 16][None, :].astype(np.float32)).astype(np.float32).astype(np.float64)
    cs = np.cos(ang)
    sn = np.sin(ang) * np.where((dd % 32) < 16, -1.0, 1.0)[None, :]
    tb["rope_cos"] = np.ascontiguousarray(cs.reshape(NT, 128, 64).transpose(1, 0, 2).reshape(128, NT * 64)).astype(np.float32)
    tb["rope_sin"] = np.ascontiguousarray(sn.reshape(NT, 128, 64).transpose(1, 0, 2).reshape(128, NT * 64)).astype(np.float32)
    jj = np.arange(128)[:, None]
    ii = np.arange(128)[None, :]
    tb["wmask"] = np.concatenate([(jj >= ii), (jj <= ii)], axis=1).astype(np.float32)
    return tb


class Builder:
    def __init__(self, mode):
        self.mode = mode
        self.stop = None
        self.nt_l1 = None
        self.cur_arena = None
        self.nc = bass.Bass("TRN2", target_bir_lowering=False)
        self.p = Prog(self.nc)
        self.rr = {}

    def din(self, name, shape, dt=F32):
        return self.nc.dram_tensor(name, list(shape), dt, kind="ExternalInput")

    def dout(self, name, shape, dt=F32):
        return self.nc.dram_tensor(name, list(shape), dt, kind="ExternalOutput")

    def sbt(self, name, shape, dt=F32, buf=None):
        if self.cur_arena is not None:
            return self.cur_arena.alloc(name, shape, dt)
        return T(self.p.sb("s_" + name, shape, dt), name, buf)

    def pst(self, name, shape, dt=F32):
        t = T(self.p.ps("p_" + name, shape, dt), name)
        t.b.excl = True
        return t

    def pick(self, key, engines):
        i = self.rr.get(key, 0)
        self.rr[key] = i + 1
        return engines[i % len(engines)]

    def mm(self, out, lhsT, rhs, start, stop, reads, writes):
        self.p.add("pe", lambda e: e.matmul(out=out, lhsT=lhsT, rhs=rhs, start=start, stop=stop), reads, writes)

    def tr(self, out, in_, ident, reads, writes):
        self.p.add("pe", lambda e: e.transpose(out=out, in_=in_, identity=ident), reads, writes)

    def act(self, out, in_, func, reads, writes, scale=None, bias=None, accum=None):
        kw = {}
        if scale is not None:
            kw["scale"] = scale
        if bias is not None:
            kw["bias"] = bias
        if accum is not None:
            kw["accum_out"] = accum
        self.p.add("act", lambda e: e.activation(out=out, in_=in_, func=func, **kw), reads, writes)

    def tt(self, eng, out, in0, in1, op, reads, writes):
        self.p.add(eng, lambda e: e.tensor_tensor(out=out, in0=in0, in1=in1, op=op), reads, writes)

    def ts(self, eng, out, in0, s1, s2, op0, op1, reads, writes):
        if op1 is None:
            self.p.add(eng, lambda e: e.tensor_scalar(out=out, in0=in0, scalar1=s1, scalar2=None, op0=op0), reads, writes)
        else:
            self.p.add(eng, lambda e: e.tensor_scalar(out=out, in0=in0, scalar1=s1, scalar2=s2, op0=op0, op1=op1), reads, writes)

    def cp(self, eng, out, in_, reads, writes):
        if eng == "act":
            self.p.add("act", lambda e: e.copy(out=out, in_=in_), reads, writes)
        else:
            self.p.add(eng, lambda e: e.tensor_copy(out=out, in_=in_), reads, writes)

    def load(self, out, in_, writes, reads=(), q=None):
        self.p.dma(q if q is not None else "sp", out, in_, reads=reads, writes=writes)

    def loadw(self, out, in_, writes):
        self.load(out, in_, writes, q=self.pick("wq", ["sp", "act"]))

    def dump(self, name, t, ap=None, shape=None, dt=F32):
        shape = list(shape if shape is not None else t.shape)
        d = self.dout("dbg_" + name, shape, dt)
        self.p.dma("sp", d.ap(), ap if ap is not None else t[:], reads=[t.b])

    def fence(self, tiles):
        bufs = [t.b for t in tiles]
        self.p.add("pool", lambda e: e.memset(self.fz[:, 0:1], 0.0), writes=bufs + [self.fz.b])

    def retarget(self, old_tiles, new_tiles):
        if not old_tiles:
            return
        bufs = [t.b for t in old_tiles] + [t.b for t in new_tiles]
        self.p.add("pool", lambda e: e.memset(self.fz[:, 0:1], 0.0), writes=bufs + [self.fz.b])

    def setup_common(self, l0=True):
        b = self
        self.d_cc = self.din("cc", [128, 16])
        self.d_ng = self.din("ng", [128, 16])
        self.d_ident = self.din("ident", [128, 128])
        self.d_ada_w = self.din("ada_w", [2, D, 3 * D])
        self.d_ada_b = self.din("ada_b", [2, 3 * D])
        self.fz = b.sbt("fz", [128, 8])
        self.identF = b.sbt("identF", [128, 128])
        self.identB = b.sbt("identB", [128, 128], BF16)
        b.load(self.identF[:], self.d_ident.ap(), [self.identF.b])
        b.cp("dve", self.identB[:], self.identF[:], [self.identF.b], [self.identB.b])
        self.onesF = b.sbt("onesF", [128, 128])
        self.p.add("pool", lambda e: e.memset(self.onesF[:], 1.0), writes=[self.onesF.b])
        self.cc = b.sbt("cc_t", [128, 16])
        self.ng = b.sbt("ng_t", [128, 16])
        b.load(self.cc[:], self.d_cc.ap(), [self.cc.b])
        b.load(self.ng[:], self.d_ng.ap(), [self.ng.b])
        self.sc = b.sbt("sc_t", [128, 16])
        b.act(self.sc[:], self.cc[:], AF.Silu, [self.cc.b], [self.sc.b])
        self.junk = b.sbt("junk", [128, D], BF16)
        self.mhalf = b.sbt("mhalf", [128, 1])
        self.p.add("pool", lambda e: e.memset(self.mhalf[:], -0.5), writes=[self.mhalf.b])
        self.pZ = [b.pst("pZ%d" % i, [128, 512]) for i in range(4)]
        self.pS = b.pst("pS", [128, 512])
        self.pX = b.pst("pX", [128, 1024], BF16)
        self.pA = b.pst("pA", [128, 1024], BF16)
        self.pB = b.pst("pB", [128, 1024], BF16)
        self.pXs = [self.pX]
        self.d_ngrow = self.din("ngrow", [2, D])
        self.pZ4 = [T(z.t.reshape([128, 4, 128]), "pZ4", z.b) for z in self.pZ]
        self.xn = [b.sbt("xn%d" % i, [128, D], BF16) for i in range(3)]
        self.ss = [b.sbt("ss%d" % i, [128, 4]) for i in range(4)]


    def make_scdup(self, ar, scdup=None):
        b = self
        if scdup is None:
            scdup = ar.alloc("scdup", [128, 8, 128])
        b.cp("dve", scdup[:, :, 0:64], self.sc.ap(0, 128, 0, [[1, 8], [0, 64]]), [self.sc.b], [scdup.b])
        b.cp("dve", scdup[:, :, 64:128], self.sc.ap(0, 128, 8, [[1, 8], [0, 64]]), [self.sc.b], [scdup.b])
        return scdup

    def adaln(self, l, mod, scdup, awst, adab, pAda, pTm):
        b = self
        NB = 256
        for cb in range(3 * D // NB):
            st = awst[cb % len(awst)]
            ab = adab[cb % len(adab)]
            src = bass.AP(self.d_ada_w, l * D * 3 * D + cb * NB, [[3 * D, 128], [128 * 3 * D, 8], [1, NB]])
            b.loadw(st[:], src, [st.b])
            b.load(ab[:], bass.AP(self.d_ada_b, l * 3 * D + cb * NB, [[0, 128], [1, NB]]), [ab.b])
            pa = pAda[cb % len(pAda)]
            for k in range(8):
                b.mm(pa[:, 0:NB], scdup[:, k, :], st[:, k, :], k == 0, k == 7, [scdup.b, st.b], [pa.b])
            b.tt("dve", mod[:, cb * NB:(cb + 1) * NB], pa[:, 0:NB], ab[:], ALU.add, [pa.b, ab.b], [mod.b])
        modT = b.sbt("modT%d" % l, [128, 2, 16])
        for q in range(4):
            pt = pTm[q % len(pTm)]
            for i in range(4):
                ch = q * 4 + i
                b.tr(pt[:, i, :], mod[:, ch * 128:(ch + 1) * 128], self.identF[:], [mod.b, self.identF.b], [pt.b])
            b.cp("dve", modT.ap(0, 128, q * 4, [[16, 2], [1, 4]]), pt.ap(0, 128, 0, [[64, 2], [128, 4]]), [pt.b], [modT.b])
        AB = b.sbt("AB%d" % l, [128, 2, 16])
        for xc in range(2):
            self.p.add("dve", lambda e, xc=xc: e.scalar_tensor_tensor(
                out=AB[:, xc, 0:8], in0=modT[:, xc, 8:16], scalar=1.0, in1=self.ng[:, l * 8:(l + 1) * 8],
                op0=ALU.add, op1=ALU.mult), [modT.b, self.ng.b], [AB.b])
            b.cp("dve", AB[:, xc, 8:16], modT[:, xc, 0:8], [modT.b], [AB.b])
        return AB

    def gate_bc(self, g, mod, xc, pAda):
        b = self
        p0 = 64 * xc
        for cb in range(2):
            pa = pAda[cb % len(pAda)]
            b.mm(pa[:], self.onesF[p0:p0 + 1, :], mod[p0:p0 + 1, 2 * D + cb * 512:2 * D + (cb + 1) * 512], True, True,
                 [self.onesF.b, mod.b], [pa.b])
            b.cp("act", g[:, cb * 512:(cb + 1) * 512], pa[:], [pa.b], [g.b])

    def rstd_from_ss(self, ss):
        b = self
        b.ts("pool", ss[:, 1:2], ss[:, 0:1], 1.0 / D, EPS, ALU.mult, ALU.add, [ss.b], [ss.b])
        b.tt("pool", ss[:, 3:4], ss[:, 1:2], self.mhalf[:, 0:1], ALU.pow, [ss.b, self.mhalf.b], [ss.b])

    def next(self, key, lst):
        i = self.rr.get(key, 0)
        self.rr[key] = i + 1
        return lst[i % len(lst)]

    def load_x(self, dsrc, r0):
        x_ = self.next("xt", self.xt)
        self.load(x_[:], dsrc.ap()[r0:r0 + 128, :], [x_.b])
        return x_

    def sumsq(self, x_, s_, eng="dve"):
        if eng == "act":
            self.act(self.junk[:], x_[:], AF.Square, [x_.b], [self.junk.b, s_.b], accum=s_[:, 0:1])
            return
        self.p.add("dve", lambda e: e.scalar_tensor_tensor(out=self.junk[:], in0=x_[:], scalar=1.0, in1=x_[:], op0=ALU.mult,
                                                          op1=ALU.mult, accum_out=s_[:, 0:1]), [x_.b], [self.junk.b, s_.b])

    def hT_prep(self, x_, Arow=None, rstd=None, save_rstd=None, sq_eng="dve", n_=None):
        b = self
        if n_ is None:
            n_ = self.next("xn", self.xn)
        if rstd is None:
            s_ = self.next("ss", self.ss)
            b.sumsq(x_, s_, sq_eng)
            b.rstd_from_ss(s_)
            rs, rsb = s_[:, 3:4], s_.b
            if save_rstd is not None:
                sap, sbuf = save_rstd
                b.cp("pool", sap, s_[:, 3:4], [s_.b], [sbuf])
        else:
            rs, rsb = rstd
        if Arow is not None:
            self.p.add("dve", lambda e: e.scalar_tensor_tensor(out=n_[:], in0=x_[:], scalar=rs, in1=Arow[:],
                                                              op0=ALU.mult, op1=ALU.mult), [x_.b, rsb, Arow.b], [n_.b])
        else:
            b.ts("dve", n_[:], x_[:], rs, None, ALU.mult, None, [x_.b, rsb], [n_.b])
        return n_

    def hT_fin(self, n_, AB, xc, dst, fused_A):
        b = self
        dap, dbuf = dst
        pX = self.next("pX", self.pXs)
        for c in range(8):
            b.tr(pX[:, c * 128:(c + 1) * 128], n_[:, c * 128:(c + 1) * 128], self.identB[:], [n_.b, self.identB.b], [pX.b])
        pv = pX.ap(0, 128, 0, [[128, 8], [1, 128]])
        if not fused_A:
            b.tt("dve", dap, pv, AB.ap(0, 128, xc * 16, [[1, 8], [0, 128]]), ALU.mult, [pX.b, AB.b], [dbuf])
            b.tt("pool", dap, dap, AB.ap(0, 128, xc * 16 + 8, [[1, 8], [0, 128]]), ALU.add, [dbuf, AB.b], [dbuf])
        else:
            b.tt("dve", dap, pv, AB.ap(0, 128, xc * 16 + 8, [[1, 8], [0, 128]]), ALU.add, [pX.b, AB.b], [dbuf])

    def hT_tile(self, x_, AB, xc, dst, Arow=None, rstd=None, save_rstd=None, sq_eng="dve", n_=None):
        n_ = self.hT_prep(x_, Arow, rstd, save_rstd, sq_eng, n_)
        self.hT_fin(n_, AB, xc, dst, Arow is not None)

    def arow_bc(self, Arow, mod, l, pAda, gbc):
        b = self
        for cb in range(2):
            pa = pAda[cb % len(pAda)]
            b.mm(pa[:], self.onesF[0:1, :], mod[0:1, D + cb * 512:D + (cb + 1) * 512], True, True,
                 [self.onesF.b, mod.b], [pa.b])
            self.p.add("dve", lambda e, pa=pa, cb=cb: e.scalar_tensor_tensor(
                out=Arow[:, cb * 512:(cb + 1) * 512], in0=pa[:], scalar=1.0, in1=gbc[:, cb * 512:(cb + 1) * 512],
                op0=ALU.add, op1=ALU.mult), [pa.b, gbc.b], [Arow.b])

    def layer0(self, d_x, d_ctx, d_x1, d_ctx1):
        b = self
        p = self.p
        pZ, pS, pX, pZ4 = self.pZ, self.pS, self.pX, self.pZ4
        self.xt = [b.sbt("xt%d" % i, [128, D]) for i in range(4)]
        self.hT = [b.sbt("hT%d" % i, [128, 8, 512], BF16) for i in range(2)]
        hT = self.hT
        d_win = self.din("w_in_ab", [D, 2560])
        d_wout = self.din("w_out_ab", [D, D])
        d_vg = self.din("v_norm_g", [1, 512])
        d_sw = self.din("spatial_w", [4, 128, 128])
        d_sb = self.din("spatial_b", [1, 512])
        d_tab1 = self.din("tab1", [128, 32 * 128])
        d_tab2 = self.din("tab2", [128, 128])
        d_fc = self.din("fc", [128, 512])
        d_tabc = self.din("tabc", [128, 1024])

        arX = Arena(p, "arX", 40, parent=self.cur_arena)
        arY = Arena(p, "arY", 32, parent=self.cur_arena)
        self.arX, self.arY = arX, arY
        win = b.sbt("win", [128, 8, 2560], BF16)
        gx0 = b.sbt("gx0", [128, D])
        Arow0 = b.sbt("Arow0", [128, D])
        rstd_all = b.sbt("rstd_all", [128, NT])
        swT = b.sbt("swT", [128, 4, 128], BF16)
        self.pXs = [self.pX, self.pA]
        sbk = b.sbt("sbk", [33, 512], BF16)
        onesk = b.sbt("onesk", [33, 128], BF16)
        vg = b.sbt("vg", [128, 512])
        arZ = Arena(p, "arZ", 8, parent=self.cur_arena)
        tab1 = arZ.alloc("tab1", [128, 32, 128], BF16)
        tab2 = b.sbt("tab2", [128, 128], BF16)
        fc = b.sbt("fc", [128, 512], BF16)
        u_t = [b.sbt("u_t%d" % i, [128, 512]) for i in range(2)]
        sg_t = [b.sbt("sg_t%d" % i, [128, 512]) for i in range(2)]
        vh_t = [b.sbt("vh_t%d" % i, [128, 512]) for i in range(2)]
        vb_t = [b.sbt("vb_t%d" % i, [128, 512], BF16) for i in range(2)]
        ya_t = [b.sbt("ya_t%d" % i, [128, 512], BF16) for i in range(2)]
        st6 = [b.sbt("st6_%d" % i, [128, 8]) for i in range(2)]
        yT_t = [b.sbt("yT_t%d" % i, [128, 8, 128], BF16) for i in range(2)]
        sgm_t = [b.sbt("sgm_t%d" % i, [128, 512], BF16) for i in range(2)]
        ybg_t = [b.sbt("ybg_t%d" % i, [128, 4, 512], BF16) for i in range(2)]

        woutc = arX.alloc("woutc", [128, 8, D], BF16)
        gc0 = arX.alloc("gc0", [128, D])
        sbf = arX.alloc("sbf", [33, 512])
        tmpb = arX.alloc("tmpb", [33, 512], BF16)
        adab = [arX.alloc("adab%d" % i, [128, 256]) for i in range(2)]
        xbc = [arX.alloc("xbc%d" % j, [128, 512], BF16) for j in range(2)]
        sgbc = [arX.alloc("sgbc%d" % j, [128, 512], BF16) for j in range(2)]
        ZT = [arX.alloc("ZT%d" % g, [128, 512], BF16) for g in range(4)]
        tabc = arX.alloc("tabc", [128, 2, 512], BF16)
        hTc = arX.alloc("hTc", [128, 8, 256], BF16)
        awst = [arY.alloc("awst%d" % i, [128, 8, 256]) for i in range(2)]
        scdup = b.make_scdup(arY)
        mod0 = arY.alloc("mod0", [128, 3 * D])

        AB0 = b.adaln(0, mod0, scdup, awst, adab, pZ[0:2], pZ4[2:4])
        b.gate_bc(gx0, mod0, 0, pZ[0:2])
        b.gate_bc(gc0, mod0, 1, pZ[0:2])
        gbc = T(awst[0].t, "gbc", awst[0].b, awst[0].base, [128, D])
        b.load(gbc[:], bass.AP(self.d_ngrow, 0, [[0, 128], [1, D]]), [gbc.b])
        b.arow_bc(Arow0, mod0, 0, pZ[0:2], gbc)

        engs3 = ["dve", "act"]
        stv = [T(a.t, "stv", a.b, a.base, [128, 2048]) for a in awst]
        for k in range(8):
            for hh in range(2):
                st = stv[(k * 2 + hh) % 2]
                b.loadw(st[:, 0:1280], d_win.ap()[k * 128:(k + 1) * 128, hh * 1280:(hh + 1) * 1280], [st.b])
                b.cp(b.pick("wcast", engs3), win[:, k, hh * 1280:(hh + 1) * 1280], st[:, 0:1280], [st.b], [win.b])
        for k in range(8):
            st = stv[k % 2]
            b.loadw(st[:, 0:D], d_wout.ap()[k * 128:(k + 1) * 128, :], [st.b])
            b.tt("pool", woutc[:, k, :], st[:, 0:D], gc0[:], ALU.mult, [st.b, gc0.b], [woutc.b])
        st = stv[0]
        b.load(st[:, 0:512].rearrange("p (h q) -> p h q", h=4), d_sw.ap().rearrange("h q p -> q h p"), [st.b])
        for h in range(4):
            b.tr(pZ4[2][:, h, :], st[:, h * 128:(h + 1) * 128], self.identF[:], [st.b, self.identF.b], [pZ4[2].b])
        b.cp("dve", swT[:], pZ4[2][:], [pZ4[2].b], [swT.b])
        b.load(sbf[0:1, :], d_sb.ap(), [sbf.b])
        b.load(sbf[32:33, :], d_sb.ap(), [sbf.b])
        p.add("pool", lambda e: e.memset(sbk[:], 0.0), writes=[sbk.b])
        p.add("pool", lambda e: e.memset(onesk[:], 1.0), writes=[onesk.b])
        b.cp("dve", sbk[0:1, :], sbf[0:1, :], [sbf.b], [sbk.b])
        b.cp("dve", tmpb[32:33, :], sbf[32:33, :], [sbf.b], [tmpb.b])
        b.tt("dve", sbk[32:33, :], sbf[32:33, :], tmpb[32:33, :], ALU.subtract, [sbf.b, tmpb.b], [sbk.b])
        b.load(vg[:], bass.AP(d_vg, 0, [[0, 128], [1, 512]]), [vg.b])
        for q in range(2):
            st = stv[q % 2]
            b.load(st[:, 0:2048], d_tab1.ap()[:, q * 2048:(q + 1) * 2048], [st.b])
            b.cp(b.pick("wcast", engs3), tab1.ap(0, 128, q * 2048, [[1, 2048]]), st[:, 0:2048], [st.b], [tab1.b])
        st = stv[0]
        b.load(st[:, 0:128], d_tab2.ap(), [st.b])
        b.cp("dve", tab2[:], st[:, 0:128], [st.b], [tab2.b])
        st = stv[1]
        b.load(st[:, 0:512], d_fc.ap(), [st.b])
        b.cp("dve", fc[:], st[:, 0:512], [st.b], [fc.b])
        st = stv[0]
        b.load(st[:, 0:1024], d_tabc.ap(), [st.b])
        b.cp("dve", tabc.ap(0, 128, 0, [[1, 1024]]), st[:, 0:1024], [st.b], [tabc.b])

        if self.stop == "pro":
            b.dump("AB0", AB0)
            b.dump("mod0", mod0)
            b.dump("gx0", gx0)
            b.dump("gc0", gc0)
            b.dump("win", win, dt=BF16)
            b.dump("woutc", woutc, dt=BF16)
            b.dump("swT", swT, dt=BF16)
            b.dump("sbk", sbk, dt=BF16)
            b.dump("tab1", tab1, dt=BF16)
            b.dump("tabc", tabc, dt=BF16)
            return
        def a_branch(hTm, j, slot):
            pu, pv, pg = pZ[0], pZ[1], pZ[2]
            for (pz, c0) in ((pv, 512), (pg, 1024), (pu, 0)):
                for k in range(8):
                    b.mm(pz[:], hTm[:, k, j * 128:(j + 1) * 128], win[:, k, c0:c0 + 512], k == 0, k == 7,
                         [hTm.b, win.b], [pz.b])
            s6 = st6[slot]
            vh, vb, sg, u, ya = vh_t[slot], vb_t[slot], sg_t[slot], u_t[slot], ya_t[slot]
            p.add("dve", lambda e: e.bn_stats(out=s6[:, 0:6], in_=pv[:]), [pv.b], [s6.b])
            p.add("dve", lambda e: e.bn_aggr(out=s6[:, 6:8], in_=s6[:, 0:6]), [s6.b], [s6.b])
            b.ts("pool", s6[:, 0:1], s6[:, 7:8], EPS, None, ALU.add, None, [s6.b], [s6.b])
            b.tt("pool", s6[:, 2:3], s6[:, 0:1], self.mhalf[:, 0:1], ALU.pow, [s6.b, self.mhalf.b], [s6.b])
            b.ts("dve", vh[:], pv[:], s6[:, 6:7], s6[:, 2:3], ALU.subtract, ALU.mult, [pv.b, s6.b], [vh.b])
            b.tt("pool", vb[:], vh[:], vg[:], ALU.mult, [vh.b, vg.b], [vb.b])
            b.act(sg[:], pg[:], AF.Silu, [pg.b], [sg.b])
            b.tt("dve", u[:], pu[:], sg[:], ALU.mult, [pu.b, sg.b], [u.b])
            for h in range(4):
                b.mm(pS[:, h * 128:(h + 1) * 128], swT[:, h, :], vb[:, h * 128:(h + 1) * 128], True, False,
                     [swT.b, vb.b], [pS.b])
                b.mm(pS[:, h * 128:(h + 1) * 128], sbk[0:33, h * 128:(h + 1) * 128], onesk[0:33, :], False, True,
                     [sbk.b, onesk.b], [pS.b])
            b.tt("dve", ya[:], pS[:], u[:], ALU.mult, [pS.b, u.b], [ya.b])
            return ya

        def out_proj(lhs_list, wmat, x_):
            for cb in range(2):
                pz = pZ[cb]
                for k in range(8):
                    ap_, bf_ = lhs_list[k]
                    b.mm(pz[:], ap_, wmat[:, k, cb * 512:(cb + 1) * 512], k == 0, k == 7, [bf_, wmat.b], [pz.b])
                b.tt("dve", x_[:, cb * 512:(cb + 1) * 512], pz[:], x_[:, cb * 512:(cb + 1) * 512], ALU.add,
                     [pz.b, x_.b], [x_.b])

        def ctx_gen():
            for j in range(2):
                x_ = b.load_x(d_ctx, j * 128)
                b.hT_tile(x_, AB0, 1, (hTc.ap(0, 128, j * 128, [[256, 8], [1, 128]]), hTc.b), sq_eng="act", n_=self.junk)
                yield
            yac = []
            for j in range(2):
                yac.append(a_branch(hTc, j, j))
                yield
                for (pz, c0) in ((pZ[3], 1536), (pZ[2], 2048)):
                    for k in range(8):
                        b.mm(pz[:], hTc[:, k, j * 128:(j + 1) * 128], win[:, k, c0:c0 + 512], k == 0, k == 7,
                             [hTc.b, win.b], [pz.b])
                b.cp("dve", xbc[j][:], pZ[3][:], [pZ[3].b], [xbc[j].b])
                b.act(sgbc[j][:], pZ[2][:], AF.Silu, [pZ[2].b], [sgbc[j].b])
                yield
            for g in range(4):
                pz = pZ[g % 2]
                for j in range(2):
                    b.mm(pz[:], xbc[j][:, g * 128:(g + 1) * 128], tabc[:, j, :], j == 0, j == 1, [xbc[j].b, tabc.b], [pz.b])
                b.cp(b.pick("zt", ["act", "dve"]), ZT[g][:], pz[:], [pz.b], [ZT[g].b])
                yield
            for kt in range(2):
                pz = pZ[2 + kt]
                for g in range(4):
                    b.mm(pz[:, g * 128:(g + 1) * 128], ZT[g][:, kt * 128:(kt + 1) * 128], fc[:, 0:128], True, False,
                         [ZT[g].b, fc.b], [pz.b])
                    b.mm(pz[:, g * 128:(g + 1) * 128], ZT[g][:, 256 + kt * 128:256 + (kt + 1) * 128], fc[:, 256:384], False, True,
                         [ZT[g].b, fc.b], [pz.b])
                ybc = vb_t[kt]
                b.tt("dve", ybc[:], pz[:], sgbc[kt][:], ALU.mult, [pz.b, sgbc[kt].b], [ybc.b])
                yield
                yT = yT_t[kt]
                for c in range(4):
                    b.tr(pX[:, c * 128:(c + 1) * 128], yac[kt][:, c * 128:(c + 1) * 128], self.identB[:], [yac[kt].b, self.identB.b], [pX.b])
                for c in range(4):
                    b.tr(pX[:, (4 + c) * 128:(5 + c) * 128], ybc[:, c * 128:(c + 1) * 128], self.identB[:], [ybc.b, self.identB.b], [pX.b])
                b.cp("act", yT.ap(0, 128, 0, [[1, 1024]]), pX[:], [pX.b], [yT.b])
                xc_ = b.load_x(d_ctx, kt * 128)
                yield
                out_proj([(yT[:, k, :], yT.b) for k in range(8)], woutc, xc_)
                p.dma("sp", d_ctx1.ap()[kt * 128:(kt + 1) * 128, :], xc_[:], reads=[xc_.b])
                yield

        ctxg = ctx_gen()
        oldY = arY.reset()
        xbT = [arY.alloc("xbT%d" % g, [128, S], BF16) for g in range(4)]
        b.retarget(oldY, xbT)
        prepB = {}

        def HB1(t):
            if t >= NT:
                return
            x_ = b.load_x(d_x, t * 128)
            prepB[t] = b.hT_prep(x_, Arow=Arow0, save_rstd=(rstd_all[:, t:t + 1], rstd_all.b), sq_eng="act")

        def HB2(t):
            if t >= NT:
                return
            m, j = t // 4, t % 4
            hTm = hT[m % 2]
            b.hT_fin(prepB.pop(t), AB0, 0, (hTm.ap(0, 128, j * 128, [[512, 8], [1, 128]]), hTm.b), True)

        HB1(0)
        HB1(1)
        for j in range(4):
            HB1(j + 2)
            HB2(j)
        HB2(4)
        for m in range(8):
            hTm = hT[m % 2]
            for g in range(4):
                c0 = 1536 + g * 128
                pz = self.next("pzB", pZ)
                for k in range(8):
                    b.mm(pz[:], win[:, k, c0:c0 + 128], hTm[:, k, :], k == 0, k == 7, [win.b, hTm.b], [pz.b])
                b.cp("act", xbT[g][:, m * 512:(m + 1) * 512], pz[:], [pz.b], [xbT[g].b])
                t2 = 4 * (m + 1) + g + 1
                HB1(t2 + 1)
                if g < 3:
                    HB2(t2)
                next(ctxg, None)
            HB2(4 * (m + 2))
        for _ in ctxg:
            pass

        if self.stop == "passB":
            for g in range(4):
                b.dump("xbT%d" % g, xbT[g], dt=BF16)
            return
        oldX = arX.reset()
        GT = arX.alloc("GT", [128, 8192], BF16)
        XR = arX.alloc("XR", [128, 32, 128], BF16)
        TT = arX.alloc("TT", [128, 32, 256], BF16)
        b.retarget(oldX, [GT, XR, TT])
        if self.stop == "m0":
            b.dump("xbT0", xbT[0], dt=BF16)
            return
        XRb = [Buf("XR%d" % i) for i in range(4)]
        TTb = [Buf("TT%d" % i) for i in range(4)]
        for bb in XRb:
            bb.last_w = XR.b.last_w
        for bb in TTb:
            bb.last_w = TT.b.last_w
        pB32 = T(self.pB.t.bitcast(F32), "pB32", self.pB.b)
        bpool = [pZ[0], pZ[1], pZ[2], pZ[3], pS, pB32]
        self.pXs = [self.pX, self.pA]

        def M1(g, q):
            pX_ = self.next("pX", self.pXs)
            for jj in range(8):
                j = q * 8 + jj
                for r2 in range(2):
                    src = xbT[g].ap(0, 128, 2 * j + r2, [[64, 64]])
                    b.tr(pX_[64 * r2:64 * r2 + 64, jj * 128:(jj + 1) * 128], src, self.identB[:],
                         [xbT[g].b, self.identB.b], [pX_.b])
            b.cp(b.pick("m1ev", ["act", "dve"]), XR.ap(0, 128, q * 1024, [[1, 1024]]), pX_[:], [pX_.b], [XRb[q]])

        def S1(g, q):
            pzs = (self.next("bp", bpool), self.next("bp", bpool))
            for jj in range(4):
                j = q * 4 + jj
                for r2 in range(2):
                    b.mm(pzs[r2][:, jj * 128:(jj + 1) * 128], XR[64 * r2:64 * r2 + 64, j, :],
                         tab1[64 * r2:64 * r2 + 64, j, :], True, True, [XRb[q // 2], tab1.b], [pzs[r2].b])
            for r2 in range(2):
                dst = GT.ap(0, 128, (q * 8 + r2) * 64, [[4096, 2], [128, 4], [1, 64]])
                srcp = pzs[r2].ap(0, 128, 0, [[64, 2], [128, 4], [1, 64]])
                b.cp(b.pick("s1ev", ["act", "dve"]), dst, srcp, [pzs[r2].b], [GT.b])

        def M2(g, qq):
            pz = self.next("bp", bpool)
            for h2 in range(2):
                q = qq * 2 + h2
                for k1p in range(2):
                    k1 = 2 * q + k1p
                    l0 = GT.ap(0, 128, k1, [[64, 64]])
                    l1 = GT.ap(0, 128, 4096 + k1, [[64, 64]])
                    o_ = pz[64 * k1p:64 * k1p + 64, h2 * 256:(h2 + 1) * 256]
                    b.mm(o_, l0, fc[:, 0:256], True, False, [GT.b, fc.b], [pz.b])
                    b.mm(o_, l1, fc[:, 256:512], False, True, [GT.b, fc.b], [pz.b])
            b.cp(b.pick("m2ev", ["act", "dve"]), TT.ap(0, 128, qq * 512, [[1, 512]]), pz[:], [pz.b], [TTb[qq // 4]])

        def S2(g, i4):
            pzs = (self.next("bp", bpool), self.next("bp", bpool))
            for ql in range(8):
                q = i4 * 8 + ql
                for k1p in range(2):
                    pz = pzs[k1p]
                    b.mm(pz[:, ql * 64:(ql + 1) * 64], TT[64 * k1p:64 * k1p + 64, q, 0:128],
                         tab2[64 * k1p:64 * k1p + 64, 0:64], True, False, [TTb[i4], tab2.b], [pz.b])
                    b.mm(pz[:, ql * 64:(ql + 1) * 64], TT[64 * k1p:64 * k1p + 64, q, 128:256],
                         tab2[64 * k1p:64 * k1p + 64, 64:128], False, True, [TTb[i4], tab2.b], [pz.b])
            for k1p in range(2):
                srcp = pzs[k1p].ap(0, 128, 0, [[64, 8], [1, 64]])
                dst = xbT[g].ap(0, 128, 16 * i4 + k1p, [[2, 8], [64, 64]])
                b.cp(b.pick("s2ev", ["act", "dve"]), dst, srcp, [pzs[k1p].b], [xbT[g].b])

        for q in range(4):
            M1(0, q)
        for q in range(8):
            S1(0, q)
        for g in range(4):
            for q in range(4):
                for i in range(4):
                    M2(g, 4 * q + i)
                if g + 1 < 4:
                    M1(g + 1, q)
            for i4 in range(4):
                S2(g, i4)
                if g + 1 < 4:
                    S1(g + 1, 2 * i4)
                    S1(g + 1, 2 * i4 + 1)
        XR.b.last_w = XRb[3].last_w
        XR.b.readers = [r for bb in XRb for r in bb.readers]
        TT.b.last_w = TTb[3].last_w
        TT.b.readers = [r for bb in TTb for r in bb.readers]

        if self.stop == "Bpipe":
            for g in range(4):
                b.dump("fT%d" % g, xbT[g], dt=BF16)
            return
        oldX = arX.reset()
        wout = arX.alloc("wout", [128, 8, D], BF16)
        wst = [arX.alloc("wst%d" % i, [128, D]) for i in range(2)]
        b.retarget(oldX, [wout] + wst)
        for k in range(8):
            st = wst[k % 2]
            b.load(st[:], d_wout.ap()[k * 128:(k + 1) * 128, :], [st.b])
            b.tt(b.pick("wos", ["dve", "pool"]), wout[:, k, :], st[:], gx0[:], ALU.mult, [st.b, gx0.b], [wout.b])
        oldZ = arZ.reset()
        xr = [self.xt[3]] + [arZ.alloc("xr%d" % i, [128, D]) for i in range(2)]
        b.retarget(oldZ, xr[1:])
        xnorm = self.xt[0:3]
        pA32 = T(self.pA.t.bitcast(F32), "pA32", self.pA.b)
        sets = [(pZ[0], pZ[1], pZ[2]), (pZ[3], pS, pA32)]
        pG = T(self.pB.t.bitcast(F32), "pG", self.pB.b)
        self.pXs = [self.pX]

        prepA = {}

        def HA1(t):
            if t >= NT:
                return
            x_ = self.next("xnorm", xnorm)
            b.load(x_[:], d_x.ap()[t * 128:(t + 1) * 128, :], [x_.b])
            prepA[t] = b.hT_prep(x_, Arow=Arow0, rstd=(rstd_all[:, t:t + 1], rstd_all.b))

        def HA2(t):
            if t >= NT:
                return
            m, j = t // 4, t % 4
            hTm = hT[m % 2]
            b.hT_fin(prepA.pop(t), AB0, 0, (hTm.ap(0, 128, j * 128, [[512, 8], [1, 128]]), hTm.b), True)

        def GA(m):
            if m >= 8:
                return
            hTm = hT[m % 2]
            ybg = ybg_t[m % 2]
            for g in range(4):
                c0 = 2048 + g * 128
                for k in range(8):
                    b.mm(pG[:], win[:, k, c0:c0 + 128], hTm[:, k, :], k == 0, k == 7, [win.b, hTm.b], [pG.b])
                sgm = sgm_t[g % 2]
                b.act(sgm[:], pG[:], AF.Silu, [pG.b], [sgm.b])
                b.tt("pool", ybg[:, g, :], xbT[g][:, m * 512:(m + 1) * 512], sgm[:], ALU.mult, [xbT[g].b, sgm.b], [ybg.b])

        def inpA(t, c0, pz):
            hTm = hT[(t // 4) % 2]
            j = t % 4
            for k in range(8):
                b.mm(pz[:], hTm[:, k, j * 128:(j + 1) * 128], win[:, k, c0:c0 + 512], k == 0, k == 7, [hTm.b, win.b], [pz.b])

        def VA(t):
            if t >= NT:
                return
            pv = sets[t % 2][1]
            inpA(t, 512, pv)
            s6, vh, vb = st6[t % 2], vh_t[t % 2], vb_t[t % 2]
            p.add("dve", lambda e: e.bn_stats(out=s6[:, 0:6], in_=pv[:]), [pv.b], [s6.b])
            p.add("dve", lambda e: e.bn_aggr(out=s6[:, 6:8], in_=s6[:, 0:6]), [s6.b], [s6.b])
            b.ts("pool", s6[:, 0:1], s6[:, 7:8], EPS, None, ALU.add, None, [s6.b], [s6.b])
            b.tt("pool", s6[:, 2:3], s6[:, 0:1], self.mhalf[:, 0:1], ALU.pow, [s6.b, self.mhalf.b], [s6.b])
            b.ts("dve", vh[:], pv[:], s6[:, 6:7], s6[:, 2:3], ALU.subtract, ALU.mult, [pv.b, s6.b], [vh.b])
            b.tt("pool", vb[:], vh[:], vg[:], ALU.mult, [vh.b, vg.b], [vb.b])

        def GaA(t):
            if t >= NT:
                return
            pg = sets[t % 2][2]
            inpA(t, 1024, pg)
            b.act(sg_t[t % 2][:], pg[:], AF.Silu, [pg.b], [sg_t[t % 2].b])

        def UA(t):
            if t >= NT:
                return
            pu = sets[t % 2][0]
            inpA(t, 0, pu)
            b.tt("dve", u_t[t % 2][:], pu[:], sg_t[t % 2][:], ALU.mult, [pu.b, sg_t[t % 2].b], [u_t[t % 2].b])

        def SA(t):
            pv = sets[t % 2][1]
            vb, u, ya = vb_t[t % 2], u_t[t % 2], ya_t[t % 2]
            for h in range(4):
                b.mm(pv[:, h * 128:(h + 1) * 128], swT[:, h, :], vb[:, h * 128:(h + 1) * 128], True, False,
                     [swT.b, vb.b], [pv.b])
                b.mm(pv[:, h * 128:(h + 1) * 128], sbk[0:33, h * 128:(h + 1) * 128], onesk[0:33, :], False, True,
                     [sbk.b, onesk.b], [pv.b])
            b.tt("dve", ya[:], pv[:], u[:], ALU.mult, [pv.b, u.b], [ya.b])

        def TyA(t):
            ya, yT = ya_t[t % 2], yT_t[t % 2]
            pX = self.pX
            for c in range(4):
                b.tr(pX[:, c * 128:(c + 1) * 128], ya[:, c * 128:(c + 1) * 128], self.identB[:], [ya.b, self.identB.b], [pX.b])
            b.cp("act", yT.ap(0, 128, 0, [[1, 512]]), pX[:, 0:512], [pX.b], [yT.b])

        xres = {}

        def LX(t):
            if t >= NT:
                return
            x_ = self.next("xres", xr)
            b.load(x_[:], d_x.ap()[t * 128:(t + 1) * 128, :], [x_.b])
            xres[t] = x_

        def OA(t):
            yT, ybg = yT_t[t % 2], ybg_t[(t // 4) % 2]
            j = t % 4
            x_ = xres.pop(t)
            banks = (sets[t % 2][0], sets[t % 2][2])
            for cb in range(2):
                pz = banks[cb]
                for k in range(8):
                    if k < 4:
                        ap_, bf_ = yT[:, k, :], yT.b
                    else:
                        ap_, bf_ = ybg[:, k - 4, j * 128:(j + 1) * 128], ybg.b
                    b.mm(pz[:], ap_, wout[:, k, cb * 512:(cb + 1) * 512], k == 0, k == 7, [bf_, wout.b], [pz.b])
                b.tt("dve", x_[:, cb * 512:(cb + 1) * 512], pz[:], x_[:, cb * 512:(cb + 1) * 512], ALU.add,
                     [pz.b, x_.b], [x_.b])
            p.dma("sp", d_x1.ap()[t * 128:(t + 1) * 128, :], x_[:], reads=[x_.b])

        HA1(0)
        for j in range(4):
            HA1(j + 1)
            HA2(j)
        GA(0)
        LX(0)
        LX(1)
        VA(0)
        GaA(0)
        UA(0)
        for t in range(NT):
            m, j = t // 4, t % 4
            LX(t + 2)
            VA(t + 1)
            SA(t)
            GaA(t + 1)
            TyA(t)
            UA(t + 1)
            HA1(t + 5)
            HA2(t + 4)
            OA(t)
            if j == 3:
                GA(m + 1)

    def layer1(self, d_x1, d_ctx1, d_out, ar):
        b = self
        p = self.p
        pZ, pS, pX, pZ4 = self.pZ, self.pS, self.pX, self.pZ4
        d_win = self.din("w_in_c", [D, 2560])
        d_wout = self.din("w_out_c", [D, D])
        d_sink = self.din("sink_logit", [1, 16])
        d_fg = self.din("final_g", [1, D])
        d_cos = self.din("rope_cos", [128, NT * 64])
        d_sin = self.din("rope_sin", [128, NT * 64])
        d_mask = self.din("wmask", [128, 256])

        winc = b.sbt("winc", [128, 8, 2560], BF16)
        wo1 = b.sbt("wo1", [128, 8, D], BF16)
        cosT = b.sbt("cosT", [128, NT * 64])
        sinT = b.sbt("sinT", [128, NT * 64])
        maskb = b.sbt("maskb", [128, 256], BF16)
        esink = b.sbt("esink", [128, 16])
        fg = b.sbt("fg", [128, D])
        gx1 = b.sbt("gx1", [128, D])
        NR = 4
        kTd = b.sbt("kTd", [128, 4, NR, 128], BF16)
        Vp = b.sbt("Vp", [128, NR, 4, 65], BF16)
        kcT = b.sbt("kcT", [128, 4, 2, 128], BF16)
        Vc = b.sbt("Vc", [128, 2, 4, 65], BF16)
        qT = [b.sbt("qT%d" % i, [128, 8, 128], BF16) for i in range(3)]
        sg = [b.sbt("sg1_%d" % i, [128, D]) for i in range(3)]
        hT1 = [b.sbt("hT1_%d" % i, [128, 8, 128], BF16) for i in range(2)]
        t1 = [b.sbt("rt1_%d" % i, [128, 512]) for i in range(2)]
        t2 = [b.sbt("rt2_%d" % i, [128, 512]) for i in range(2)]
        qr = [b.sbt("qr%d" % i, [128, D], BF16) for i in range(2)]
        krd = [b.sbt("krd%d" % i, [128, 4, 2, 64], BF16) for i in range(2)]
        den = [b.sbt("den%d" % i, [128, 8]) for i in range(2)]
        on_t = [b.sbt("on%d" % i, [128, 256]) for i in range(2)]
        og = [b.sbt("og%d" % i, [128, D], BF16) for i in range(2)]
        ogT = [b.sbt("ogT%d" % i, [128, 8, 128], BF16) for i in range(2)]
        ss2 = [b.sbt("ss2_%d" % i, [128, 4]) for i in range(2)]
        pSt = [pZ[2], pZ[3], pS]
        pO = [T(self.pA.t.bitcast(F32), "pO0", self.pA.b)]
        self.pXs = [self.pX, self.pB]
        Arow1 = b.sbt("Arow1", [128, D])

        old = ar.reset() + list(getattr(self, "l1_old", []))
        l1_new = [winc, wo1, cosT, sinT, maskb, esink, fg, gx1, Arow1, kTd, Vp, kcT, Vc] + qT + sg + hT1 + t1 + t2 + qr + krd + den \
            + on_t + og + ogT + ss2
        mod1 = ar.alloc("mod1", [128, 3 * D])
        nst = 4 if ar.cap >= 50 * 1024 else 2
        awst = [ar.alloc("awst1_%d" % i, [128, 8, 256]) for i in range(nst)]
        scd = ar.alloc("scdup1", [128, 8, 128])
        adab = [ar.alloc("adab1_%d" % i, [128, 256]) for i in range(2)]
        b.retarget(old, l1_new + [mod1, scd] + awst + adab)
        scdup = b.make_scdup(ar, scd)
        AB1 = b.adaln(1, mod1, scdup, awst, adab, pZ[0:2], pZ4[2:4])
        b.gate_bc(gx1, mod1, 0, pZ[0:2])
        gbc = T(awst[0].t, "gbc1", awst[0].b, awst[0].base, [128, D])
        b.load(gbc[:], bass.AP(self.d_ngrow, D, [[0, 128], [1, D]]), [gbc.b])
        b.arow_bc(Arow1, mod1, 1, pZ[0:2], gbc)
        engs3 = ["dve", "act"]
        stv = [T(a.t, "stv1", a.b, a.base, [128, 2048]) for a in awst]
        for k in range(8):
            for hh in range(2):
                st = stv[(k * 2 + hh) % len(stv)]
                b.loadw(st[:, 0:1280], d_win.ap()[k * 128:(k + 1) * 128, hh * 1280:(hh + 1) * 1280], [st.b])
                b.cp(b.pick("wcast", engs3), winc[:, k, hh * 1280:(hh + 1) * 1280], st[:, 0:1280], [st.b], [winc.b])
        for k in range(8):
            st = stv[k % len(stv)]
            b.loadw(st[:, 0:D], d_wout.ap()[k * 128:(k + 1) * 128, :], [st.b])
            b.tt(b.pick("wos", ["dve", "pool"]), wo1[:, k, :], st[:, 0:D], gx1[:], ALU.mult, [st.b, gx1.b], [wo1.b])
        b.load(cosT[:], d_cos.ap(), [cosT.b])
        b.load(sinT[:], d_sin.ap(), [sinT.b])
        st = stv[0]
        b.load(st[:, 0:256], d_mask.ap(), [st.b])
        b.cp("dve", maskb[:], st[:, 0:256], [st.b], [maskb.b])
        b.load(esink[:], bass.AP(d_sink, 0, [[0, 128], [1, 16]]), [esink.b])
        b.act(esink[:], esink[:], AF.Exp, [esink.b], [esink.b])
        b.ts("dve", esink[:], esink[:], 2.0, None, ALU.mult, None, [esink.b], [esink.b])
        b.load(fg[:], bass.AP(d_fg, 0, [[0, 128], [1, D]]), [fg.b])
        p.add("pool", lambda e: e.memset(Vp[:], 1.0), writes=[Vp.b])
        p.add("pool", lambda e: e.memset(Vc[:], 1.0), writes=[Vc.b])
        for kk in krd:
            p.add("pool", lambda e, kk=kk: e.memset(kk[:], 0.0), writes=[kk.b])
        oldp = ar.reset()
        x1t = [ar.alloc("x1t%d" % i, [128, D]) for i in range(5)]
        PT = [ar.alloc("PT%d" % i, [128, 512], BF16) for i in range(12)]
        b.retarget(oldp, x1t + PT)

        def hT_tile1(x_, xc, hTt):
            b.hT_tile(x_, AB1, xc, (hTt.ap(0, 128, 0, [[128, 8], [1, 128]]), hTt.b), Arow=(Arow1 if xc == 0 else None))

        def inproj(hTt, c0, pz, ncols=512):
            for k in range(8):
                b.mm(pz[:, 0:ncols], hTt[:, k, :], winc[:, k, c0:c0 + ncols], k == 0, k == 7, [hTt.b, winc.b], [pz.b])

        def rope(pz, col0, nh, t, outs):
            a1 = self.next("rt1", t1)
            a2 = self.next("rt2", t2)
            n = nh * 64
            cosb = cosT.ap(0, 128, t * 64, [[0, nh], [1, 64]])
            b.tt("dve", a1.ap(0, 128, 0, [[64, nh], [1, 64]]), pz.ap(0, 128, col0, [[64, nh], [1, 64]]), cosb, ALU.mult,
                 [pz.b, cosT.b], [a1.b])
            for hf in range(2):
                o_ = a2.ap(0, 128, hf * 16, [[64, nh], [32, 2], [1, 16]])
                i_ = pz.ap(0, 128, col0 + (1 - hf) * 16, [[64, nh], [32, 2], [1, 16]])
                s_ = sinT.ap(0, 128, t * 64 + hf * 16, [[0, nh], [32, 2], [1, 16]])
                b.tt("dve", o_, i_, s_, ALU.mult, [pz.b, sinT.b], [a2.b])
            for o_ in outs:
                oap, obuf = o_[0], o_[1]
                dims = o_[2] if len(o_) > 2 else [[64, nh], [1, 64]]
                b.tt("pool", oap, a1.ap(0, 128, 0, dims), a2.ap(0, 128, 0, dims), ALU.add, [a1.b, a2.b], [obuf])

        for j in range(2):
            x_ = self.next("x1t", x1t)
            b.load(x_[:], d_ctx1.ap()[j * 128:(j + 1) * 128, :], [x_.b])
            hTt = hT1[j % 2]
            hT_tile1(x_, 1, hTt)
            pz = pZ[j % 2]
            inproj(hTt, 1024, pz)
            kc = krd[j % 2]
            b.cp("dve", kc.ap(0, 128, 0, [[320, 2], [128, 2], [1, 64]]), pz.ap(0, 128, 0, [[128, 2], [64, 2], [1, 64]]),
                 [pz.b], [kc.b])
            b.cp("act", Vc.ap(0, 128, j * 260, [[65, 4], [1, 64]]), pz.ap(0, 128, 256, [[64, 4], [1, 64]]), [pz.b], [Vc.b])
            for kh in range(4):
                b.tr(pX[:, kh * 128:(kh + 1) * 128], kc.ap(0, 128, kh * 128, [[1, 128]]), self.identB[:], [kc.b, self.identB.b], [pX.b])
            b.cp("act", kcT.ap(0, 128, j * 128, [[256, 4], [1, 128]]), pX.ap(0, 128, 0, [[128, 4], [1, 128]]), [pX.b], [kcT.b])

        def stageA(t):
            x_ = self.next("x1t", x1t)
            xs[t] = x_
            b.load(x_[:], d_x1.ap()[t * 128:(t + 1) * 128, :], [x_.b])
            yield
            hTt = hT1[t % 2]
            n_ = self.next("xn", self.xn)
            s0 = self.next("ss", self.ss)
            b.sumsq(x_, s0, "act")
            b.rstd_from_ss(s0)
            self.p.add("dve", lambda e: e.scalar_tensor_tensor(out=n_[:], in0=x_[:], scalar=s0[:, 3:4], in1=Arow1[:],
                                                              op0=ALU.mult, op1=ALU.mult), [x_.b, s0.b, Arow1.b], [n_.b])
            yield
            pX = self.next("pXr", self.pXs)
            for c in range(8):
                b.tr(pX[:, c * 128:(c + 1) * 128], n_[:, c * 128:(c + 1) * 128], self.identB[:], [n_.b, self.identB.b], [pX.b])
            b.tt("dve", hTt.ap(0, 128, 0, [[128, 8], [1, 128]]), pX.ap(0, 128, 0, [[128, 8], [1, 128]]),
                 AB1.ap(0, 128, 8, [[1, 8], [0, 128]]), ALU.add, [pX.b, AB1.b], [hTt.b])
            yield
            q_ = qr[t % 2]
            for qb in range(2):
                pz = self.next("pzin", pZ[0:2])
                inproj(hTt, qb * 512, pz)
                rope(pz, 0, 8, t, [(q_.ap(0, 128, qb * 512, [[64, 8], [1, 64]]), q_.b)])
                yield
            pz = self.next("pzin", pZ[0:2])
            inproj(hTt, 1024, pz)
            kc = krd[t % 2]
            slot = t % NR
            b.cp("act", Vp.ap(0, 128, slot * 260, [[65, 4], [1, 64]]), pz.ap(0, 128, 256, [[64, 4], [1, 64]]), [pz.b], [Vp.b])
            rope(pz, 0, 4, t, [(kc.ap(0, 128, 0, [[320, 2], [128, 2], [1, 64]]), kc.b, [[128, 2], [64, 2], [1, 64]])])
            yield
            s_ = sg[t % 3]
            for gb in range(2):
                pz = self.next("pzin", pZ[0:2])
                inproj(hTt, 1536 + gb * 512, pz)
                b.act(s_[:, gb * 512:(gb + 1) * 512], pz[:], AF.Tanh, [pz.b], [s_.b], scale=0.5)
                p.add("dve", lambda e, s_=s_, pz=pz, gb=gb: e.scalar_tensor_tensor(
                    out=s_[:, gb * 512:(gb + 1) * 512], in0=s_[:, gb * 512:(gb + 1) * 512], scalar=1.0, in1=pz[:],
                    op0=ALU.add, op1=ALU.mult), [s_.b, pz.b], [s_.b])
                yield
            pX = self.next("pXr", self.pXs)
            for h in range(16):
                r0 = 64 * (h // 8)
                b.tr(pX[r0:r0 + 64, (h % 8) * 128:(h % 8 + 1) * 128], q_[:, h * 64:(h + 1) * 64], self.identB[:],
                     [q_.b, self.identB.b], [pX.b])
            b.cp("act", qT[t % 3].ap(0, 128, 0, [[1, 1024]]), pX[:], [pX.b], [qT[t % 3].b])
            pX = self.next("pXr", self.pXs)
            for kh in range(4):
                b.tr(pX[:, kh * 128:(kh + 1) * 128], kc.ap(0, 128, kh * 128, [[1, 128]]), self.identB[:], [kc.b, self.identB.b], [pX.b])
            b.cp("dve", kTd.ap(0, 128, slot * 128, [[NR * 128, 4], [1, 128]]), pX.ap(0, 128, 0, [[128, 4], [1, 128]]), [pX.b], [kTd.b])
            yield

        def stageB(n):
            x_ = xs.pop(n)
            qTn = qT[n % 3]
            s_ = sg[n % 3]
            o_ = og[n % 2]
            blocks = [("c", 0, None), ("c", 1, None)]
            if n > 0:
                blocks.append(("w", (n - 1) % NR, 0))
            blocks.append(("w", n % NR, None))
            if n < NT - 1:
                blocks.append(("w", (n + 1) % NR, 1))
            rounds = [blocks[i:i + 2] for i in range(0, len(blocks), 2)]
            ptss = {}

            def QK(kh):
                pts = {}
                rt = 64 * (kh // 2)
                c4 = 4 * (kh % 2)
                for (kind, idx, mk) in blocks:
                    ps_ = self.next("pSt", pSt)
                    pt = self.next("PT", PT)
                    if kind == "c":
                        lhs = kcT[:, kh, idx, :]
                        lb = kcT.b
                    else:
                        lhs = kTd[:, kh, idx, :]
                        lb = kTd.b
                    b.mm(ps_[:], lhs, qTn[:, c4:c4 + 4, :], True, True, [lb, qTn.b], [ps_.b])
                    b.act(pt[:], ps_[:], AF.Exp, [ps_.b], [pt.b], scale=0.125)
                    if mk is not None:
                        b.tt(b.pick("mask_eng", ["dve", "pool"]), pt.ap(0, 128, 0, [[128, 4], [1, 128]]), pt.ap(0, 128, 0, [[128, 4], [1, 128]]),
                             maskb.ap(0, 128, mk * 128, [[0, 4], [1, 128]]), ALU.mult, [pt.b, maskb.b], [pt.b])
                    pts[(kind, idx)] = pt
                ptss[kh] = pts

            def PV(kh):
                pts = ptss[kh]
                po = pO[0]
                for hl in range(4):
                    for bi2, (kind, idx, mk) in enumerate(blocks):
                        pt = pts[(kind, idx)]
                        if kind == "c":
                            rhs = Vc[:, idx, kh, :]
                            rb = Vc.b
                        else:
                            rhs = Vp[:, idx, kh, :]
                            rb = Vp.b
                        b.mm(po[:, hl * 65:(hl + 1) * 65], pt[:, hl * 128:(hl + 1) * 128], rhs,
                             bi2 == 0, bi2 == len(blocks) - 1, [pt.b, rb], [po.b])
                dn = den[kh % 2]
                p.add("dve", lambda e, dn=dn, po=po, kh=kh: e.scalar_tensor_tensor(
                    out=dn[:, 0:4], in0=po.ap(0, 128, 64, [[65, 4]]), scalar=2.0, in1=esink[:, 4 * kh:4 * kh + 4],
                    op0=ALU.mult, op1=ALU.add), [po.b, esink.b], [dn.b])
                p.add("dve", lambda e, dn=dn: e.reciprocal(out=dn[:, 4:8], in_=dn[:, 0:4]), [dn.b], [dn.b])
                ot = on_t[kh % 2]
                b.tt("dve", ot.ap(0, 128, 0, [[64, 4], [1, 64]]), po.ap(0, 128, 0, [[65, 4], [1, 64]]),
                     dn.ap(0, 128, 4, [[1, 4], [0, 64]]), ALU.mult, [po.b, dn.b], [ot.b])
                b.tt("pool", o_[:, kh * 256:(kh + 1) * 256], ot[:], s_[:, kh * 256:(kh + 1) * 256], ALU.mult, [ot.b, s_.b], [o_.b])

            QK(0)
            yield
            QK(1)
            yield
            PV(0)
            yield
            QK(2)
            yield
            PV(1)
            yield
            QK(3)
            yield
            PV(2)
            yield
            PV(3)
            yield
            oT = ogT[n % 2]
            pX = self.next("pXr", self.pXs)
            for c in range(8):
                b.tr(pX[:, c * 128:(c + 1) * 128], o_[:, c * 128:(c + 1) * 128], self.identB[:], [o_.b, self.identB.b], [pX.b])
            b.cp("act", oT.ap(0, 128, 0, [[1, 1024]]), pX[:], [pX.b], [oT.b])
            yield
            for cb in range(2):
                pz = self.next("pzin", pZ[0:2])
                for k in range(8):
                    b.mm(pz[:], oT[:, k, :], wo1[:, k, cb * 512:(cb + 1) * 512], k == 0, k == 7, [oT.b, wo1.b], [pz.b])
                b.tt("dve", x_[:, cb * 512:(cb + 1) * 512], pz[:], x_[:, cb * 512:(cb + 1) * 512], ALU.add, [pz.b, x_.b], [x_.b])
            yield
            s2 = ss2[n % 2]
            b.sumsq(x_, s2, "act")
            b.rstd_from_ss(s2)
            p.add("dve", lambda e: e.scalar_tensor_tensor(out=x_[:], in0=x_[:], scalar=s2[:, 3:4], in1=fg[:],
                                                          op0=ALU.mult, op1=ALU.mult), [x_.b, s2.b, fg.b], [x_.b])
            p.dma("sp", d_out.ap()[n * 128:(n + 1) * 128, :], x_[:], reads=[x_.b])
            yield

        xs = {}
        nt_run = self.nt_l1 if self.nt_l1 is not None else NT
        nb_run = nt_run if nt_run == NT else nt_run - 1
        gA, gB = {}, {}

        def stepA(t):
            if 0 <= t < nt_run:
                if t not in gA:
                    gA[t] = stageA(t)
                next(gA[t], None)

        def stepB(n):
            if 0 <= n < nb_run:
                if n not in gB:
                    gB[n] = stageB(n)
                next(gB[n], None)

        order = "lbAbbAbbAbbAbaAobAnb"
        stepA(0)
        stepA(0)
        stepA(0)
        for t in range(nt_run + 3):
            for ch in order:
                if ch == "a" or ch == "l" or ch == "n":
                    stepA(t + 1)
                elif ch == "A":
                    stepA(t)
                elif ch == "o":
                    stepB(t - 3)
                else:
                    stepB(t - 2)


def build_l0(stop=None):
    B = Builder("l0")
    B.stop = stop
    d_x = B.din("x", [S, D])
    d_ctx = B.din("ctx", [LC, D])
    d_x1 = B.dout("x1", [S, D])
    d_ctx1 = B.dout("ctx1", [LC, D])
    B.setup_common()
    B.layer0(d_x, d_ctx, d_x1, d_ctx1)
    B.p.emit()
    print("l0 stats", B.p.stats)
    return B


_TB = None


L0_KEYS = ("x", "ctx", "cc", "ng", "ngrow", "ident", "ada_w", "ada_b", "w_in_ab", "w_out_ab", "v_norm_g", "spatial_w",
           "spatial_b", "tab1", "tab2", "fc", "tabc")
L1_KEYS = ("cc", "ng", "ngrow", "ident", "ada_w", "ada_b", "w_in_c", "w_out_c", "sink_logit", "final_g", "rope_cos",
           "rope_sin", "wmask")


def host_inputs(inputs):
    global _TB
    if _TB is None:
        _TB = _tables()
    f = lambda a: np.ascontiguousarray(np.asarray(a, dtype=np.float32))
    x, c, ctx, c_ctx = f(inputs["x"]), f(inputs["c"]), f(inputs["ctx"]), f(inputs["c_ctx"])
    norm_g = f(inputs["norm_g"])
    common = {
        "ident": _TB["ident"], "ada_w": f(inputs["ada_w"]), "ada_b": f(inputs["ada_b"]),
        "w_in_ab": f(inputs["w_in_ab"])[0], "w_out_ab": f(inputs["w_out_ab"])[0],
        "v_norm_g": f(inputs["v_norm_g"]).reshape(1, 512),
        "spatial_w": f(inputs["spatial_w"])[0], "spatial_b": f(inputs["spatial_b"]).reshape(1, 512),
        "tab1": _TB["tab1"], "tab2": _TB["tab2"], "fc": _TB["fc"], "tabc": _TB["tabc"],
        "w_in_c": f(inputs["w_in_c"])[0], "w_out_c": f(inputs["w_out_c"])[0],
        "sink_logit": f(inputs["sink_logit"]).reshape(1, 16), "final_g": f(inputs["final_g"]).reshape(1, D),
        "rope_cos": _TB["rope_cos"], "rope_sin": _TB["rope_sin"], "wmask": _TB["wmask"],
    }
    ng = np.concatenate([norm_g[0].reshape(8, 128).T, norm_g[1].reshape(8, 128).T], axis=1)
    maps = []
    for bi in range(x.shape[0]):
        cc = np.concatenate([c[bi].reshape(8, 128).T, c_ctx.reshape(8, 128).T], axis=1)
        m = dict(common)
        m.update({"x": x[bi], "ctx": ctx[bi], "cc": f(cc), "ng": f(ng), "ngrow": norm_g})
        maps.append(m)
    return maps


def build_l1(nt=None):
    B = Builder("l1")
    B.nt_l1 = nt
    d_x1 = B.din("x1", [S, D])
    d_ctx1 = B.din("ctx1", [LC, D])
    d_out = B.dout("out", [S, D])
    B.setup_common(l0=False)
    ar = Arena(B.p, "arL1", 36)
    B.layer1(d_x1, d_ctx1, d_out, ar)
    B.p.emit()
    print("l1 stats", B.p.stats)
    return B


def build_fused():
    B = Builder("fused")
    nc = B.nc
    d_x = B.din("x", [S, D])
    d_ctx = B.din("ctx", [LC, D])
    d_x1 = nc.dram_tensor("x1_scratch", [S, D], F32, kind="Internal")
    d_ctx1 = nc.dram_tensor("ctx1_scratch", [LC, D], F32, kind="Internal")
    d_out = B.dout("out", [S, D])
    B.setup_common(l0=True)
    main = Arena(B.p, "main", MAIN_KIB)
    B.cur_arena = main
    B.layer0(d_x, d_ctx, d_x1, d_ctx1)
    old = main.reset()
    B.l1_old = old
    ar = Arena(B.p, "arL1", 52, parent=main)
    B.layer1(d_x1, d_ctx1, d_out, ar)
    B.cur_arena = None
    B.p.emit()
    print("fused stats", B.p.stats)
    return B


MAIN_KIB = 196
ALL_KEYS = tuple(dict.fromkeys(L0_KEYS + L1_KEYS))
_PROGS = {}


def kernel(**inputs):
    maps = host_inputs(inputs)
    n = len(maps)
    if "fused" not in _PROGS:
        _PROGS["fused"] = build_fused()
    r = run_bass_kernel_spmd(_PROGS["fused"].nc, [{k: m[k] for k in ALL_KEYS} for m in maps], core_ids=list(range(n)))
    return np.stack([np.asarray(r.results[i]["out"], dtype=np.float32) for i in range(n)], axis=0)
```

```python
import numpy as np
import concourse.bass as bass
import concourse.mybir as mybir
from concourse.bass_utils import run_bass_kernel_spmd
from contextlib import ExitStack

F32 = mybir.dt.float32
BF16 = mybir.dt.bfloat16
AF = mybir.ActivationFunctionType
ALU = mybir.AluOpType

COMPUTE = ("pe", "act", "dve", "pool")
ALL_ENG = ("pe", "act", "dve", "pool", "sp")

D = 1024
S = 4096
LC = 256
NT = S // 128
EPS = 1e-6


class Buf:
    __slots__ = ("name", "last_w", "readers", "excl")

    def __init__(self, name):
        self.name = name
        self.last_w = None
        self.readers = []
        self.excl = False


class Op:
    __slots__ = ("eng", "fn", "deps", "signal", "sig_val", "is_dma", "dma_slot", "dma_val",
                 "pre_dma_wait")

    def __init__(self, eng, fn, is_dma=False):
        self.eng = eng
        self.fn = fn
        self.deps = []
        self.signal = False
        self.sig_val = None
        self.is_dma = is_dma
        self.dma_slot = None
        self.dma_val = None
        self.pre_dma_wait = None


class Prog:
    N_DMA_SEMS = 24
    STRICT_SAME_ENGINE = True

    def __init__(self, nc):
        self.nc = nc
        self.ops = {e: [] for e in ALL_ENG}
        self.dma_count = {e: 0 for e in ALL_ENG}
        self.stack = ExitStack()

    def sb(self, name, shape, dtype=F32):
        return self.stack.enter_context(self.nc.sbuf_tensor(name, list(shape), dtype))

    def ps(self, name, shape, dtype=F32):
        return self.stack.enter_context(self.nc.psum_tensor(name, list(shape), dtype))

    def _dep(self, op, w):
        if w is None or w is op:
            return
        w.signal = True
        op.deps.append(w)

    def add(self, eng, fn, reads=(), writes=(), is_dma=False):
        op = Op(eng, fn, is_dma)
        for b in reads:
            w = b.last_w
            if w is not None:
                self._dep(op, w)
            if b.excl:
                for r in b.readers:
                    if r.eng != eng:
                        self._dep(op, r)
        strict = self.STRICT_SAME_ENGINE and eng != "pe"
        for b in writes:
            w = b.last_w
            if w is not None and (w.eng != eng or w.is_dma or is_dma or strict):
                self._dep(op, w)
            for r in b.readers:
                if r.eng != eng or r.is_dma or is_dma or strict:
                    self._dep(op, r)
        for b in reads:
            if not is_dma:
                b.readers = [r for r in b.readers if r.eng != eng or r.is_dma]
            b.readers.append(op)
        for b in writes:
            b.last_w = op
            b.readers = []
        if is_dma:
            j = self.dma_count[eng]
            self.dma_count[eng] = j + 1
            op.dma_slot = j % self.N_DMA_SEMS
            op.dma_val = 16 * (j // self.N_DMA_SEMS + 1)
            op.pre_dma_wait = 16 * (j // self.N_DMA_SEMS)
        self.ops[eng].append(op)
        return op

    def dma(self, eng, out, in_, reads=(), writes=(), **kw):
        return self.add(eng, lambda e: e.dma_start(out=out, in_=in_, **kw), reads, writes, is_dma=True)

    def emit(self):
        nc = self.nc
        st = self.stack
        esem = {e: st.enter_context(nc.semaphore("s_" + e)) for e in COMPUTE}
        dsem = {}
        for e in ALL_ENG:
            if self.dma_count[e] > 0:
                dsem[e] = [st.enter_context(nc.semaphore("d_%s_%d" % (e, i)))
                           for i in range(min(self.N_DMA_SEMS, self.dma_count[e]))]
        for e in ALL_ENG:
            c = 0
            for op in self.ops[e]:
                if op.is_dma:
                    continue
                if op.signal:
                    c += 1
                    op.sig_val = c
        stats = {e: [0, 0] for e in ALL_ENG}

        def run(ename):
            def body(e):
                waited = {}
                for op in self.ops[ename]:
                    need = {}
                    for w in op.deps:
                        if w.is_dma:
                            key = ("d", w.eng, w.dma_slot)
                            val = w.dma_val
                        else:
                            key = ("e", w.eng)
                            val = w.sig_val
                        if need.get(key, 0) < val:
                            need[key] = val
                    if op.is_dma and op.pre_dma_wait > 0:
                        key = ("d", ename, op.dma_slot)
                        if need.get(key, 0) < op.pre_dma_wait:
                            need[key] = op.pre_dma_wait
                    for key, val in need.items():
                        if waited.get(key, 0) >= val:
                            continue
                        waited[key] = val
                        sem = esem[key[1]] if key[0] == "e" else dsem[key[1]][key[2]]
                        e.wait_ge(sem, val)
                        stats[ename][1] += 1
                    inst = op.fn(e)
                    stats[ename][0] += 1
                    if op.is_dma:
                        inst.then_inc(dsem[ename][op.dma_slot], 16)
                    elif op.signal:
                        inst.then_inc(esem[ename], 1)
                if ename in dsem:
                    nd = self.dma_count[ename]
                    for s in range(len(dsem[ename])):
                        cnt = (nd - 1 - s) // self.N_DMA_SEMS + 1 if nd > s else 0
                        if cnt > 0 and waited.get(("d", ename, s), 0) < 16 * cnt:
                            e.wait_ge(dsem[ename][s], 16 * cnt)
            return body

        with nc.Block() as block:
            if self.ops["sp"]:
                block.sync(run("sp"))
            if self.ops["pe"]:
                block.tensor(run("pe"))
            if self.ops["act"]:
                block.scalar(run("act"))
            if self.ops["dve"]:
                block.vector(run("dve"))
            if self.ops["pool"]:
                block.gpsimd(run("pool"))
        self.stats = stats
        st.close()


class T:
    def __init__(self, t, name, buf=None, base=0, shape=None):
        self.t = t
        self.name = name
        self.b = buf if buf is not None else Buf(name)
        self.rowfull = int(np.prod(list(t.shape)[1:]))
        self.base = base
        self.shape = list(shape) if shape is not None else list(t.shape)
        self.size = int(np.prod(self.shape[1:]))

    def ap(self, p0, npart, off, dims):
        return bass.AP(self.t, p0 * self.rowfull + self.base + off, [[self.rowfull, npart]] + [list(d) for d in dims])

    def view(self):
        if len(self.t.shape) != 2:
            assert self.base == 0
            return self.t
        v = self.t[:, self.base:self.base + self.size]
        if len(self.shape) == 3:
            v = v.rearrange("p (a b) -> p a b", a=self.shape[1], b=self.shape[2])
        elif len(self.shape) == 4:
            v = v.rearrange("p (a b c) -> p a b c", a=self.shape[1], b=self.shape[2], c=self.shape[3])
        return v

    def __getitem__(self, idx):
        return self.view()[idx]


class Arena:
    def __init__(self, prog, name, kib, parent=None):
        self.cap = int(kib * 1024)
        if parent is None:
            self.h = prog.sb(name, [128, self.cap // 2], BF16)
            self.views = {BF16: self.h, F32: self.h.bitcast(F32)}
            self.org = 0
        else:
            parent.off = (parent.off + 31) // 32 * 32
            assert parent.off + self.cap <= parent.cap, (name, parent.off, self.cap, parent.cap)
            self.views = parent.views
            self.org = parent.org + parent.off
            parent.off += self.cap
        self.parent = parent
        self.off = 0
        self.live = []
        self.children = []
        if parent is not None:
            parent.children.append(self)

    def all_live(self):
        out = list(self.live)
        for c in self.children:
            out += c.all_live()
        return out

    def reset(self):
        old = self.all_live()
        self.off = 0
        self.live = []
        self.children = []
        return old

    def alloc(self, name, shape, dt=F32):
        esz = 2 if dt == BF16 else 4
        n = int(np.prod(list(shape)[1:]))
        self.off = (self.off + 31) // 32 * 32
        assert self.off + n * esz <= self.cap, (name, self.off, n * esz, self.cap)
        t = T(self.views[dt], name, None, (self.org + self.off) // esz, shape)
        self.off += n * esz
        self.live.append(t)
        return t


def _tables():
    tb = {}
    tb["ident"] = np.eye(128, dtype=np.float32)
    r2 = np.arange(2)[:, None, None, None]
    a = np.arange(64)[None, :, None, None]
    j = np.arange(32)[None, None, :, None]
    k1 = np.arange(64)[None, None, None, :]
    n = 64 * a + 2 * j + r2
    th = 2 * np.pi * ((k1 * n) % 4096) / 4096.0
    t1 = np.concatenate([np.cos(th), -np.sin(th)], axis=-1) / 8.0
    tb["tab1"] = np.ascontiguousarray(t1.reshape(128, 32 * 128)).astype(np.float32)
    r = np.arange(64)[:, None]
    k2 = np.arange(64)[None, :]
    ph = 2 * np.pi * ((r * k2) % 64) / 64.0
    t2 = np.concatenate([np.cos(ph), np.sin(ph)], axis=1) / 8.0
    tb["tab2"] = np.concatenate([t2, t2], axis=0).astype(np.float32)
    c = np.arange(128)[:, None]
    c2 = np.arange(128)[None, :]
    al = 2 * np.pi * ((c * c2) % 128) / 128.0
    Cc, Sc = np.cos(al) / np.sqrt(128.0), np.sin(al) / np.sqrt(128.0)
    tb["fc"] = np.concatenate([Cc, -Sc, Sc, Cc], axis=1).astype(np.float32)
    nn = np.arange(256)[:, None]
    kk = np.arange(256)[None, :]
    be = 2 * np.pi * ((nn * kk) % 256) / 256.0
    tc = np.concatenate([np.cos(be), -np.sin(be)], axis=1) / 16.0
    tb["tabc"] = np.ascontiguousarray(tc.reshape(2, 128, 512).transpose(1, 0, 2).reshape(128, 1024)).astype(np.float32)
    tok = np.arange(S)
    row, col = tok // 64, tok % 64
    inv = 10000.0 ** (-np.arange(16) / 16.0)
    dd = np.arange(64)
    pos = np.where(dd[None, :] < 32, row[:, None], col[:, None]).astype(np.float64)
    ang = (pos.astype(np.float32) * inv[dd % 16][None, :].astype(np.float32)).astype(np.float32).astype(np.float64)
    cs = np.cos(ang)
    sn = np.sin(ang) * np.where((dd % 32) < 16, -1.0, 1.0)[None, :]
    tb["rope_cos"] = np.ascontiguousarray(cs.reshape(NT, 128, 64).transpose(1, 0, 2).reshape(128, NT * 64)).astype(np.float32)
    tb["rope_sin"] = np.ascontiguousarray(sn.reshape(NT, 128, 64).transpose(1, 0, 2).reshape(128, NT * 64)).astype(np.float32)
    jj = np.arange(128)[:, None]
    ii = np.arange(128)[None, :]
    tb["wmask"] = np.concatenate([(jj >= ii), (jj <= ii)], axis=1).astype(np.float32)
    return tb


class Builder:
    def __init__(self, mode):
        self.mode = mode
        self.stop = None
        self.nt_l1 = None
        self.cur_arena = None
        self.nc = bass.Bass("TRN2", target_bir_lowering=False)
        self.p = Prog(self.nc)
        self.rr = {}

    def din(self, name, shape, dt=F32):
        return self.nc.dram_tensor(name, list(shape), dt, kind="ExternalInput")

    def dout(self, name, shape, dt=F32):
        return self.nc.dram_tensor(name, list(shape), dt, kind="ExternalOutput")

    def sbt(self, name, shape, dt=F32, buf=None):
        if self.cur_arena is not None:
            return self.cur_arena.alloc(name, shape, dt)
        return T(self.p.sb("s_" + name, shape, dt), name, buf)

    def pst(self, name, shape, dt=F32):
        t = T(self.p.ps("p_" + name, shape, dt), name)
        t.b.excl = True
        return t

    def pick(self, key, engines):
        i = self.rr.get(key, 0)
        self.rr[key] = i + 1
        return engines[i % len(engines)]

    def mm(self, out, lhsT, rhs, start, stop, reads, writes):
        self.p.add("pe", lambda e: e.matmul(out=out, lhsT=lhsT, rhs=rhs, start=start, stop=stop), reads, writes)

    def tr(self, out, in_, ident, reads, writes):
        self.p.add("pe", lambda e: e.transpose(out=out, in_=in_, identity=ident), reads, writes)

    def act(self, out, in_, func, reads, writes, scale=None, bias=None, accum=None):
        kw = {}
        if scale is not None:
            kw["scale"] = scale
        if bias is not None:
            kw["bias"] = bias
        if accum is not None:
            kw["accum_out"] = accum
        self.p.add("act", lambda e: e.activation(out=out, in_=in_, func=func, **kw), reads, writes)

    def tt(self, eng, out, in0, in1, op, reads, writes):
        self.p.add(eng, lambda e: e.tensor_tensor(out=out, in0=in0, in1=in1, op=op), reads, writes)

    def ts(self, eng, out, in0, s1, s2, op0, op1, reads, writes):
        if op1 is None:
            self.p.add(eng, lambda e: e.tensor_scalar(out=out, in0=in0, scalar1=s1, scalar2=None, op0=op0), reads, writes)
        else:
            self.p.add(eng, lambda e: e.tensor_scalar(out=out, in0=in0, scalar1=s1, scalar2=s2, op0=op0, op1=op1), reads, writes)

    def cp(self, eng, out, in_, reads, writes):
        if eng == "act":
            self.p.add("act", lambda e: e.copy(out=out, in_=in_), reads, writes)
        else:
            self.p.add(eng, lambda e: e.tensor_copy(out=out, in_=in_), reads, writes)

    def load(self, out, in_, writes, reads=(), q=None):
        self.p.dma(q if q is not None else "sp", out, in_, reads=reads, writes=writes)

    def loadw(self, out, in_, writes):
        self.load(out, in_, writes, q=self.pick("wq", ["sp", "act"]))

    def dump(self, name, t, ap=None, shape=None, dt=F32):
        shape = list(shape if shape is not None else t.shape)
        d = self.dout("dbg_" + name, shape, dt)
        self.p.dma("sp", d.ap(), ap if ap is not None else t[:], reads=[t.b])

    def fence(self, tiles):
        bufs = [t.b for t in tiles]
        self.p.add("pool", lambda e: e.memset(self.fz[:, 0:1], 0.0), writes=bufs + [self.fz.b])

    def retarget(self, old_tiles, new_tiles):
        if not old_tiles:
            return
        bufs = [t.b for t in old_tiles] + [t.b for t in new_tiles]
        self.p.add("pool", lambda e: e.memset(self.fz[:, 0:1], 0.0), writes=bufs + [self.fz.b])

    def setup_common(self, l0=True):
        b = self
        self.d_cc = self.din("cc", [128, 16])
        self.d_ng = self.din("ng", [128, 16])
        self.d_ident = self.din("ident", [128, 128])
        self.d_ada_w = self.din("ada_w", [2, D, 3 * D])
        self.d_ada_b = self.din("ada_b", [2, 3 * D])
        self.fz = b.sbt("fz", [128, 8])
        self.identF = b.sbt("identF", [128, 128])
        self.identB = b.sbt("identB", [128, 128], BF16)
        b.load(self.identF[:], self.d_ident.ap(), [self.identF.b])
        b.cp("dve", self.identB[:], self.identF[:], [self.identF.b], [self.identB.b])
        self.onesF = b.sbt("onesF", [128, 128])
        self.p.add("pool", lambda e: e.memset(self.onesF[:], 1.0), writes=[self.onesF.b])
        self.cc = b.sbt("cc_t", [128, 16])
        self.ng = b.sbt("ng_t", [128, 16])
        b.load(self.cc[:], self.d_cc.ap(), [self.cc.b])
        b.load(self.ng[:], self.d_ng.ap(), [self.ng.b])
        self.sc = b.sbt("sc_t", [128, 16])
        b.act(self.sc[:], self.cc[:], AF.Silu, [self.cc.b], [self.sc.b])
        self.junk = b.sbt("junk", [128, D], BF16)
        self.mhalf = b.sbt("mhalf", [128, 1])
        self.p.add("pool", lambda e: e.memset(self.mhalf[:], -0.5), writes=[self.mhalf.b])
        self.pZ = [b.pst("pZ%d" % i, [128, 512]) for i in range(4)]
        self.pS = b.pst("pS", [128, 512])
        self.pX = b.pst("pX", [128, 1024], BF16)
        self.pA = b.pst("pA", [128, 1024], BF16)
        self.pB = b.pst("pB", [128, 1024], BF16)
        self.pXs = [self.pX]
        self.d_ngrow = self.din("ngrow", [2, D])
        self.pZ4 = [T(z.t.reshape([128, 4, 128]), "pZ4", z.b) for z in self.pZ]
        self.xn = [b.sbt("xn%d" % i, [128, D], BF16) for i in range(2)]
        self.ss = [b.sbt("ss%d" % i, [128, 4]) for i in range(4)]


    def make_scdup(self, ar, scdup=None):
        b = self
        if scdup is None:
            scdup = ar.alloc("scdup", [128, 8, 128])
        b.cp("dve", scdup[:, :, 0:64], self.sc.ap(0, 128, 0, [[1, 8], [0, 64]]), [self.sc.b], [scdup.b])
        b.cp("dve", scdup[:, :, 64:128], self.sc.ap(0, 128, 8, [[1, 8], [0, 64]]), [self.sc.b], [scdup.b])
        return scdup

    def adaln(self, l, mod, scdup, awst, adab, pAda, pTm):
        b = self
        NB = 256
        for cb in range(3 * D // NB):
            st = awst[cb % len(awst)]
            ab = adab[cb % len(adab)]
            src = bass.AP(self.d_ada_w, l * D * 3 * D + cb * NB, [[3 * D, 128], [128 * 3 * D, 8], [1, NB]])
            b.loadw(st[:], src, [st.b])
            b.load(ab[:], bass.AP(self.d_ada_b, l * 3 * D + cb * NB, [[0, 128], [1, NB]]), [ab.b])
            pa = pAda[cb % len(pAda)]
            for k in range(8):
                b.mm(pa[:, 0:NB], scdup[:, k, :], st[:, k, :], k == 0, k == 7, [scdup.b, st.b], [pa.b])
            b.tt("dve", mod[:, cb * NB:(cb + 1) * NB], pa[:, 0:NB], ab[:], ALU.add, [pa.b, ab.b], [mod.b])
        modT = b.sbt("modT%d" % l, [128, 2, 16])
        for q in range(4):
            pt = pTm[q % len(pTm)]
            for i in range(4):
                ch = q * 4 + i
                b.tr(pt[:, i, :], mod[:, ch * 128:(ch + 1) * 128], self.identF[:], [mod.b, self.identF.b], [pt.b])
            b.cp("dve", modT.ap(0, 128, q * 4, [[16, 2], [1, 4]]), pt.ap(0, 128, 0, [[64, 2], [128, 4]]), [pt.b], [modT.b])
        AB = b.sbt("AB%d" % l, [128, 2, 16])
        for xc in range(2):
            self.p.add("dve", lambda e, xc=xc: e.scalar_tensor_tensor(
                out=AB[:, xc, 0:8], in0=modT[:, xc, 8:16], scalar=1.0, in1=self.ng[:, l * 8:(l + 1) * 8],
                op0=ALU.add, op1=ALU.mult), [modT.b, self.ng.b], [AB.b])
            b.cp("dve", AB[:, xc, 8:16], modT[:, xc, 0:8], [modT.b], [AB.b])
        return AB

    def gate_bc(self, g, mod, xc, pAda):
        b = self
        p0 = 64 * xc
        for cb in range(2):
            pa = pAda[cb % len(pAda)]
            b.mm(pa[:], self.onesF[p0:p0 + 1, :], mod[p0:p0 + 1, 2 * D + cb * 512:2 * D + (cb + 1) * 512], True, True,
                 [self.onesF.b, mod.b], [pa.b])
            b.cp("act", g[:, cb * 512:(cb + 1) * 512], pa[:], [pa.b], [g.b])

    def rstd_from_ss(self, ss):
        b = self
        b.ts("pool", ss[:, 1:2], ss[:, 0:1], 1.0 / D, EPS, ALU.mult, ALU.add, [ss.b], [ss.b])
        b.tt("pool", ss[:, 3:4], ss[:, 1:2], self.mhalf[:, 0:1], ALU.pow, [ss.b, self.mhalf.b], [ss.b])

    def next(self, key, lst):
        i = self.rr.get(key, 0)
        self.rr[key] = i + 1
        return lst[i % len(lst)]

    def load_x(self, dsrc, r0):
        x_ = self.next("xt", self.xt)
        self.load(x_[:], dsrc.ap()[r0:r0 + 128, :], [x_.b])
        return x_

    def sumsq(self, x_, s_, eng="dve"):
        if eng == "act":
            self.act(self.junk[:], x_[:], AF.Square, [x_.b], [self.junk.b, s_.b], accum=s_[:, 0:1])
            return
        self.p.add("dve", lambda e: e.scalar_tensor_tensor(out=self.junk[:], in0=x_[:], scalar=1.0, in1=x_[:], op0=ALU.mult,
                                                          op1=ALU.mult, accum_out=s_[:, 0:1]), [x_.b], [self.junk.b, s_.b])

    def hT_prep(self, x_, Arow=None, rstd=None, save_rstd=None, sq_eng="dve"):
        b = self
        n_ = self.next("xn", self.xn)
        if rstd is None:
            s_ = self.next("ss", self.ss)
            b.sumsq(x_, s_, sq_eng)
            b.rstd_from_ss(s_)
            rs, rsb = s_[:, 3:4], s_.b
            if save_rstd is not None:
                sap, sbuf = save_rstd
                b.cp("pool", sap, s_[:, 3:4], [s_.b], [sbuf])
        else:
            rs, rsb = rstd
        if Arow is not None:
            self.p.add("dve", lambda e: e.scalar_tensor_tensor(out=n_[:], in0=x_[:], scalar=rs, in1=Arow[:],
                                                              op0=ALU.mult, op1=ALU.mult), [x_.b, rsb, Arow.b], [n_.b])
        else:
            b.ts("dve", n_[:], x_[:], rs, None, ALU.mult, None, [x_.b, rsb], [n_.b])
        return n_

    def hT_fin(self, n_, AB, xc, dst, fused_A):
        b = self
        dap, dbuf = dst
        pX = self.next("pX", self.pXs)
        for c in range(8):
            b.tr(pX[:, c * 128:(c + 1) * 128], n_[:, c * 128:(c + 1) * 128], self.identB[:], [n_.b, self.identB.b], [pX.b])
        pv = pX.ap(0, 128, 0, [[128, 8], [1, 128]])
        if not fused_A:
            b.tt("dve", dap, pv, AB.ap(0, 128, xc * 16, [[1, 8], [0, 128]]), ALU.mult, [pX.b, AB.b], [dbuf])
            b.tt("pool", dap, dap, AB.ap(0, 128, xc * 16 + 8, [[1, 8], [0, 128]]), ALU.add, [dbuf, AB.b], [dbuf])
        else:
            b.tt("dve", dap, pv, AB.ap(0, 128, xc * 16 + 8, [[1, 8], [0, 128]]), ALU.add, [pX.b, AB.b], [dbuf])

    def hT_tile(self, x_, AB, xc, dst, Arow=None, rstd=None, save_rstd=None, sq_eng="dve"):
        n_ = self.hT_prep(x_, Arow, rstd, save_rstd, sq_eng)
        self.hT_fin(n_, AB, xc, dst, Arow is not None)

    def arow_bc(self, Arow, mod, l, pAda, gbc):
        b = self
        for cb in range(2):
            pa = pAda[cb % len(pAda)]
            b.mm(pa[:], self.onesF[0:1, :], mod[0:1, D + cb * 512:D + (cb + 1) * 512], True, True,
                 [self.onesF.b, mod.b], [pa.b])
            self.p.add("dve", lambda e, pa=pa, cb=cb: e.scalar_tensor_tensor(
                out=Arow[:, cb * 512:(cb + 1) * 512], in0=pa[:], scalar=1.0, in1=gbc[:, cb * 512:(cb + 1) * 512],
                op0=ALU.add, op1=ALU.mult), [pa.b, gbc.b], [Arow.b])

    def layer0(self, d_x, d_ctx, d_x1, d_ctx1):
        b = self
        p = self.p
        pZ, pS, pX, pZ4 = self.pZ, self.pS, self.pX, self.pZ4
        self.xt = [b.sbt("xt%d" % i, [128, D]) for i in range(4)]
        self.hT = [b.sbt("hT%d" % i, [128, 8, 512], BF16) for i in range(2)]
        hT = self.hT
        d_win = self.din("w_in_ab", [D, 2560])
        d_wout = self.din("w_out_ab", [D, D])
        d_vg = self.din("v_norm_g", [1, 512])
        d_sw = self.din("spatial_w", [4, 128, 128])
        d_sb = self.din("spatial_b", [1, 512])
        d_tab1 = self.din("tab1", [128, 32 * 128])
        d_tab2 = self.din("tab2", [128, 128])
        d_fc = self.din("fc", [128, 512])
        d_tabc = self.din("tabc", [128, 1024])

        arX = Arena(p, "arX", 40, parent=self.cur_arena)
        arY = Arena(p, "arY", 32, parent=self.cur_arena)
        self.arX, self.arY = arX, arY
        win = b.sbt("win", [128, 8, 2560], BF16)
        gx0 = b.sbt("gx0", [128, D])
        Arow0 = b.sbt("Arow0", [128, D])
        rstd_all = b.sbt("rstd_all", [128, NT])
        swT = b.sbt("swT", [128, 4, 128], BF16)
        self.pXs = [self.pX, self.pA]
        sbk = b.sbt("sbk", [33, 512], BF16)
        onesk = b.sbt("onesk", [33, 128], BF16)
        vg = b.sbt("vg", [128, 512])
        arZ = Arena(p, "arZ", 8, parent=self.cur_arena)
        tab1 = arZ.alloc("tab1", [128, 32, 128], BF16)
        tab2 = b.sbt("tab2", [128, 128], BF16)
        fc = b.sbt("fc", [128, 512], BF16)
        u_t = [b.sbt("u_t%d" % i, [128, 512]) for i in range(2)]
        sg_t = [b.sbt("sg_t%d" % i, [128, 512]) for i in range(2)]
        vh_t = [b.sbt("vh_t%d" % i, [128, 512]) for i in range(2)]
        vb_t = [b.sbt("vb_t%d" % i, [128, 512], BF16) for i in range(2)]
        ya_t = [b.sbt("ya_t%d" % i, [128, 512], BF16) for i in range(2)]
        st6 = [b.sbt("st6_%d" % i, [128, 8]) for i in range(2)]
        yT_t = [b.sbt("yT_t%d" % i, [128, 8, 128], BF16) for i in range(2)]
        sgm_t = [b.sbt("sgm_t%d" % i, [128, 512], BF16) for i in range(2)]
        ybg_t = [b.sbt("ybg_t%d" % i, [128, 4, 512], BF16) for i in range(2)]

        woutc = arX.alloc("woutc", [128, 8, D], BF16)
        mod0 = arX.alloc("mod0", [128, 3 * D])
        gc0 = arX.alloc("gc0", [128, D])
        sbf = arX.alloc("sbf", [33, 512])
        tmpb = arX.alloc("tmpb", [33, 512], BF16)
        awst = [arY.alloc("awst%d" % i, [128, 8, 256]) for i in range(2)]
        scdup = b.make_scdup(arY)
        adab = [arY.alloc("adab%d" % i, [128, 256]) for i in range(2)]
        xbc = [arY.alloc("xbc%d" % j, [128, 512], BF16) for j in range(2)]
        sgbc = [arY.alloc("sgbc%d" % j, [128, 512], BF16) for j in range(2)]
        ZT = [arY.alloc("ZT%d" % g, [128, 512], BF16) for g in range(4)]
        tabc = arY.alloc("tabc", [128, 2, 512], BF16)

        AB0 = b.adaln(0, mod0, scdup, awst, adab, pZ[0:2], pZ4[2:4])
        b.gate_bc(gx0, mod0, 0, pZ[0:2])
        b.gate_bc(gc0, mod0, 1, pZ[0:2])
        gbc = T(awst[0].t, "gbc", awst[0].b, awst[0].base, [128, D])
        b.load(gbc[:], bass.AP(self.d_ngrow, 0, [[0, 128], [1, D]]), [gbc.b])
        b.arow_bc(Arow0, mod0, 0, pZ[0:2], gbc)

        engs3 = ["dve", "act"]
        stv = [T(a.t, "stv", a.b, a.base, [128, 2048]) for a in awst]
        for k in range(8):
            for hh in range(2):
                st = stv[(k * 2 + hh) % 2]
                b.loadw(st[:, 0:1280], d_win.ap()[k * 128:(k + 1) * 128, hh * 1280:(hh + 1) * 1280], [st.b])
                b.cp(b.pick("wcast", engs3), win[:, k, hh * 1280:(hh + 1) * 1280], st[:, 0:1280], [st.b], [win.b])
        for k in range(8):
            st = stv[k % 2]
            b.loadw(st[:, 0:D], d_wout.ap()[k * 128:(k + 1) * 128, :], [st.b])
            b.tt("pool", woutc[:, k, :], st[:, 0:D], gc0[:], ALU.mult, [st.b, gc0.b], [woutc.b])
        st = stv[0]
        b.load(st[:, 0:512].rearrange("p (h q) -> p h q", h=4), d_sw.ap().rearrange("h q p -> q h p"), [st.b])
        for h in range(4):
            b.tr(pZ4[2][:, h, :], st[:, h * 128:(h + 1) * 128], self.identF[:], [st.b, self.identF.b], [pZ4[2].b])
        b.cp("dve", swT[:], pZ4[2][:], [pZ4[2].b], [swT.b])
        b.load(sbf[0:1, :], d_sb.ap(), [sbf.b])
        b.load(sbf[32:33, :], d_sb.ap(), [sbf.b])
        p.add("pool", lambda e: e.memset(sbk[:], 0.0), writes=[sbk.b])
        p.add("pool", lambda e: e.memset(onesk[:], 1.0), writes=[onesk.b])
        b.cp("dve", sbk[0:1, :], sbf[0:1, :], [sbf.b], [sbk.b])
        b.cp("dve", tmpb[32:33, :], sbf[32:33, :], [sbf.b], [tmpb.b])
        b.tt("dve", sbk[32:33, :], sbf[32:33, :], tmpb[32:33, :], ALU.subtract, [sbf.b, tmpb.b], [sbk.b])
        b.load(vg[:], bass.AP(d_vg, 0, [[0, 128], [1, 512]]), [vg.b])
        for q in range(2):
            st = stv[q % 2]
            b.load(st[:, 0:2048], d_tab1.ap()[:, q * 2048:(q + 1) * 2048], [st.b])
            b.cp(b.pick("wcast", engs3), tab1.ap(0, 128, q * 2048, [[1, 2048]]), st[:, 0:2048], [st.b], [tab1.b])
        st = stv[0]
        b.load(st[:, 0:128], d_tab2.ap(), [st.b])
        b.cp("dve", tab2[:], st[:, 0:128], [st.b], [tab2.b])
        st = stv[1]
        b.load(st[:, 0:512], d_fc.ap(), [st.b])
        b.cp("dve", fc[:], st[:, 0:512], [st.b], [fc.b])
        st = stv[0]
        b.load(st[:, 0:1024], d_tabc.ap(), [st.b])
        b.cp("dve", tabc.ap(0, 128, 0, [[1, 1024]]), st[:, 0:1024], [st.b], [tabc.b])

        if self.stop == "pro":
            b.dump("AB0", AB0)
            b.dump("mod0", mod0)
            b.dump("gx0", gx0)
            b.dump("gc0", gc0)
            b.dump("win", win, dt=BF16)
            b.dump("woutc", woutc, dt=BF16)
            b.dump("swT", swT, dt=BF16)
            b.dump("sbk", sbk, dt=BF16)
            b.dump("tab1", tab1, dt=BF16)
            b.dump("tabc", tabc, dt=BF16)
            return
        def a_branch(hTm, j, slot):
            pu, pv, pg = pZ[0], pZ[1], pZ[2]
            for (pz, c0) in ((pv, 512), (pg, 1024), (pu, 0)):
                for k in range(8):
                    b.mm(pz[:], hTm[:, k, j * 128:(j + 1) * 128], win[:, k, c0:c0 + 512], k == 0, k == 7,
                         [hTm.b, win.b], [pz.b])
            s6 = st6[slot]
            vh, vb, sg, u, ya = vh_t[slot], vb_t[slot], sg_t[slot], u_t[slot], ya_t[slot]
            p.add("dve", lambda e: e.bn_stats(out=s6[:, 0:6], in_=pv[:]), [pv.b], [s6.b])
            p.add("dve", lambda e: e.bn_aggr(out=s6[:, 6:8], in_=s6[:, 0:6]), [s6.b], [s6.b])
            b.ts("pool", s6[:, 0:1], s6[:, 7:8], EPS, None, ALU.add, None, [s6.b], [s6.b])
            b.tt("pool", s6[:, 2:3], s6[:, 0:1], self.mhalf[:, 0:1], ALU.pow, [s6.b, self.mhalf.b], [s6.b])
            b.ts("dve", vh[:], pv[:], s6[:, 6:7], s6[:, 2:3], ALU.subtract, ALU.mult, [pv.b, s6.b], [vh.b])
            b.tt("pool", vb[:], vh[:], vg[:], ALU.mult, [vh.b, vg.b], [vb.b])
            b.act(sg[:], pg[:], AF.Silu, [pg.b], [sg.b])
            b.tt("dve", u[:], pu[:], sg[:], ALU.mult, [pu.b, sg.b], [u.b])
            for h in range(4):
                b.mm(pS[:, h * 128:(h + 1) * 128], swT[:, h, :], vb[:, h * 128:(h + 1) * 128], True, False,
                     [swT.b, vb.b], [pS.b])
                b.mm(pS[:, h * 128:(h + 1) * 128], sbk[0:33, h * 128:(h + 1) * 128], onesk[0:33, :], False, True,
                     [sbk.b, onesk.b], [pS.b])
            b.tt("dve", ya[:], pS[:], u[:], ALU.mult, [pS.b, u.b], [ya.b])
            return ya

        def out_proj(lhs_list, wmat, x_):
            for cb in range(2):
                pz = pZ[cb]
                for k in range(8):
                    ap_, bf_ = lhs_list[k]
                    b.mm(pz[:], ap_, wmat[:, k, cb * 512:(cb + 1) * 512], k == 0, k == 7, [bf_, wmat.b], [pz.b])
                b.tt("dve", x_[:, cb * 512:(cb + 1) * 512], pz[:], x_[:, cb * 512:(cb + 1) * 512], ALU.add,
                     [pz.b, x_.b], [x_.b])

        hTc = hT[1]
        xcs = []
        for j in range(2):
            x_ = b.load_x(d_ctx, j * 128)
            xcs.append(x_)
            b.hT_tile(x_, AB0, 1, (hTc.ap(0, 128, j * 128, [[512, 8], [1, 128]]), hTc.b))
        yac = []
        for j in range(2):
            yac.append(a_branch(hTc, j, j))
            for (pz, c0) in ((pZ[3], 1536), (pZ[2], 2048)):
                for k in range(8):
                    b.mm(pz[:], hTc[:, k, j * 128:(j + 1) * 128], win[:, k, c0:c0 + 512], k == 0, k == 7,
                         [hTc.b, win.b], [pz.b])
            b.cp("dve", xbc[j][:], pZ[3][:], [pZ[3].b], [xbc[j].b])
            b.act(sgbc[j][:], pZ[2][:], AF.Silu, [pZ[2].b], [sgbc[j].b])
        for g in range(4):
            pz = pZ[g % 2]
            for j in range(2):
                b.mm(pz[:], xbc[j][:, g * 128:(g + 1) * 128], tabc[:, j, :], j == 0, j == 1, [xbc[j].b, tabc.b], [pz.b])
            b.cp(b.pick("zt", ["act", "dve"]), ZT[g][:], pz[:], [pz.b], [ZT[g].b])
        for kt in range(2):
            pz = pZ[2 + kt]
            for g in range(4):
                b.mm(pz[:, g * 128:(g + 1) * 128], ZT[g][:, kt * 128:(kt + 1) * 128], fc[:, 0:128], True, False,
                     [ZT[g].b, fc.b], [pz.b])
                b.mm(pz[:, g * 128:(g + 1) * 128], ZT[g][:, 256 + kt * 128:256 + (kt + 1) * 128], fc[:, 256:384], False, True,
                     [ZT[g].b, fc.b], [pz.b])
            ybc = vb_t[kt]
            b.tt("dve", ybc[:], pz[:], sgbc[kt][:], ALU.mult, [pz.b, sgbc[kt].b], [ybc.b])
            yT = yT_t[kt]
            for c in range(4):
                b.tr(pX[:, c * 128:(c + 1) * 128], yac[kt][:, c * 128:(c + 1) * 128], self.identB[:], [yac[kt].b, self.identB.b], [pX.b])
            for c in range(4):
                b.tr(pX[:, (4 + c) * 128:(5 + c) * 128], ybc[:, c * 128:(c + 1) * 128], self.identB[:], [ybc.b, self.identB.b], [pX.b])
            b.cp("act", yT.ap(0, 128, 0, [[1, 1024]]), pX[:], [pX.b], [yT.b])
            out_proj([(yT[:, k, :], yT.b) for k in range(8)], woutc, xcs[kt])
            p.dma("sp", d_ctx1.ap()[kt * 128:(kt + 1) * 128, :], xcs[kt][:], reads=[xcs[kt].b])

        if self.stop == "ctx":
            b.dump("hTc", hTc, dt=BF16)
            b.dump("ya0", ya_t[0], dt=BF16)
            b.dump("yb0", vb_t[0], dt=BF16)
            return
        oldY = arY.reset()
        xbT = [arY.alloc("xbT%d" % g, [128, S], BF16) for g in range(4)]
        b.retarget(oldY, xbT)
        prepB = {}

        def HB1(t):
            if t >= NT:
                return
            x_ = b.load_x(d_x, t * 128)
            prepB[t] = b.hT_prep(x_, Arow=Arow0, save_rstd=(rstd_all[:, t:t + 1], rstd_all.b), sq_eng="act")

        def HB2(t):
            if t >= NT:
                return
            m, j = t // 4, t % 4
            hTm = hT[m % 2]
            b.hT_fin(prepB.pop(t), AB0, 0, (hTm.ap(0, 128, j * 128, [[512, 8], [1, 128]]), hTm.b), True)

        HB1(0)
        for j in range(4):
            HB1(j + 1)
            HB2(j)
        for m in range(8):
            hTm = hT[m % 2]
            for g in range(4):
                c0 = 1536 + g * 128
                pz = self.next("pzB", pZ)
                for k in range(8):
                    b.mm(pz[:], win[:, k, c0:c0 + 128], hTm[:, k, :], k == 0, k == 7, [win.b, hTm.b], [pz.b])
                b.cp("act", xbT[g][:, m * 512:(m + 1) * 512], pz[:], [pz.b], [xbT[g].b])
                HB1(4 * (m + 1) + g + 1)
                HB2(4 * (m + 1) + g)

        if self.stop == "passB":
            for g in range(4):
                b.dump("xbT%d" % g, xbT[g], dt=BF16)
            return
        oldX = arX.reset()
        GT = arX.alloc("GT", [128, 8192], BF16)
        XR = arX.alloc("XR", [128, 32, 128], BF16)
        TT = arX.alloc("TT", [128, 32, 256], BF16)
        b.retarget(oldX, [GT, XR, TT])
        if self.stop == "m0":
            b.dump("xbT0", xbT[0], dt=BF16)
            return
        XRb = [Buf("XR%d" % i) for i in range(4)]
        TTb = [Buf("TT%d" % i) for i in range(4)]
        for bb in XRb:
            bb.last_w = XR.b.last_w
        for bb in TTb:
            bb.last_w = TT.b.last_w
        pB32 = T(self.pB.t.bitcast(F32), "pB32", self.pB.b)
        bpool = [pZ[0], pZ[1], pZ[2], pZ[3], pS, pB32]
        self.pXs = [self.pX, self.pA]

        def M1(g, q):
            pX_ = self.next("pX", self.pXs)
            for jj in range(8):
                j = q * 8 + jj
                for r2 in range(2):
                    src = xbT[g].ap(0, 128, 2 * j + r2, [[64, 64]])
                    b.tr(pX_[64 * r2:64 * r2 + 64, jj * 128:(jj + 1) * 128], src, self.identB[:],
                         [xbT[g].b, self.identB.b], [pX_.b])
            b.cp(b.pick("m1ev", ["act", "dve"]), XR.ap(0, 128, q * 1024, [[1, 1024]]), pX_[:], [pX_.b], [XRb[q]])

        def S1(g, q):
            pzs = (self.next("bp", bpool), self.next("bp", bpool))
            for jj in range(4):
                j = q * 4 + jj
                for r2 in range(2):
                    b.mm(pzs[r2][:, jj * 128:(jj + 1) * 128], XR[64 * r2:64 * r2 + 64, j, :],
                         tab1[64 * r2:64 * r2 + 64, j, :], True, True, [XRb[q // 2], tab1.b], [pzs[r2].b])
            for r2 in range(2):
                dst = GT.ap(0, 128, (q * 8 + r2) * 64, [[4096, 2], [128, 4], [1, 64]])
                srcp = pzs[r2].ap(0, 128, 0, [[64, 2], [128, 4], [1, 64]])
                b.cp(b.pick("s1ev", ["act", "dve"]), dst, srcp, [pzs[r2].b], [GT.b])

        def M2(g, qq):
            pz = self.next("bp", bpool)
            for h2 in range(2):
                q = qq * 2 + h2
                for k1p in range(2):
                    k1 = 2 * q + k1p
                    l0 = GT.ap(0, 128, k1, [[64, 64]])
                    l1 = GT.ap(0, 128, 4096 + k1, [[64, 64]])
                    o_ = pz[64 * k1p:64 * k1p + 64, h2 * 256:(h2 + 1) * 256]
                    b.mm(o_, l0, fc[:, 0:256], True, False, [GT.b, fc.b], [pz.b])
                    b.mm(o_, l1, fc[:, 256:512], False, True, [GT.b, fc.b], [pz.b])
            b.cp(b.pick("m2ev", ["act", "dve"]), TT.ap(0, 128, qq * 512, [[1, 512]]), pz[:], [pz.b], [TTb[qq // 4]])

        def S2(g, i4):
            pzs = (self.next("bp", bpool), self.next("bp", bpool))
            for ql in range(8):
                q = i4 * 8 + ql
                for k1p in range(2):
                    pz = pzs[k1p]
                    b.mm(pz[:, ql * 64:(ql + 1) * 64], TT[64 * k1p:64 * k1p + 64, q, 0:128],
                         tab2[64 * k1p:64 * k1p + 64, 0:64], True, False, [TTb[i4], tab2.b], [pz.b])
                    b.mm(pz[:, ql * 64:(ql + 1) * 64], TT[64 * k1p:64 * k1p + 64, q, 128:256],
                         tab2[64 * k1p:64 * k1p + 64, 64:128], False, True, [TTb[i4], tab2.b], [pz.b])
            for k1p in range(2):
                srcp = pzs[k1p].ap(0, 128, 0, [[64, 8], [1, 64]])
                dst = xbT[g].ap(0, 128, 16 * i4 + k1p, [[2, 8], [64, 64]])
                b.cp(b.pick("s2ev", ["act", "dve"]), dst, srcp, [pzs[k1p].b], [xbT[g].b])

        for q in range(4):
            M1(0, q)
        for q in range(8):
            S1(0, q)
        for g in range(4):
            for q in range(4):
                for i in range(4):
                    M2(g, 4 * q + i)
                if g + 1 < 4:
                    M1(g + 1, q)
            for i4 in range(4):
                S2(g, i4)
                if g + 1 < 4:
                    S1(g + 1, 2 * i4)
                    S1(g + 1, 2 * i4 + 1)
        XR.b.last_w = XRb[3].last_w
        XR.b.readers = [r for bb in XRb for r in bb.readers]
        TT.b.last_w = TTb[3].last_w
        TT.b.readers = [r for bb in TTb for r in bb.readers]

        if self.stop == "Bpipe":
            for g in range(4):
                b.dump("fT%d" % g, xbT[g], dt=BF16)
            return
        oldX = arX.reset()
        wout = arX.alloc("wout", [128, 8, D], BF16)
        wst = [arX.alloc("wst%d" % i, [128, D]) for i in range(2)]
        b.retarget(oldX, [wout] + wst)
        for k in range(8):
            st = wst[k % 2]
            b.load(st[:], d_wout.ap()[k * 128:(k + 1) * 128, :], [st.b])
            b.tt(b.pick("wos", ["dve", "pool"]), wout[:, k, :], st[:], gx0[:], ALU.mult, [st.b, gx0.b], [wout.b])
        oldZ = arZ.reset()
        xr = [self.xt[3]] + [arZ.alloc("xr%d" % i, [128, D]) for i in range(2)]
        b.retarget(oldZ, xr[1:])
        xnorm = self.xt[0:3]
        pA32 = T(self.pA.t.bitcast(F32), "pA32", self.pA.b)
        sets = [(pZ[0], pZ[1], pZ[2]), (pZ[3], pS, pA32)]
        pG = T(self.pB.t.bitcast(F32), "pG", self.pB.b)
        self.pXs = [self.pX]

        prepA = {}

        def HA1(t):
            if t >= NT:
                return
            x_ = self.next("xnorm", xnorm)
            b.load(x_[:], d_x.ap()[t * 128:(t + 1) * 128, :], [x_.b])
            prepA[t] = b.hT_prep(x_, Arow=Arow0, rstd=(rstd_all[:, t:t + 1], rstd_all.b))

        def HA2(t):
            if t >= NT:
                return
            m, j = t // 4, t % 4
            hTm = hT[m % 2]
            b.hT_fin(prepA.pop(t), AB0, 0, (hTm.ap(0, 128, j * 128, [[512, 8], [1, 128]]), hTm.b), True)

        def GA(m):
            if m >= 8:
                return
            hTm = hT[m % 2]
            ybg = ybg_t[m % 2]
            for g in range(4):
                c0 = 2048 + g * 128
                for k in range(8):
                    b.mm(pG[:], win[:, k, c0:c0 + 128], hTm[:, k, :], k == 0, k == 7, [win.b, hTm.b], [pG.b])
                sgm = sgm_t[g % 2]
                b.act(sgm[:], pG[:], AF.Silu, [pG.b], [sgm.b])
                b.tt("pool", ybg[:, g, :], xbT[g][:, m * 512:(m + 1) * 512], sgm[:], ALU.mult, [xbT[g].b, sgm.b], [ybg.b])

        def inpA(t, c0, pz):
            hTm = hT[(t // 4) % 2]
            j = t % 4
            for k in range(8):
                b.mm(pz[:], hTm[:, k, j * 128:(j + 1) * 128], win[:, k, c0:c0 + 512], k == 0, k == 7, [hTm.b, win.b], [pz.b])

        def VA(t):
            if t >= NT:
                return
            pv = sets[t % 2][1]
            inpA(t, 512, pv)
            s6, vh, vb = st6[t % 2], vh_t[t % 2], vb_t[t % 2]
            p.add("dve", lambda e: e.bn_stats(out=s6[:, 0:6], in_=pv[:]), [pv.b], [s6.b])
            p.add("dve", lambda e: e.bn_aggr(out=s6[:, 6:8], in_=s6[:, 0:6]), [s6.b], [s6.b])
            b.ts("pool", s6[:, 0:1], s6[:, 7:8], EPS, None, ALU.add, None, [s6.b], [s6.b])
            b.tt("pool", s6[:, 2:3], s6[:, 0:1], self.mhalf[:, 0:1], ALU.pow, [s6.b, self.mhalf.b], [s6.b])
            b.ts("dve", vh[:], pv[:], s6[:, 6:7], s6[:, 2:3], ALU.subtract, ALU.mult, [pv.b, s6.b], [vh.b])
            b.tt("pool", vb[:], vh[:], vg[:], ALU.mult, [vh.b, vg.b], [vb.b])

        def GaA(t):
            if t >= NT:
                return
            pg = sets[t % 2][2]
            inpA(t, 1024, pg)
            b.act(sg_t[t % 2][:], pg[:], AF.Silu, [pg.b], [sg_t[t % 2].b])

        def UA(t):
            if t >= NT:
                return
            pu = sets[t % 2][0]
            inpA(t, 0, pu)
            b.tt("dve", u_t[t % 2][:], pu[:], sg_t[t % 2][:], ALU.mult, [pu.b, sg_t[t % 2].b], [u_t[t % 2].b])

        def SA(t):
            pv = sets[t % 2][1]
            vb, u, ya = vb_t[t % 2], u_t[t % 2], ya_t[t % 2]
            for h in range(4):
                b.mm(pv[:, h * 128:(h + 1) * 128], swT[:, h, :], vb[:, h * 128:(h + 1) * 128], True, False,
                     [swT.b, vb.b], [pv.b])
                b.mm(pv[:, h * 128:(h + 1) * 128], sbk[0:33, h * 128:(h + 1) * 128], onesk[0:33, :], False, True,
                     [sbk.b, onesk.b], [pv.b])
            b.tt("dve", ya[:], pv[:], u[:], ALU.mult, [pv.b, u.b], [ya.b])

        def TyA(t):
            ya, yT = ya_t[t % 2], yT_t[t % 2]
            pX = self.pX
            for c in range(4):
                b.tr(pX[:, c * 128:(c + 1) * 128], ya[:, c * 128:(c + 1) * 128], self.identB[:], [ya.b, self.identB.b], [pX.b])
            b.cp("act", yT.ap(0, 128, 0, [[1, 512]]), pX[:, 0:512], [pX.b], [yT.b])

        xres = {}

        def LX(t):
            if t >= NT:
                return
            x_ = self.next("xres", xr)
            b.load(x_[:], d_x.ap()[t * 128:(t + 1) * 128, :], [x_.b])
            xres[t] = x_

        def OA(t):
            yT, ybg = yT_t[t % 2], ybg_t[(t // 4) % 2]
            j = t % 4
            x_ = xres.pop(t)
            banks = (sets[t % 2][0], sets[t % 2][2])
            for cb in range(2):
                pz = banks[cb]
                for k in range(8):
                    if k < 4:
                        ap_, bf_ = yT[:, k, :], yT.b
                    else:
                        ap_, bf_ = ybg[:, k - 4, j * 128:(j + 1) * 128], ybg.b
                    b.mm(pz[:], ap_, wout[:, k, cb * 512:(cb + 1) * 512], k == 0, k == 7, [bf_, wout.b], [pz.b])
                b.tt("dve", x_[:, cb * 512:(cb + 1) * 512], pz[:], x_[:, cb * 512:(cb + 1) * 512], ALU.add,
                     [pz.b, x_.b], [x_.b])
            p.dma("sp", d_x1.ap()[t * 128:(t + 1) * 128, :], x_[:], reads=[x_.b])

        HA1(0)
        for j in range(4):
            HA1(j + 1)
            HA2(j)
        GA(0)
        LX(0)
        LX(1)
        VA(0)
        GaA(0)
        UA(0)
        for t in range(NT):
            m, j = t // 4, t % 4
            LX(t + 2)
            VA(t + 1)
            SA(t)
            GaA(t + 1)
            TyA(t)
            UA(t + 1)
            HA1(t + 5)
            HA2(t + 4)
            OA(t)
            if j == 3:
                GA(m + 1)

    def layer1(self, d_x1, d_ctx1, d_out, ar):
        b = self
        p = self.p
        pZ, pS, pX, pZ4 = self.pZ, self.pS, self.pX, self.pZ4
        d_win = self.din("w_in_c", [D, 2560])
        d_wout = self.din("w_out_c", [D, D])
        d_sink = self.din("sink_logit", [1, 16])
        d_fg = self.din("final_g", [1, D])
        d_cos = self.din("rope_cos", [128, NT * 64])
        d_sin = self.din("rope_sin", [128, NT * 64])
        d_mask = self.din("wmask", [128, 256])

        winc = b.sbt("winc", [128, 8, 2560], BF16)
        wo1 = b.sbt("wo1", [128, 8, D], BF16)
        cosT = b.sbt("cosT", [128, NT * 64])
        sinT = b.sbt("sinT", [128, NT * 64])
        maskb = b.sbt("maskb", [128, 256], BF16)
        esink = b.sbt("esink", [128, 16])
        fg = b.sbt("fg", [128, D])
        gx1 = b.sbt("gx1", [128, D])
        NR = 4
        kTd = b.sbt("kTd", [128, 4, NR, 128], BF16)
        Vp = b.sbt("Vp", [128, NR, 4, 65], BF16)
        kcT = b.sbt("kcT", [128, 4, 2, 128], BF16)
        Vc = b.sbt("Vc", [128, 2, 4, 65], BF16)
        qT = [b.sbt("qT%d" % i, [128, 8, 128], BF16) for i in range(3)]
        sg = [b.sbt("sg1_%d" % i, [128, D]) for i in range(3)]
        hT1 = [b.sbt("hT1_%d" % i, [128, 8, 128], BF16) for i in range(2)]
        t1 = [b.sbt("rt1_%d" % i, [128, 512]) for i in range(2)]
        t2 = [b.sbt("rt2_%d" % i, [128, 512]) for i in range(2)]
        qr = [b.sbt("qr%d" % i, [128, D], BF16) for i in range(2)]
        krd = [b.sbt("krd%d" % i, [128, 4, 2, 64], BF16) for i in range(2)]
        den = [b.sbt("den%d" % i, [128, 8]) for i in range(2)]
        on_t = [b.sbt("on%d" % i, [128, 256]) for i in range(2)]
        og = [b.sbt("og%d" % i, [128, D], BF16) for i in range(2)]
        ogT = [b.sbt("ogT%d" % i, [128, 8, 128], BF16) for i in range(2)]
        ss2 = [b.sbt("ss2_%d" % i, [128, 4]) for i in range(2)]
        pSt = [pZ[2], pZ[3], pS]
        pO = [T(self.pA.t.bitcast(F32), "pO0", self.pA.b)]
        self.pXs = [self.pX, self.pB]
        Arow1 = b.sbt("Arow1", [128, D])

        old = ar.reset() + list(getattr(self, "l1_old", []))
        l1_new = [winc, wo1, cosT, sinT, maskb, esink, fg, gx1, Arow1, kTd, Vp, kcT, Vc] + qT + sg + hT1 + t1 + t2 + qr + krd + den \
            + on_t + og + ogT + ss2
        mod1 = ar.alloc("mod1", [128, 3 * D])
        nst = 4 if ar.cap >= 50 * 1024 else 2
        awst = [ar.alloc("awst1_%d" % i, [128, 8, 256]) for i in range(nst)]
        scd = ar.alloc("scdup1", [128, 8, 128])
        adab = [ar.alloc("adab1_%d" % i, [128, 256]) for i in range(2)]
        b.retarget(old, l1_new + [mod1, scd] + awst + adab)
        scdup = b.make_scdup(ar, scd)
        AB1 = b.adaln(1, mod1, scdup, awst, adab, pZ[0:2], pZ4[2:4])
        b.gate_bc(gx1, mod1, 0, pZ[0:2])
        gbc = T(awst[0].t, "gbc1", awst[0].b, awst[0].base, [128, D])
        b.load(gbc[:], bass.AP(self.d_ngrow, D, [[0, 128], [1, D]]), [gbc.b])
        b.arow_bc(Arow1, mod1, 1, pZ[0:2], gbc)
        engs3 = ["dve", "act"]
        stv = [T(a.t, "stv1", a.b, a.base, [128, 2048]) for a in awst]
        for k in range(8):
            for hh in range(2):
                st = stv[(k * 2 + hh) % len(stv)]
                b.loadw(st[:, 0:1280], d_win.ap()[k * 128:(k + 1) * 128, hh * 1280:(hh + 1) * 1280], [st.b])
                b.cp(b.pick("wcast", engs3), winc[:, k, hh * 1280:(hh + 1) * 1280], st[:, 0:1280], [st.b], [winc.b])
        for k in range(8):
            st = stv[k % len(stv)]
            b.loadw(st[:, 0:D], d_wout.ap()[k * 128:(k + 1) * 128, :], [st.b])
            b.tt(b.pick("wos", ["dve", "pool"]), wo1[:, k, :], st[:, 0:D], gx1[:], ALU.mult, [st.b, gx1.b], [wo1.b])
        b.load(cosT[:], d_cos.ap(), [cosT.b])
        b.load(sinT[:], d_sin.ap(), [sinT.b])
        st = stv[0]
        b.load(st[:, 0:256], d_mask.ap(), [st.b])
        b.cp("dve", maskb[:], st[:, 0:256], [st.b], [maskb.b])
        b.load(esink[:], bass.AP(d_sink, 0, [[0, 128], [1, 16]]), [esink.b])
        b.act(esink[:], esink[:], AF.Exp, [esink.b], [esink.b])
        b.ts("dve", esink[:], esink[:], 2.0, None, ALU.mult, None, [esink.b], [esink.b])
        b.load(fg[:], bass.AP(d_fg, 0, [[0, 128], [1, D]]), [fg.b])
        p.add("pool", lambda e: e.memset(Vp[:], 1.0), writes=[Vp.b])
        p.add("pool", lambda e: e.memset(Vc[:], 1.0), writes=[Vc.b])
        for kk in krd:
            p.add("pool", lambda e, kk=kk: e.memset(kk[:], 0.0), writes=[kk.b])
        oldp = ar.reset()
        x1t = [ar.alloc("x1t%d" % i, [128, D]) for i in range(5)]
        PT = [ar.alloc("PT%d" % i, [128, 512], BF16) for i in range(12)]
        b.retarget(oldp, x1t + PT)

        def hT_tile1(x_, xc, hTt):
            b.hT_tile(x_, AB1, xc, (hTt.ap(0, 128, 0, [[128, 8], [1, 128]]), hTt.b), Arow=(Arow1 if xc == 0 else None))

        def inproj(hTt, c0, pz, ncols=512):
            for k in range(8):
                b.mm(pz[:, 0:ncols], hTt[:, k, :], winc[:, k, c0:c0 + ncols], k == 0, k == 7, [hTt.b, winc.b], [pz.b])

        def rope(pz, col0, nh, t, outs):
            a1 = self.next("rt1", t1)
            a2 = self.next("rt2", t2)
            n = nh * 64
            cosb = cosT.ap(0, 128, t * 64, [[0, nh], [1, 64]])
            b.tt("dve", a1.ap(0, 128, 0, [[64, nh], [1, 64]]), pz.ap(0, 128, col0, [[64, nh], [1, 64]]), cosb, ALU.mult,
                 [pz.b, cosT.b], [a1.b])
            for hf in range(2):
                o_ = a2.ap(0, 128, hf * 16, [[64, nh], [32, 2], [1, 16]])
                i_ = pz.ap(0, 128, col0 + (1 - hf) * 16, [[64, nh], [32, 2], [1, 16]])
                s_ = sinT.ap(0, 128, t * 64 + hf * 16, [[0, nh], [32, 2], [1, 16]])
                b.tt("dve", o_, i_, s_, ALU.mult, [pz.b, sinT.b], [a2.b])
            for o_ in outs:
                oap, obuf = o_[0], o_[1]
                dims = o_[2] if len(o_) > 2 else [[64, nh], [1, 64]]
                b.tt("pool", oap, a1.ap(0, 128, 0, dims), a2.ap(0, 128, 0, dims), ALU.add, [a1.b, a2.b], [obuf])

        for j in range(2):
            x_ = self.next("x1t", x1t)
            b.load(x_[:], d_ctx1.ap()[j * 128:(j + 1) * 128, :], [x_.b])
            hTt = hT1[j % 2]
            hT_tile1(x_, 1, hTt)
            pz = pZ[j % 2]
            inproj(hTt, 1024, pz)
            kc = krd[j % 2]
            b.cp("dve", kc.ap(0, 128, 0, [[320, 2], [128, 2], [1, 64]]), pz.ap(0, 128, 0, [[128, 2], [64, 2], [1, 64]]),
                 [pz.b], [kc.b])
            b.cp("act", Vc.ap(0, 128, j * 260, [[65, 4], [1, 64]]), pz.ap(0, 128, 256, [[64, 4], [1, 64]]), [pz.b], [Vc.b])
            for kh in range(4):
                b.tr(pX[:, kh * 128:(kh + 1) * 128], kc.ap(0, 128, kh * 128, [[1, 128]]), self.identB[:], [kc.b, self.identB.b], [pX.b])
            b.cp("act", kcT.ap(0, 128, j * 128, [[256, 4], [1, 128]]), pX.ap(0, 128, 0, [[128, 4], [1, 128]]), [pX.b], [kcT.b])

        def stageA(t):
            x_ = self.next("x1t", x1t)
            xs[t] = x_
            b.load(x_[:], d_x1.ap()[t * 128:(t + 1) * 128, :], [x_.b])
            yield
            hTt = hT1[t % 2]
            n_ = self.next("xn", self.xn)
            s0 = self.next("ss", self.ss)
            b.sumsq(x_, s0, "act")
            b.rstd_from_ss(s0)
            self.p.add("dve", lambda e: e.scalar_tensor_tensor(out=n_[:], in0=x_[:], scalar=s0[:, 3:4], in1=Arow1[:],
                                                              op0=ALU.mult, op1=ALU.mult), [x_.b, s0.b, Arow1.b], [n_.b])
            yield
            pX = self.next("pXr", self.pXs)
            for c in range(8):
                b.tr(pX[:, c * 128:(c + 1) * 128], n_[:, c * 128:(c + 1) * 128], self.identB[:], [n_.b, self.identB.b], [pX.b])
            b.tt("dve", hTt.ap(0, 128, 0, [[128, 8], [1, 128]]), pX.ap(0, 128, 0, [[128, 8], [1, 128]]),
                 AB1.ap(0, 128, 8, [[1, 8], [0, 128]]), ALU.add, [pX.b, AB1.b], [hTt.b])
            yield
            q_ = qr[t % 2]
            for qb in range(2):
                pz = self.next("pzin", pZ[0:2])
                inproj(hTt, qb * 512, pz)
                rope(pz, 0, 8, t, [(q_.ap(0, 128, qb * 512, [[64, 8], [1, 64]]), q_.b)])
                yield
            pz = self.next("pzin", pZ[0:2])
            inproj(hTt, 1024, pz)
            kc = krd[t % 2]
            slot = t % NR
            b.cp("act", Vp.ap(0, 128, slot * 260, [[65, 4], [1, 64]]), pz.ap(0, 128, 256, [[64, 4], [1, 64]]), [pz.b], [Vp.b])
            rope(pz, 0, 4, t, [(kc.ap(0, 128, 0, [[320, 2], [128, 2], [1, 64]]), kc.b, [[128, 2], [64, 2], [1, 64]])])
            yield
            s_ = sg[t % 3]
            for gb in range(2):
                pz = self.next("pzin", pZ[0:2])
                inproj(hTt, 1536 + gb * 512, pz)
                b.act(s_[:, gb * 512:(gb + 1) * 512], pz[:], AF.Tanh, [pz.b], [s_.b], scale=0.5)
                p.add("dve", lambda e, s_=s_, pz=pz, gb=gb: e.scalar_tensor_tensor(
                    out=s_[:, gb * 512:(gb + 1) * 512], in0=s_[:, gb * 512:(gb + 1) * 512], scalar=1.0, in1=pz[:],
                    op0=ALU.add, op1=ALU.mult), [s_.b, pz.b], [s_.b])
                yield
            pX = self.next("pXr", self.pXs)
            for h in range(16):
                r0 = 64 * (h // 8)
                b.tr(pX[r0:r0 + 64, (h % 8) * 128:(h % 8 + 1) * 128], q_[:, h * 64:(h + 1) * 64], self.identB[:],
                     [q_.b, self.identB.b], [pX.b])
            b.cp("act", qT[t % 3].ap(0, 128, 0, [[1, 1024]]), pX[:], [pX.b], [qT[t % 3].b])
            pX = self.next("pXr", self.pXs)
            for kh in range(4):
                b.tr(pX[:, kh * 128:(kh + 1) * 128], kc.ap(0, 128, kh * 128, [[1, 128]]), self.identB[:], [kc.b, self.identB.b], [pX.b])
            b.cp("dve", kTd.ap(0, 128, slot * 128, [[NR * 128, 4], [1, 128]]), pX.ap(0, 128, 0, [[128, 4], [1, 128]]), [pX.b], [kTd.b])
            yield

        def stageB(n):
            x_ = xs.pop(n)
            qTn = qT[n % 3]
            s_ = sg[n % 3]
            o_ = og[n % 2]
            blocks = [("c", 0, None), ("c", 1, None)]
            if n > 0:
                blocks.append(("w", (n - 1) % NR, 0))
            blocks.append(("w", n % NR, None))
            if n < NT - 1:
                blocks.append(("w", (n + 1) % NR, 1))
            rounds = [blocks[i:i + 2] for i in range(0, len(blocks), 2)]
            ptss = {}

            def QK(kh):
                pts = {}
                rt = 64 * (kh // 2)
                c4 = 4 * (kh % 2)
                for (kind, idx, mk) in blocks:
                    ps_ = self.next("pSt", pSt)
                    pt = self.next("PT", PT)
                    if kind == "c":
                        lhs = kcT[:, kh, idx, :]
                        lb = kcT.b
                    else:
                        lhs = kTd[:, kh, idx, :]
                        lb = kTd.b
                    b.mm(ps_[:], lhs, qTn[:, c4:c4 + 4, :], True, True, [lb, qTn.b], [ps_.b])
                    b.act(pt[:], ps_[:], AF.Exp, [ps_.b], [pt.b], scale=0.125)
                    if mk is not None:
                        b.tt(b.pick("mask_eng", ["dve", "pool"]), pt.ap(0, 128, 0, [[128, 4], [1, 128]]), pt.ap(0, 128, 0, [[128, 4], [1, 128]]),
                             maskb.ap(0, 128, mk * 128, [[0, 4], [1, 128]]), ALU.mult, [pt.b, maskb.b], [pt.b])
                    pts[(kind, idx)] = pt
                ptss[kh] = pts

            def PV(kh):
                pts = ptss[kh]
                po = pO[0]
                for hl in range(4):
                    for bi2, (kind, idx, mk) in enumerate(blocks):
                        pt = pts[(kind, idx)]
                        if kind == "c":
                            rhs = Vc[:, idx, kh, :]
                            rb = Vc.b
                        else:
                            rhs = Vp[:, idx, kh, :]
                            rb = Vp.b
                        b.mm(po[:, hl * 65:(hl + 1) * 65], pt[:, hl * 128:(hl + 1) * 128], rhs,
                             bi2 == 0, bi2 == len(blocks) - 1, [pt.b, rb], [po.b])
                dn = den[kh % 2]
                p.add("dve", lambda e, dn=dn, po=po, kh=kh: e.scalar_tensor_tensor(
                    out=dn[:, 0:4], in0=po.ap(0, 128, 64, [[65, 4]]), scalar=2.0, in1=esink[:, 4 * kh:4 * kh + 4],
                    op0=ALU.mult, op1=ALU.add), [po.b, esink.b], [dn.b])
                p.add("dve", lambda e, dn=dn: e.reciprocal(out=dn[:, 4:8], in_=dn[:, 0:4]), [dn.b], [dn.b])
                ot = on_t[kh % 2]
                b.tt("dve", ot.ap(0, 128, 0, [[64, 4], [1, 64]]), po.ap(0, 128, 0, [[65, 4], [1, 64]]),
                     dn.ap(0, 128, 4, [[1, 4], [0, 64]]), ALU.mult, [po.b, dn.b], [ot.b])
                b.tt("pool", o_[:, kh * 256:(kh + 1) * 256], ot[:], s_[:, kh * 256:(kh + 1) * 256], ALU.mult, [ot.b, s_.b], [o_.b])

            QK(0)
            yield
            QK(1)
            yield
            PV(0)
            yield
            QK(2)
            yield
            PV(1)
            yield
            QK(3)
            yield
            PV(2)
            yield
            PV(3)
            yield
            oT = ogT[n % 2]
            pX = self.next("pXr", self.pXs)
            for c in range(8):
                b.tr(pX[:, c * 128:(c + 1) * 128], o_[:, c * 128:(c + 1) * 128], self.identB[:], [o_.b, self.identB.b], [pX.b])
            b.cp("act", oT.ap(0, 128, 0, [[1, 1024]]), pX[:], [pX.b], [oT.b])
            yield
            for cb in range(2):
                pz = self.next("pzin", pZ[0:2])
                for k in range(8):
                    b.mm(pz[:], oT[:, k, :], wo1[:, k, cb * 512:(cb + 1) * 512], k == 0, k == 7, [oT.b, wo1.b], [pz.b])
                b.tt("dve", x_[:, cb * 512:(cb + 1) * 512], pz[:], x_[:, cb * 512:(cb + 1) * 512], ALU.add, [pz.b, x_.b], [x_.b])
            yield
            s2 = ss2[n % 2]
            b.sumsq(x_, s2, "act")
            b.rstd_from_ss(s2)
            p.add("dve", lambda e: e.scalar_tensor_tensor(out=x_[:], in0=x_[:], scalar=s2[:, 3:4], in1=fg[:],
                                                          op0=ALU.mult, op1=ALU.mult), [x_.b, s2.b, fg.b], [x_.b])
            p.dma("sp", d_out.ap()[n * 128:(n + 1) * 128, :], x_[:], reads=[x_.b])
            yield

        xs = {}
        nt_run = self.nt_l1 if self.nt_l1 is not None else NT
        nb_run = nt_run if nt_run == NT else nt_run - 1
        gA, gB = {}, {}

        def stepA(t):
            if 0 <= t < nt_run:
                if t not in gA:
                    gA[t] = stageA(t)
                next(gA[t], None)

        def stepB(n):
            if 0 <= n < nb_run:
                if n not in gB:
                    gB[n] = stageB(n)
                next(gB[n], None)

        order = "lbAbbAbbAbbAbaAobAnb"
        stepA(0)
        stepA(0)
        stepA(0)
        for t in range(nt_run + 3):
            for ch in order:
                if ch == "a" or ch == "l" or ch == "n":
                    stepA(t + 1)
                elif ch == "A":
                    stepA(t)
                elif ch == "o":
                    stepB(t - 3)
                else:
                    stepB(t - 2)


def build_l0(stop=None):
    B = Builder("l0")
    B.stop = stop
    d_x = B.din("x", [S, D])
    d_ctx = B.din("ctx", [LC, D])
    d_x1 = B.dout("x1", [S, D])
    d_ctx1 = B.dout("ctx1", [LC, D])
    B.setup_common()
    B.layer0(d_x, d_ctx, d_x1, d_ctx1)
    B.p.emit()
    print("l0 stats", B.p.stats)
    return B


_TB = None


L0_KEYS = ("x", "ctx", "cc", "ng", "ngrow", "ident", "ada_w", "ada_b", "w_in_ab", "w_out_ab", "v_norm_g", "spatial_w",
           "spatial_b", "tab1", "tab2", "fc", "tabc")
L1_KEYS = ("cc", "ng", "ngrow", "ident", "ada_w", "ada_b", "w_in_c", "w_out_c", "sink_logit", "final_g", "rope_cos",
           "rope_sin", "wmask")


def host_inputs(inputs):
    global _TB
    if _TB is None:
        _TB = _tables()
    f = lambda a: np.ascontiguousarray(np.asarray(a, dtype=np.float32))
    x, c, ctx, c_ctx = f(inputs["x"]), f(inputs["c"]), f(inputs["ctx"]), f(inputs["c_ctx"])
    norm_g = f(inputs["norm_g"])
    common = {
        "ident": _TB["ident"], "ada_w": f(inputs["ada_w"]), "ada_b": f(inputs["ada_b"]),
        "w_in_ab": f(inputs["w_in_ab"])[0], "w_out_ab": f(inputs["w_out_ab"])[0],
        "v_norm_g": f(inputs["v_norm_g"]).reshape(1, 512),
        "spatial_w": f(inputs["spatial_w"])[0], "spatial_b": f(inputs["spatial_b"]).reshape(1, 512),
        "tab1": _TB["tab1"], "tab2": _TB["tab2"], "fc": _TB["fc"], "tabc": _TB["tabc"],
        "w_in_c": f(inputs["w_in_c"])[0], "w_out_c": f(inputs["w_out_c"])[0],
        "sink_logit": f(inputs["sink_logit"]).reshape(1, 16), "final_g": f(inputs["final_g"]).reshape(1, D),
        "rope_cos": _TB["rope_cos"], "rope_sin": _TB["rope_sin"], "wmask": _TB["wmask"],
    }
    ng = np.concatenate([norm_g[0].reshape(8, 128).T, norm_g[1].reshape(8, 128).T], axis=1)
    maps = []
    for bi in range(x.shape[0]):
        cc = np.concatenate([c[bi].reshape(8, 128).T, c_ctx.reshape(8, 128).T], axis=1)
        m = dict(common)
        m.update({"x": x[bi], "ctx": ctx[bi], "cc": f(cc), "ng": f(ng), "ngrow": norm_g})
        maps.append(m)
    return maps


def build_l1(nt=None):
    B = Builder("l1")
    B.nt_l1 = nt
    d_x1 = B.din("x1", [S, D])
    d_ctx1 = B.din("ctx1", [LC, D])
    d_out = B.dout("out", [S, D])
    B.setup_common(l0=False)
    ar = Arena(B.p, "arL1", 36)
    B.layer1(d_x1, d_ctx1, d_out, ar)
    B.p.emit()
    print("l1 stats", B.p.stats)
    return B


def build_fused():
    B = Builder("fused")
    nc = B.nc
    d_x = B.din("x", [S, D])
    d_ctx = B.din("ctx", [LC, D])
    d_x1 = nc.dram_tensor("x1_scratch", [S, D], F32, kind="Internal")
    d_ctx1 = nc.dram_tensor("ctx1_scratch", [LC, D], F32, kind="Internal")
    d_out = B.dout("out", [S, D])
    B.setup_common(l0=True)
    main = Arena(B.p, "main", MAIN_KIB)
    B.cur_arena = main
    B.layer0(d_x, d_ctx, d_x1, d_ctx1)
    old = main.reset()
    B.l1_old = old
    ar = Arena(B.p, "arL1", 52, parent=main)
    B.layer1(d_x1, d_ctx1, d_out, ar)
    B.cur_arena = None
    B.p.emit()
    print("fused stats", B.p.stats)
    return B


MAIN_KIB = 196
ALL_KEYS = tuple(dict.fromkeys(L0_KEYS + L1_KEYS))
_PROGS = {}


def kernel(**inputs):
    maps = host_inputs(inputs)
    n = len(maps)
    if "fused" not in _PROGS:
        _PROGS["fused"] = build_fused()
    r = run_bass_kernel_spmd(_PROGS["fused"].nc, [{k: m[k] for k in ALL_KEYS} for m in maps], core_ids=list(range(n)))
    return np.stack([np.asarray(r.results[i]["out"], dtype=np.float32) for i in range(n)], axis=0)
```

```python
import numpy as np
import concourse.bass as bass
import concourse.mybir as mybir
from concourse.bass_utils import run_bass_kernel_spmd
from contextlib import ExitStack

F32 = mybir.dt.float32
BF16 = mybir.dt.bfloat16
AF = mybir.ActivationFunctionType
ALU = mybir.AluOpType

COMPUTE = ("pe", "act", "dve", "pool")
ALL_ENG = ("pe", "act", "dve", "pool", "sp")

D = 1024
S = 4096
LC = 256
NT = S // 128
EPS = 1e-6


class Buf:
    __slots__ = ("name", "last_w", "readers", "excl")

    def __init__(self, name):
        self.name = name
        self.last_w = None
        self.readers = []
        self.excl = False


class Op:
    __slots__ = ("eng", "fn", "deps", "signal", "sig_val", "is_dma", "dma_slot", "dma_val",
                 "pre_dma_wait")

    def __init__(self, eng, fn, is_dma=False):
        self.eng = eng
        self.fn = fn
        self.deps = []
        self.signal = False
        self.sig_val = None
        self.is_dma = is_dma
        self.dma_slot = None
        self.dma_val = None
        self.pre_dma_wait = None


class Prog:
    N_DMA_SEMS = 24
    STRICT_SAME_ENGINE = True

    def __init__(self, nc):
        self.nc = nc
        self.ops = {e: [] for e in ALL_ENG}
        self.dma_count = {e: 0 for e in ALL_ENG}
        self.stack = ExitStack()

    def sb(self, name, shape, dtype=F32):
        return self.stack.enter_context(self.nc.sbuf_tensor(name, list(shape), dtype))

    def ps(self, name, shape, dtype=F32):
        return self.stack.enter_context(self.nc.psum_tensor(name, list(shape), dtype))

    def _dep(self, op, w):
        if w is None or w is op:
            return
        w.signal = True
        op.deps.append(w)

    def add(self, eng, fn, reads=(), writes=(), is_dma=False):
        op = Op(eng, fn, is_dma)
        for b in reads:
            w = b.last_w
            if w is not None:
                self._dep(op, w)
            if b.excl:
                for r in b.readers:
                    if r.eng != eng:
                        self._dep(op, r)
        strict = self.STRICT_SAME_ENGINE and eng != "pe"
        for b in writes:
            w = b.last_w
            if w is not None and (w.eng != eng or w.is_dma or is_dma or strict):
                self._dep(op, w)
            for r in b.readers:
                if r.eng != eng or r.is_dma or is_dma or strict:
                    self._dep(op, r)
        for b in reads:
            if not is_dma:
                b.readers = [r for r in b.readers if r.eng != eng or r.is_dma]
            b.readers.append(op)
        for b in writes:
            b.last_w = op
            b.readers = []
        if is_dma:
            j = self.dma_count[eng]
            self.dma_count[eng] = j + 1
            op.dma_slot = j % self.N_DMA_SEMS
            op.dma_val = 16 * (j // self.N_DMA_SEMS + 1)
            op.pre_dma_wait = 16 * (j // self.N_DMA_SEMS)
        self.ops[eng].append(op)
        return op

    def dma(self, eng, out, in_, reads=(), writes=(), **kw):
        return self.add(eng, lambda e: e.dma_start(out=out, in_=in_, **kw), reads, writes, is_dma=True)

    def emit(self):
        nc = self.nc
        st = self.stack
        esem = {e: st.enter_context(nc.semaphore("s_" + e)) for e in COMPUTE}
        dsem = {}
        for e in ALL_ENG:
            if self.dma_count[e] > 0:
                dsem[e] = [st.enter_context(nc.semaphore("d_%s_%d" % (e, i)))
                           for i in range(min(self.N_DMA_SEMS, self.dma_count[e]))]
        for e in ALL_ENG:
            c = 0
            for op in self.ops[e]:
                if op.is_dma:
                    continue
                if op.signal:
                    c += 1
                    op.sig_val = c
        stats = {e: [0, 0] for e in ALL_ENG}

        def run(ename):
            def body(e):
                waited = {}
                for op in self.ops[ename]:
                    need = {}
                    for w in op.deps:
                        if w.is_dma:
                            key = ("d", w.eng, w.dma_slot)
                            val = w.dma_val
                        else:
                            key = ("e", w.eng)
                            val = w.sig_val
                        if need.get(key, 0) < val:
                            need[key] = val
                    if op.is_dma and op.pre_dma_wait > 0:
                        key = ("d", ename, op.dma_slot)
                        if need.get(key, 0) < op.pre_dma_wait:
                            need[key] = op.pre_dma_wait
                    for key, val in need.items():
                        if waited.get(key, 0) >= val:
                            continue
                        waited[key] = val
                        sem = esem[key[1]] if key[0] == "e" else dsem[key[1]][key[2]]
                        e.wait_ge(sem, val)
                        stats[ename][1] += 1
                    inst = op.fn(e)
                    stats[ename][0] += 1
                    if op.is_dma:
                        inst.then_inc(dsem[ename][op.dma_slot], 16)
                    elif op.signal:
                        inst.then_inc(esem[ename], 1)
                if ename in dsem:
                    nd = self.dma_count[ename]
                    for s in range(len(dsem[ename])):
                        cnt = (nd - 1 - s) // self.N_DMA_SEMS + 1 if nd > s else 0
                        if cnt > 0 and waited.get(("d", ename, s), 0) < 16 * cnt:
                            e.wait_ge(dsem[ename][s], 16 * cnt)
            return body

        with nc.Block() as block:
            if self.ops["sp"]:
                block.sync(run("sp"))
            if self.ops["pe"]:
                block.tensor(run("pe"))
            if self.ops["act"]:
                block.scalar(run("act"))
            if self.ops["dve"]:
                block.vector(run("dve"))
            if self.ops["pool"]:
                block.gpsimd(run("pool"))
        self.stats = stats
        st.close()


class T:
    def __init__(self, t, name, buf=None, base=0, shape=None):
        self.t = t
        self.name = name
        self.b = buf if buf is not None else Buf(name)
        self.rowfull = int(np.prod(list(t.shape)[1:]))
        self.base = base
        self.shape = list(shape) if shape is not None else list(t.shape)
        self.size = int(np.prod(self.shape[1:]))

    def ap(self, p0, npart, off, dims):
        return bass.AP(self.t, p0 * self.rowfull + self.base + off, [[self.rowfull, npart]] + [list(d) for d in dims])

    def view(self):
        if len(self.t.shape) != 2:
            assert self.base == 0
            return self.t
        v = self.t[:, self.base:self.base + self.size]
        if len(self.shape) == 3:
            v = v.rearrange("p (a b) -> p a b", a=self.shape[1], b=self.shape[2])
        elif len(self.shape) == 4:
            v = v.rearrange("p (a b c) -> p a b c", a=self.shape[1], b=self.shape[2], c=self.shape[3])
        return v

    def __getitem__(self, idx):
        return self.view()[idx]


class Arena:
    def __init__(self, prog, name, kib, parent=None):
        self.cap = int(kib * 1024)
        if parent is None:
            self.h = prog.sb(name, [128, self.cap // 2], BF16)
            self.views = {BF16: self.h, F32: self.h.bitcast(F32)}
            self.org = 0
        else:
            parent.off = (parent.off + 31) // 32 * 32
            assert parent.off + self.cap <= parent.cap, (name, parent.off, self.cap, parent.cap)
            self.views = parent.views
            self.org = parent.org + parent.off
            parent.off += self.cap
        self.parent = parent
        self.off = 0
        self.live = []
        self.children = []
        if parent is not None:
            parent.children.append(self)

    def all_live(self):
        out = list(self.live)
        for c in self.children:
            out += c.all_live()
        return out

    def reset(self):
        old = self.all_live()
        self.off = 0
        self.live = []
        self.children = []
        return old

    def alloc(self, name, shape, dt=F32):
        esz = 2 if dt == BF16 else 4
        n = int(np.prod(list(shape)[1:]))
        self.off = (self.off + 31) // 32 * 32
        assert self.off + n * esz <= self.cap, (name, self.off, n * esz, self.cap)
        t = T(self.views[dt], name, None, (self.org + self.off) // esz, shape)
        self.off += n * esz
        self.live.append(t)
        return t


def _tables():
    tb = {}
    tb["ident"] = np.eye(128, dtype=np.float32)
    r2 = np.arange(2)[:, None, None, None]
    a = np.arange(64)[None, :, None, None]
    j = np.arange(32)[None, None, :, None]
    k1 = np.arange(64)[None, None, None, :]
    n = 64 * a + 2 * j + r2
    th = 2 * np.pi * ((k1 * n) % 4096) / 4096.0
    t1 = np.concatenate([np.cos(th), -np.sin(th)], axis=-1) / 8.0
    tb["tab1"] = np.ascontiguousarray(t1.reshape(128, 32 * 128)).astype(np.float32)
    r = np.arange(64)[:, None]
    k2 = np.arange(64)[None, :]
    ph = 2 * np.pi * ((r * k2) % 64) / 64.0
    t2 = np.concatenate([np.cos(ph), np.sin(ph)], axis=1) / 8.0
    tb["tab2"] = np.concatenate([t2, t2], axis=0).astype(np.float32)
    c = np.arange(128)[:, None]
    c2 = np.arange(128)[None, :]
    al = 2 * np.pi * ((c * c2) % 128) / 128.0
    Cc, Sc = np.cos(al) / np.sqrt(128.0), np.sin(al) / np.sqrt(128.0)
    tb["fc"] = np.concatenate([Cc, -Sc, Sc, Cc], axis=1).astype(np.float32)
    nn = np.arange(256)[:, None]
    kk = np.arange(256)[None, :]
    be = 2 * np.pi * ((nn * kk) % 256) / 256.0
    tc = np.concatenate([np.cos(be), -np.sin(be)], axis=1) / 16.0
    tb["tabc"] = np.ascontiguousarray(tc.reshape(2, 128, 512).transpose(1, 0, 2).reshape(128, 1024)).astype(np.float32)
    tok = np.arange(S)
    row, col = tok // 64, tok % 64
    inv = 10000.0 ** (-np.arange(16) / 16.0)
    dd = np.arange(64)
    pos = np.where(dd[None, :] < 32, row[:, None], col[:, None]).astype(np.float64)
    ang = (pos.astype(np.float32) * inv[dd % 16][None, :].astype(np.float32)).astype(np.float32).astype(np.float64)
    cs = np.cos(ang)
    sn = np.sin(ang) * np.where((dd % 32) < 16, -1.0, 1.0)[None, :]
    tb["rope_cos"] = np.ascontiguousarray(cs.reshape(NT, 128, 64).transpose(1, 0, 2).reshape(128, NT * 64)).astype(np.float32)
    tb["rope_sin"] = np.ascontiguousarray(sn.reshape(NT, 128, 64).transpose(1, 0, 2).reshape(128, NT * 64)).astype(np.float32)
    jj = np.arange(128)[:, None]
    ii = np.arange(128)[None, :]
    tb["wmask"] = np.concatenate([(jj >= ii), (jj <= ii)], axis=1).astype(np.float32)
    return tb


class Builder:
    def __init__(self, mode):
        self.mode = mode
        self.stop = None
        self.nt_l1 = None
        self.cur_arena = None
        self.nc = bass.Bass("TRN2", target_bir_lowering=False)
        self.p = Prog(self.nc)
        self.rr = {}

    def din(self, name, shape, dt=F32):
        return self.nc.dram_tensor(name, list(shape), dt, kind="ExternalInput")

    def dout(self, name, shape, dt=F32):
        return self.nc.dram_tensor(name, list(shape), dt, kind="ExternalOutput")

    def sbt(self, name, shape, dt=F32, buf=None):
        if self.cur_arena is not None:
            return self.cur_arena.alloc(name, shape, dt)
        return T(self.p.sb("s_" + name, shape, dt), name, buf)

    def pst(self, name, shape, dt=F32):
        t = T(self.p.ps("p_" + name, shape, dt), name)
        t.b.excl = True
        return t

    def pick(self, key, engines):
        i = self.rr.get(key, 0)
        self.rr[key] = i + 1
        return engines[i % len(engines)]

    def mm(self, out, lhsT, rhs, start, stop, reads, writes):
        self.p.add("pe", lambda e: e.matmul(out=out, lhsT=lhsT, rhs=rhs, start=start, stop=stop), reads, writes)

    def tr(self, out, in_, ident, reads, writes):
        self.p.add("pe", lambda e: e.transpose(out=out, in_=in_, identity=ident), reads, writes)

    def act(self, out, in_, func, reads, writes, scale=None, bias=None, accum=None):
        kw = {}
        if scale is not None:
            kw["scale"] = scale
        if bias is not None:
            kw["bias"] = bias
        if accum is not None:
            kw["accum_out"] = accum
        self.p.add("act", lambda e: e.activation(out=out, in_=in_, func=func, **kw), reads, writes)

    def tt(self, eng, out, in0, in1, op, reads, writes):
        self.p.add(eng, lambda e: e.tensor_tensor(out=out, in0=in0, in1=in1, op=op), reads, writes)

    def ts(self, eng, out, in0, s1, s2, op0, op1, reads, writes):
        if op1 is None:
            self.p.add(eng, lambda e: e.tensor_scalar(out=out, in0=in0, scalar1=s1, scalar2=None, op0=op0), reads, writes)
        else:
            self.p.add(eng, lambda e: e.tensor_scalar(out=out, in0=in0, scalar1=s1, scalar2=s2, op0=op0, op1=op1), reads, writes)

    def cp(self, eng, out, in_, reads, writes):
        if eng == "act":
            self.p.add("act", lambda e: e.copy(out=out, in_=in_), reads, writes)
        else:
            self.p.add(eng, lambda e: e.tensor_copy(out=out, in_=in_), reads, writes)

    def load(self, out, in_, writes, reads=(), q=None):
        self.p.dma(q if q is not None else "sp", out, in_, reads=reads, writes=writes)

    def loadw(self, out, in_, writes):
        self.load(out, in_, writes, q=self.pick("wq", ["sp", "act"]))

    def dump(self, name, t, ap=None, shape=None, dt=F32):
        shape = list(shape if shape is not None else t.shape)
        d = self.dout("dbg_" + name, shape, dt)
        self.p.dma("sp", d.ap(), ap if ap is not None else t[:], reads=[t.b])

    def fence(self, tiles):
        bufs = [t.b for t in tiles]
        self.p.add("pool", lambda e: e.memset(self.fz[:, 0:1], 0.0), writes=bufs + [self.fz.b])

    def retarget(self, old_tiles, new_tiles):
        if not old_tiles:
            return
        bufs = [t.b for t in old_tiles] + [t.b for t in new_tiles]
        self.p.add("pool", lambda e: e.memset(self.fz[:, 0:1], 0.0), writes=bufs + [self.fz.b])

    def setup_common(self, l0=True):
        b = self
        self.d_cc = self.din("cc", [128, 16])
        self.d_ng = self.din("ng", [128, 16])
        self.d_ident = self.din("ident", [128, 128])
        self.d_ada_w = self.din("ada_w", [2, D, 3 * D])
        self.d_ada_b = self.din("ada_b", [2, 3 * D])
        self.fz = b.sbt("fz", [128, 8])
        self.identF = b.sbt("identF", [128, 128])
        self.identB = b.sbt("identB", [128, 128], BF16)
        b.load(self.identF[:], self.d_ident.ap(), [self.identF.b])
        b.cp("dve", self.identB[:], self.identF[:], [self.identF.b], [self.identB.b])
        self.onesF = b.sbt("onesF", [128, 128])
        self.p.add("pool", lambda e: e.memset(self.onesF[:], 1.0), writes=[self.onesF.b])
        self.cc = b.sbt("cc_t", [128, 16])
        self.ng = b.sbt("ng_t", [128, 16])
        b.load(self.cc[:], self.d_cc.ap(), [self.cc.b])
        b.load(self.ng[:], self.d_ng.ap(), [self.ng.b])
        self.sc = b.sbt("sc_t", [128, 16])
        b.act(self.sc[:], self.cc[:], AF.Silu, [self.cc.b], [self.sc.b])
        self.mhalf = b.sbt("mhalf", [128, 1])
        self.p.add("pool", lambda e: e.memset(self.mhalf[:], -0.5), writes=[self.mhalf.b])
        self.pZ = [b.pst("pZ%d" % i, [128, 512]) for i in range(4)]
        self.pS = b.pst("pS", [128, 512])
        self.pX = b.pst("pX", [128, 1024], BF16)
        self.pA = b.pst("pA", [128, 1024], BF16)
        self.pB = b.pst("pB", [128, 1024], BF16)
        self.pXs = [self.pX]
        self.d_ngrow = self.din("ngrow", [2, D])
        self.pZ4 = [T(z.t.reshape([128, 4, 128]), "pZ4", z.b) for z in self.pZ]
        self.xn = [b.sbt("xn%d" % i, [128, D], BF16) for i in range(3)]
        self.ss = [b.sbt("ss%d" % i, [128, 4]) for i in range(4)]


    def make_scdup(self, ar, scdup=None):
        b = self
        if scdup is None:
            scdup = ar.alloc("scdup", [128, 8, 128])
        b.cp("dve", scdup[:, :, 0:64], self.sc.ap(0, 128, 0, [[1, 8], [0, 64]]), [self.sc.b], [scdup.b])
        b.cp("dve", scdup[:, :, 64:128], self.sc.ap(0, 128, 8, [[1, 8], [0, 64]]), [self.sc.b], [scdup.b])
        return scdup

    def adaln(self, l, mod, scdup, awst, adab, pAda, pTm):
        b = self
        NB = 256
        for cb in range(3 * D // NB):
            st = awst[cb % len(awst)]
            ab = adab[cb % len(adab)]
            src = bass.AP(self.d_ada_w, l * D * 3 * D + cb * NB, [[3 * D, 128], [128 * 3 * D, 8], [1, NB]])
            b.loadw(st[:], src, [st.b])
            b.load(ab[:], bass.AP(self.d_ada_b, l * 3 * D + cb * NB, [[0, 128], [1, NB]]), [ab.b])
            pa = pAda[cb % len(pAda)]
            for k in range(8):
                b.mm(pa[:, 0:NB], scdup[:, k, :], st[:, k, :], k == 0, k == 7, [scdup.b, st.b], [pa.b])
            b.tt("dve", mod[:, cb * NB:(cb + 1) * NB], pa[:, 0:NB], ab[:], ALU.add, [pa.b, ab.b], [mod.b])
        modT = b.sbt("modT%d" % l, [128, 2, 16])
        for q in range(4):
            pt = pTm[q % len(pTm)]
            for i in range(4):
                ch = q * 4 + i
                b.tr(pt[:, i, :], mod[:, ch * 128:(ch + 1) * 128], self.identF[:], [mod.b, self.identF.b], [pt.b])
            b.cp("dve", modT.ap(0, 128, q * 4, [[16, 2], [1, 4]]), pt.ap(0, 128, 0, [[64, 2], [128, 4]]), [pt.b], [modT.b])
        AB = b.sbt("AB%d" % l, [128, 2, 16])
        for xc in range(2):
            self.p.add("dve", lambda e, xc=xc: e.scalar_tensor_tensor(
                out=AB[:, xc, 0:8], in0=modT[:, xc, 8:16], scalar=1.0, in1=self.ng[:, l * 8:(l + 1) * 8],
                op0=ALU.add, op1=ALU.mult), [modT.b, self.ng.b], [AB.b])
            b.cp("dve", AB[:, xc, 8:16], modT[:, xc, 0:8], [modT.b], [AB.b])
        return AB

    def gate_bc(self, g, mod, xc, pAda):
        b = self
        p0 = 64 * xc
        for cb in range(2):
            pa = pAda[cb % len(pAda)]
            b.mm(pa[:], self.onesF[p0:p0 + 1, :], mod[p0:p0 + 1, 2 * D + cb * 512:2 * D + (cb + 1) * 512], True, True,
                 [self.onesF.b, mod.b], [pa.b])
            b.cp("act", g[:, cb * 512:(cb + 1) * 512], pa[:], [pa.b], [g.b])

    def rstd_from_ss(self, ss):
        b = self
        b.ts("pool", ss[:, 1:2], ss[:, 0:1], 1.0 / D, EPS, ALU.mult, ALU.add, [ss.b], [ss.b])
        b.tt("pool", ss[:, 3:4], ss[:, 1:2], self.mhalf[:, 0:1], ALU.pow, [ss.b, self.mhalf.b], [ss.b])

    def next(self, key, lst):
        i = self.rr.get(key, 0)
        self.rr[key] = i + 1
        return lst[i % len(lst)]

    def load_x(self, dsrc, r0):
        x_ = self.next("xt", self.xt)
        self.load(x_[:], dsrc.ap()[r0:r0 + 128, :], [x_.b])
        return x_

    def sumsq(self, x_, s_, eng, dummy):
        if eng == "act":
            self.act(dummy[:], x_[:], AF.Square, [x_.b], [dummy.b, s_.b], accum=s_[:, 0:1])
            return
        self.p.add("dve", lambda e: e.scalar_tensor_tensor(out=dummy[:], in0=x_[:], scalar=1.0, in1=x_[:], op0=ALU.mult,
                                                          op1=ALU.mult, accum_out=s_[:, 0:1]), [x_.b], [dummy.b, s_.b])

    def hT_prep(self, x_, Arow=None, rstd=None, save_rstd=None, sq_eng="dve"):
        b = self
        n_ = self.next("xn", self.xn)
        if rstd is None:
            s_ = self.next("ss", self.ss)
            b.sumsq(x_, s_, sq_eng, n_)
            b.rstd_from_ss(s_)
            rs, rsb = s_[:, 3:4], s_.b
            if save_rstd is not None:
                sap, sbuf = save_rstd
                b.cp("pool", sap, s_[:, 3:4], [s_.b], [sbuf])
        else:
            rs, rsb = rstd
        if Arow is not None:
            self.p.add("dve", lambda e: e.scalar_tensor_tensor(out=n_[:], in0=x_[:], scalar=rs, in1=Arow[:],
                                                              op0=ALU.mult, op1=ALU.mult), [x_.b, rsb, Arow.b], [n_.b])
        else:
            b.ts("dve", n_[:], x_[:], rs, None, ALU.mult, None, [x_.b, rsb], [n_.b])
        return n_

    def hT_fin(self, n_, AB, xc, dst, fused_A):
        b = self
        dap, dbuf = dst
        pX = self.next("pX", self.pXs)
        for c in range(8):
            b.tr(pX[:, c * 128:(c + 1) * 128], n_[:, c * 128:(c + 1) * 128], self.identB[:], [n_.b, self.identB.b], [pX.b])
        pv = pX.ap(0, 128, 0, [[128, 8], [1, 128]])
        if not fused_A:
            b.tt("dve", dap, pv, AB.ap(0, 128, xc * 16, [[1, 8], [0, 128]]), ALU.mult, [pX.b, AB.b], [dbuf])
            b.tt("pool", dap, dap, AB.ap(0, 128, xc * 16 + 8, [[1, 8], [0, 128]]), ALU.add, [dbuf, AB.b], [dbuf])
        else:
            b.tt("dve", dap, pv, AB.ap(0, 128, xc * 16 + 8, [[1, 8], [0, 128]]), ALU.add, [pX.b, AB.b], [dbuf])

    def hT_tile(self, x_, AB, xc, dst, Arow=None, rstd=None, save_rstd=None, sq_eng="dve"):
        n_ = self.hT_prep(x_, Arow, rstd, save_rstd, sq_eng)
        self.hT_fin(n_, AB, xc, dst, Arow is not None)

    def arow_bc(self, Arow, mod, l, pAda, gbc):
        b = self
        for cb in range(2):
            pa = pAda[cb % len(pAda)]
            b.mm(pa[:], self.onesF[0:1, :], mod[0:1, D + cb * 512:D + (cb + 1) * 512], True, True,
                 [self.onesF.b, mod.b], [pa.b])
            self.p.add("dve", lambda e, pa=pa, cb=cb: e.scalar_tensor_tensor(
                out=Arow[:, cb * 512:(cb + 1) * 512], in0=pa[:], scalar=1.0, in1=gbc[:, cb * 512:(cb + 1) * 512],
                op0=ALU.add, op1=ALU.mult), [pa.b, gbc.b], [Arow.b])

    def layer0(self, d_x, d_ctx, d_x1, d_ctx1):
        b = self
        p = self.p
        pZ, pS, pX, pZ4 = self.pZ, self.pS, self.pX, self.pZ4
        self.xt = [b.sbt("xt%d" % i, [128, D]) for i in range(4)]
        self.hT = [b.sbt("hT%d" % i, [128, 8, 512], BF16) for i in range(2)]
        hT = self.hT
        d_win = self.din("w_in_ab", [D, 2560])
        d_wout = self.din("w_out_ab", [D, D])
        d_vg = self.din("v_norm_g", [1, 512])
        d_sw = self.din("spatial_w", [4, 128, 128])
        d_sb = self.din("spatial_b", [1, 512])
        d_tab1 = self.din("tab1", [128, 32 * 128])
        d_tab2 = self.din("tab2", [128, 128])
        d_fc = self.din("fc", [128, 512])
        d_tabc = self.din("tabc", [128, 1024])

        arX = Arena(p, "arX", 40, parent=self.cur_arena)
        arY = Arena(p, "arY", 32, parent=self.cur_arena)
        self.arX, self.arY = arX, arY
        win = b.sbt("win", [128, 8, 2560], BF16)
        gx0 = b.sbt("gx0", [128, D])
        Arow0 = b.sbt("Arow0", [128, D])
        rstd_all = b.sbt("rstd_all", [128, NT])
        swT = b.sbt("swT", [128, 4, 128], BF16)
        self.pXs = [self.pX, self.pA]
        sbk = b.sbt("sbk", [33, 512], BF16)
        onesk = b.sbt("onesk", [33, 128], BF16)
        vg = b.sbt("vg", [128, 512])
        arZ = Arena(p, "arZ", 8, parent=self.cur_arena)
        tab1 = arZ.alloc("tab1", [128, 32, 128], BF16)
        tab2 = b.sbt("tab2", [128, 128], BF16)
        fc = b.sbt("fc", [128, 512], BF16)
        u_t = [b.sbt("u_t%d" % i, [128, 512]) for i in range(2)]
        sg_t = [b.sbt("sg_t%d" % i, [128, 512]) for i in range(2)]
        vh_t = [b.sbt("vh_t%d" % i, [128, 512]) for i in range(2)]
        vb_t = [b.sbt("vb_t%d" % i, [128, 512], BF16) for i in range(2)]
        ya_t = [b.sbt("ya_t%d" % i, [128, 512], BF16) for i in range(2)]
        st6 = [b.sbt("st6_%d" % i, [128, 8]) for i in range(2)]
        yT_t = [b.sbt("yT_t%d" % i, [128, 8, 128], BF16) for i in range(2)]
        sgm_t = [b.sbt("sgm_t%d" % i, [128, 512], BF16) for i in range(2)]
        ybg_t = [b.sbt("ybg_t%d" % i, [128, 4, 512], BF16) for i in range(2)]

        woutc = arX.alloc("woutc", [128, 8, D], BF16)
        mod0 = arX.alloc("mod0", [128, 3 * D])
        gc0 = arX.alloc("gc0", [128, D])
        sbf = arX.alloc("sbf", [33, 512])
        tmpb = arX.alloc("tmpb", [33, 512], BF16)
        awst = [arY.alloc("awst%d" % i, [128, 8, 256]) for i in range(2)]
        scdup = b.make_scdup(arY)
        adab = [arY.alloc("adab%d" % i, [128, 256]) for i in range(2)]
        xbc = [arY.alloc("xbc%d" % j, [128, 512], BF16) for j in range(2)]
        sgbc = [arY.alloc("sgbc%d" % j, [128, 512], BF16) for j in range(2)]
        ZT = [arY.alloc("ZT%d" % g, [128, 512], BF16) for g in range(4)]
        tabc = arY.alloc("tabc", [128, 2, 512], BF16)

        AB0 = b.adaln(0, mod0, scdup, awst, adab, pZ[0:2], pZ4[2:4])
        b.gate_bc(gx0, mod0, 0, pZ[0:2])
        b.gate_bc(gc0, mod0, 1, pZ[0:2])
        gbc = T(awst[0].t, "gbc", awst[0].b, awst[0].base, [128, D])
        b.load(gbc[:], bass.AP(self.d_ngrow, 0, [[0, 128], [1, D]]), [gbc.b])
        b.arow_bc(Arow0, mod0, 0, pZ[0:2], gbc)

        engs3 = ["dve", "act"]
        stv = [T(a.t, "stv", a.b, a.base, [128, 2048]) for a in awst]
        for k in range(8):
            for hh in range(2):
                st = stv[(k * 2 + hh) % 2]
                b.loadw(st[:, 0:1280], d_win.ap()[k * 128:(k + 1) * 128, hh * 1280:(hh + 1) * 1280], [st.b])
                b.cp(b.pick("wcast", engs3), win[:, k, hh * 1280:(hh + 1) * 1280], st[:, 0:1280], [st.b], [win.b])
        for k in range(8):
            st = stv[k % 2]
            b.loadw(st[:, 0:D], d_wout.ap()[k * 128:(k + 1) * 128, :], [st.b])
            b.tt("pool", woutc[:, k, :], st[:, 0:D], gc0[:], ALU.mult, [st.b, gc0.b], [woutc.b])
        st = stv[0]
        b.load(st[:, 0:512].rearrange("p (h q) -> p h q", h=4), d_sw.ap().rearrange("h q p -> q h p"), [st.b])
        for h in range(4):
            b.tr(pZ4[2][:, h, :], st[:, h * 128:(h + 1) * 128], self.identF[:], [st.b, self.identF.b], [pZ4[2].b])
        b.cp("dve", swT[:], pZ4[2][:], [pZ4[2].b], [swT.b])
        b.load(sbf[0:1, :], d_sb.ap(), [sbf.b])
        b.load(sbf[32:33, :], d_sb.ap(), [sbf.b])
        p.add("pool", lambda e: e.memset(sbk[:], 0.0), writes=[sbk.b])
        p.add("pool", lambda e: e.memset(onesk[:], 1.0), writes=[onesk.b])
        b.cp("dve", sbk[0:1, :], sbf[0:1, :], [sbf.b], [sbk.b])
        b.cp("dve", tmpb[32:33, :], sbf[32:33, :], [sbf.b], [tmpb.b])
        b.tt("dve", sbk[32:33, :], sbf[32:33, :], tmpb[32:33, :], ALU.subtract, [sbf.b, tmpb.b], [sbk.b])
        b.load(vg[:], bass.AP(d_vg, 0, [[0, 128], [1, 512]]), [vg.b])
        for q in range(2):
            st = stv[q % 2]
            b.load(st[:, 0:2048], d_tab1.ap()[:, q * 2048:(q + 1) * 2048], [st.b])
            b.cp(b.pick("wcast", engs3), tab1.ap(0, 128, q * 2048, [[1, 2048]]), st[:, 0:2048], [st.b], [tab1.b])
        st = stv[0]
        b.load(st[:, 0:128], d_tab2.ap(), [st.b])
        b.cp("dve", tab2[:], st[:, 0:128], [st.b], [tab2.b])
        st = stv[1]
        b.load(st[:, 0:512], d_fc.ap(), [st.b])
        b.cp("dve", fc[:], st[:, 0:512], [st.b], [fc.b])
        st = stv[0]
        b.load(st[:, 0:1024], d_tabc.ap(), [st.b])
        b.cp("dve", tabc.ap(0, 128, 0, [[1, 1024]]), st[:, 0:1024], [st.b], [tabc.b])

        if self.stop == "pro":
            b.dump("AB0", AB0)
            b.dump("mod0", mod0)
            b.dump("gx0", gx0)
            b.dump("gc0", gc0)
            b.dump("win", win, dt=BF16)
            b.dump("woutc", woutc, dt=BF16)
            b.dump("swT", swT, dt=BF16)
            b.dump("sbk", sbk, dt=BF16)
            b.dump("tab1", tab1, dt=BF16)
            b.dump("tabc", tabc, dt=BF16)
            return
        def a_branch(hTm, j, slot):
            pu, pv, pg = pZ[0], pZ[1], pZ[2]
            for (pz, c0) in ((pv, 512), (pg, 1024), (pu, 0)):
                for k in range(8):
                    b.mm(pz[:], hTm[:, k, j * 128:(j + 1) * 128], win[:, k, c0:c0 + 512], k == 0, k == 7,
                         [hTm.b, win.b], [pz.b])
            s6 = st6[slot]
            vh, vb, sg, u, ya = vh_t[slot], vb_t[slot], sg_t[slot], u_t[slot], ya_t[slot]
            p.add("dve", lambda e: e.bn_stats(out=s6[:, 0:6], in_=pv[:]), [pv.b], [s6.b])
            p.add("dve", lambda e: e.bn_aggr(out=s6[:, 6:8], in_=s6[:, 0:6]), [s6.b], [s6.b])
            b.ts("pool", s6[:, 0:1], s6[:, 7:8], EPS, None, ALU.add, None, [s6.b], [s6.b])
            b.tt("pool", s6[:, 2:3], s6[:, 0:1], self.mhalf[:, 0:1], ALU.pow, [s6.b, self.mhalf.b], [s6.b])
            b.ts("dve", vh[:], pv[:], s6[:, 6:7], s6[:, 2:3], ALU.subtract, ALU.mult, [pv.b, s6.b], [vh.b])
            b.tt("pool", vb[:], vh[:], vg[:], ALU.mult, [vh.b, vg.b], [vb.b])
            b.act(sg[:], pg[:], AF.Silu, [pg.b], [sg.b])
            b.tt("dve", u[:], pu[:], sg[:], ALU.mult, [pu.b, sg.b], [u.b])
            for h in range(4):
                b.mm(pS[:, h * 128:(h + 1) * 128], swT[:, h, :], vb[:, h * 128:(h + 1) * 128], True, False,
                     [swT.b, vb.b], [pS.b])
                b.mm(pS[:, h * 128:(h + 1) * 128], sbk[0:33, h * 128:(h + 1) * 128], onesk[0:33, :], False, True,
                     [sbk.b, onesk.b], [pS.b])
            b.tt("dve", ya[:], pS[:], u[:], ALU.mult, [pS.b, u.b], [ya.b])
            return ya

        def out_proj(lhs_list, wmat, x_):
            for cb in range(2):
                pz = pZ[cb]
                for k in range(8):
                    ap_, bf_ = lhs_list[k]
                    b.mm(pz[:], ap_, wmat[:, k, cb * 512:(cb + 1) * 512], k == 0, k == 7, [bf_, wmat.b], [pz.b])
                b.tt("dve", x_[:, cb * 512:(cb + 1) * 512], pz[:], x_[:, cb * 512:(cb + 1) * 512], ALU.add,
                     [pz.b, x_.b], [x_.b])

        hTc = hT[1]
        xcs = []
        for j in range(2):
            x_ = b.load_x(d_ctx, j * 128)
            xcs.append(x_)
            b.hT_tile(x_, AB0, 1, (hTc.ap(0, 128, j * 128, [[512, 8], [1, 128]]), hTc.b))
        yac = []
        for j in range(2):
            yac.append(a_branch(hTc, j, j))
            for (pz, c0) in ((pZ[3], 1536), (pZ[2], 2048)):
                for k in range(8):
                    b.mm(pz[:], hTc[:, k, j * 128:(j + 1) * 128], win[:, k, c0:c0 + 512], k == 0, k == 7,
                         [hTc.b, win.b], [pz.b])
            b.cp("dve", xbc[j][:], pZ[3][:], [pZ[3].b], [xbc[j].b])
            b.act(sgbc[j][:], pZ[2][:], AF.Silu, [pZ[2].b], [sgbc[j].b])
        for g in range(4):
            pz = pZ[g % 2]
            for j in range(2):
                b.mm(pz[:], xbc[j][:, g * 128:(g + 1) * 128], tabc[:, j, :], j == 0, j == 1, [xbc[j].b, tabc.b], [pz.b])
            b.cp(b.pick("zt", ["act", "dve"]), ZT[g][:], pz[:], [pz.b], [ZT[g].b])
        for kt in range(2):
            pz = pZ[2 + kt]
            for g in range(4):
                b.mm(pz[:, g * 128:(g + 1) * 128], ZT[g][:, kt * 128:(kt + 1) * 128], fc[:, 0:128], True, False,
                     [ZT[g].b, fc.b], [pz.b])
                b.mm(pz[:, g * 128:(g + 1) * 128], ZT[g][:, 256 + kt * 128:256 + (kt + 1) * 128], fc[:, 256:384], False, True,
                     [ZT[g].b, fc.b], [pz.b])
            ybc = vb_t[kt]
            b.tt("dve", ybc[:], pz[:], sgbc[kt][:], ALU.mult, [pz.b, sgbc[kt].b], [ybc.b])
            yT = yT_t[kt]
            for c in range(4):
                b.tr(pX[:, c * 128:(c + 1) * 128], yac[kt][:, c * 128:(c + 1) * 128], self.identB[:], [yac[kt].b, self.identB.b], [pX.b])
            for c in range(4):
                b.tr(pX[:, (4 + c) * 128:(5 + c) * 128], ybc[:, c * 128:(c + 1) * 128], self.identB[:], [ybc.b, self.identB.b], [pX.b])
            b.cp("act", yT.ap(0, 128, 0, [[1, 1024]]), pX[:], [pX.b], [yT.b])
            out_proj([(yT[:, k, :], yT.b) for k in range(8)], woutc, xcs[kt])
            p.dma("sp", d_ctx1.ap()[kt * 128:(kt + 1) * 128, :], xcs[kt][:], reads=[xcs[kt].b])

        if self.stop == "ctx":
            b.dump("hTc", hTc, dt=BF16)
            b.dump("ya0", ya_t[0], dt=BF16)
            b.dump("yb0", vb_t[0], dt=BF16)
            return
        oldY = arY.reset()
        xbT = [arY.alloc("xbT%d" % g, [128, S], BF16) for g in range(4)]
        b.retarget(oldY, xbT)
        prepB = {}

        def HB1(t):
            if t >= NT:
                return
            x_ = b.load_x(d_x, t * 128)
            prepB[t] = b.hT_prep(x_, Arow=Arow0, save_rstd=(rstd_all[:, t:t + 1], rstd_all.b), sq_eng="act")

        def HB2(t):
            if t >= NT:
                return
            m, j = t // 4, t % 4
            hTm = hT[m % 2]
            b.hT_fin(prepB.pop(t), AB0, 0, (hTm.ap(0, 128, j * 128, [[512, 8], [1, 128]]), hTm.b), True)

        HB1(0)
        HB1(1)
        for j in range(4):
            HB1(j + 2)
            HB2(j)
        HB2(4)
        for m in range(8):
            hTm = hT[m % 2]
            for g in range(4):
                c0 = 1536 + g * 128
                pz = self.next("pzB", pZ)
                for k in range(8):
                    b.mm(pz[:], win[:, k, c0:c0 + 128], hTm[:, k, :], k == 0, k == 7, [win.b, hTm.b], [pz.b])
                b.cp("act", xbT[g][:, m * 512:(m + 1) * 512], pz[:], [pz.b], [xbT[g].b])
                t2 = 4 * (m + 1) + g + 1
                HB1(t2 + 1)
                if g < 3:
                    HB2(t2)
            HB2(4 * (m + 2))

        if self.stop == "passB":
            for g in range(4):
                b.dump("xbT%d" % g, xbT[g], dt=BF16)
            return
        oldX = arX.reset()
        GT = arX.alloc("GT", [128, 8192], BF16)
        XR = arX.alloc("XR", [128, 32, 128], BF16)
        TT = arX.alloc("TT", [128, 32, 256], BF16)
        b.retarget(oldX, [GT, XR, TT])
        if self.stop == "m0":
            b.dump("xbT0", xbT[0], dt=BF16)
            return
        XRb = [Buf("XR%d" % i) for i in range(4)]
        TTb = [Buf("TT%d" % i) for i in range(4)]
        for bb in XRb:
            bb.last_w = XR.b.last_w
        for bb in TTb:
            bb.last_w = TT.b.last_w
        pB32 = T(self.pB.t.bitcast(F32), "pB32", self.pB.b)
        bpool = [pZ[0], pZ[1], pZ[2], pZ[3], pS, pB32]
        self.pXs = [self.pX, self.pA]

        def M1(g, q):
            pX_ = self.next("pX", self.pXs)
            for jj in range(8):
                j = q * 8 + jj
                for r2 in range(2):
                    src = xbT[g].ap(0, 128, 2 * j + r2, [[64, 64]])
                    b.tr(pX_[64 * r2:64 * r2 + 64, jj * 128:(jj + 1) * 128], src, self.identB[:],
                         [xbT[g].b, self.identB.b], [pX_.b])
            b.cp(b.pick("m1ev", ["act", "dve"]), XR.ap(0, 128, q * 1024, [[1, 1024]]), pX_[:], [pX_.b], [XRb[q]])

        def S1(g, q):
            pzs = (self.next("bp", bpool), self.next("bp", bpool))
            for jj in range(4):
                j = q * 4 + jj
                for r2 in range(2):
                    b.mm(pzs[r2][:, jj * 128:(jj + 1) * 128], XR[64 * r2:64 * r2 + 64, j, :],
                         tab1[64 * r2:64 * r2 + 64, j, :], True, True, [XRb[q // 2], tab1.b], [pzs[r2].b])
            for r2 in range(2):
                dst = GT.ap(0, 128, (q * 8 + r2) * 64, [[4096, 2], [128, 4], [1, 64]])
                srcp = pzs[r2].ap(0, 128, 0, [[64, 2], [128, 4], [1, 64]])
                b.cp(b.pick("s1ev", ["act", "dve"]), dst, srcp, [pzs[r2].b], [GT.b])

        def M2(g, qq):
            pz = self.next("bp", bpool)
            for h2 in range(2):
                q = qq * 2 + h2
                for k1p in range(2):
                    k1 = 2 * q + k1p
                    l0 = GT.ap(0, 128, k1, [[64, 64]])
                    l1 = GT.ap(0, 128, 4096 + k1, [[64, 64]])
                    o_ = pz[64 * k1p:64 * k1p + 64, h2 * 256:(h2 + 1) * 256]
                    b.mm(o_, l0, fc[:, 0:256], True, False, [GT.b, fc.b], [pz.b])
                    b.mm(o_, l1, fc[:, 256:512], False, True, [GT.b, fc.b], [pz.b])
            b.cp(b.pick("m2ev", ["act", "dve"]), TT.ap(0, 128, qq * 512, [[1, 512]]), pz[:], [pz.b], [TTb[qq // 4]])

        def S2(g, i4):
            pzs = (self.next("bp", bpool), self.next("bp", bpool))
            for ql in range(8):
                q = i4 * 8 + ql
                for k1p in range(2):
                    pz = pzs[k1p]
                    b.mm(pz[:, ql * 64:(ql + 1) * 64], TT[64 * k1p:64 * k1p + 64, q, 0:128],
                         tab2[64 * k1p:64 * k1p + 64, 0:64], True, False, [TTb[i4], tab2.b], [pz.b])
                    b.mm(pz[:, ql * 64:(ql + 1) * 64], TT[64 * k1p:64 * k1p + 64, q, 128:256],
                         tab2[64 * k1p:64 * k1p + 64, 64:128], False, True, [TTb[i4], tab2.b], [pz.b])
            for k1p in range(2):
                srcp = pzs[k1p].ap(0, 128, 0, [[64, 8], [1, 64]])
                dst = xbT[g].ap(0, 128, 16 * i4 + k1p, [[2, 8], [64, 64]])
                b.cp(b.pick("s2ev", ["act", "dve"]), dst, srcp, [pzs[k1p].b], [xbT[g].b])

        for q in range(4):
            M1(0, q)
        for q in range(8):
            S1(0, q)
        for g in range(4):
            for q in range(4):
                for i in range(4):
                    M2(g, 4 * q + i)
                if g + 1 < 4:
                    M1(g + 1, q)
            for i4 in range(4):
                S2(g, i4)
                if g + 1 < 4:
                    S1(g + 1, 2 * i4)
                    S1(g + 1, 2 * i4 + 1)
        XR.b.last_w = XRb[3].last_w
        XR.b.readers = [r for bb in XRb for r in bb.readers]
        TT.b.last_w = TTb[3].last_w
        TT.b.readers = [r for bb in TTb for r in bb.readers]

        if self.stop == "Bpipe":
            for g in range(4):
                b.dump("fT%d" % g, xbT[g], dt=BF16)
            return
        oldX = arX.reset()
        wout = arX.alloc("wout", [128, 8, D], BF16)
        wst = [arX.alloc("wst%d" % i, [128, D]) for i in range(2)]
        b.retarget(oldX, [wout] + wst)
        for k in range(8):
            st = wst[k % 2]
            b.load(st[:], d_wout.ap()[k * 128:(k + 1) * 128, :], [st.b])
            b.tt(b.pick("wos", ["dve", "pool"]), wout[:, k, :], st[:], gx0[:], ALU.mult, [st.b, gx0.b], [wout.b])
        oldZ = arZ.reset()
        xr = [self.xt[3]] + [arZ.alloc("xr%d" % i, [128, D]) for i in range(2)]
        b.retarget(oldZ, xr[1:])
        xnorm = self.xt[0:3]
        pA32 = T(self.pA.t.bitcast(F32), "pA32", self.pA.b)
        sets = [(pZ[0], pZ[1], pZ[2]), (pZ[3], pS, pA32)]
        pG = T(self.pB.t.bitcast(F32), "pG", self.pB.b)
        self.pXs = [self.pX]

        prepA = {}

        def HA1(t):
            if t >= NT:
                return
            x_ = self.next("xnorm", xnorm)
            b.load(x_[:], d_x.ap()[t * 128:(t + 1) * 128, :], [x_.b])
            prepA[t] = b.hT_prep(x_, Arow=Arow0, rstd=(rstd_all[:, t:t + 1], rstd_all.b))

        def HA2(t):
            if t >= NT:
                return
            m, j = t // 4, t % 4
            hTm = hT[m % 2]
            b.hT_fin(prepA.pop(t), AB0, 0, (hTm.ap(0, 128, j * 128, [[512, 8], [1, 128]]), hTm.b), True)

        def GA(m):
            if m >= 8:
                return
            hTm = hT[m % 2]
            ybg = ybg_t[m % 2]
            for g in range(4):
                c0 = 2048 + g * 128
                for k in range(8):
                    b.mm(pG[:], win[:, k, c0:c0 + 128], hTm[:, k, :], k == 0, k == 7, [win.b, hTm.b], [pG.b])
                sgm = sgm_t[g % 2]
                b.act(sgm[:], pG[:], AF.Silu, [pG.b], [sgm.b])
                b.tt("pool", ybg[:, g, :], xbT[g][:, m * 512:(m + 1) * 512], sgm[:], ALU.mult, [xbT[g].b, sgm.b], [ybg.b])

        def inpA(t, c0, pz):
            hTm = hT[(t // 4) % 2]
            j = t % 4
            for k in range(8):
                b.mm(pz[:], hTm[:, k, j * 128:(j + 1) * 128], win[:, k, c0:c0 + 512], k == 0, k == 7, [hTm.b, win.b], [pz.b])

        def VA(t):
            if t >= NT:
                return
            pv = sets[t % 2][1]
            inpA(t, 512, pv)
            s6, vh, vb = st6[t % 2], vh_t[t % 2], vb_t[t % 2]
            p.add("dve", lambda e: e.bn_stats(out=s6[:, 0:6], in_=pv[:]), [pv.b], [s6.b])
            p.add("dve", lambda e: e.bn_aggr(out=s6[:, 6:8], in_=s6[:, 0:6]), [s6.b], [s6.b])
            b.ts("pool", s6[:, 0:1], s6[:, 7:8], EPS, None, ALU.add, None, [s6.b], [s6.b])
            b.tt("pool", s6[:, 2:3], s6[:, 0:1], self.mhalf[:, 0:1], ALU.pow, [s6.b, self.mhalf.b], [s6.b])
            b.ts("dve", vh[:], pv[:], s6[:, 6:7], s6[:, 2:3], ALU.subtract, ALU.mult, [pv.b, s6.b], [vh.b])
            b.tt("pool", vb[:], vh[:], vg[:], ALU.mult, [vh.b, vg.b], [vb.b])

        def GaA(t):
            if t >= NT:
                return
            pg = sets[t % 2][2]
            inpA(t, 1024, pg)
            b.act(sg_t[t % 2][:], pg[:], AF.Silu, [pg.b], [sg_t[t % 2].b])

        def UA(t):
            if t >= NT:
                return
            pu = sets[t % 2][0]
            inpA(t, 0, pu)
            b.tt("dve", u_t[t % 2][:], pu[:], sg_t[t % 2][:], ALU.mult, [pu.b, sg_t[t % 2].b], [u_t[t % 2].b])

        def SA(t):
            pv = sets[t % 2][1]
            vb, u, ya = vb_t[t % 2], u_t[t % 2], ya_t[t % 2]
            for h in range(4):
                b.mm(pv[:, h * 128:(h + 1) * 128], swT[:, h, :], vb[:, h * 128:(h + 1) * 128], True, False,
                     [swT.b, vb.b], [pv.b])
                b.mm(pv[:, h * 128:(h + 1) * 128], sbk[0:33, h * 128:(h + 1) * 128], onesk[0:33, :], False, True,
                     [sbk.b, onesk.b], [pv.b])
            b.tt("dve", ya[:], pv[:], u[:], ALU.mult, [pv.b, u.b], [ya.b])

        def TyA(t):
            ya, yT = ya_t[t % 2], yT_t[t % 2]
            pX = self.pX
            for c in range(4):
                b.tr(pX[:, c * 128:(c + 1) * 128], ya[:, c * 128:(c + 1) * 128], self.identB[:], [ya.b, self.identB.b], [pX.b])
            b.cp("act", yT.ap(0, 128, 0, [[1, 512]]), pX[:, 0:512], [pX.b], [yT.b])

        xres = {}

        def LX(t):
            if t >= NT:
                return
            x_ = self.next("xres", xr)
            b.load(x_[:], d_x.ap()[t * 128:(t + 1) * 128, :], [x_.b])
            xres[t] = x_

        def OA(t):
            yT, ybg = yT_t[t % 2], ybg_t[(t // 4) % 2]
            j = t % 4
            x_ = xres.pop(t)
            banks = (sets[t % 2][0], sets[t % 2][2])
            for cb in range(2):
                pz = banks[cb]
                for k in range(8):
                    if k < 4:
                        ap_, bf_ = yT[:, k, :], yT.b
                    else:
                        ap_, bf_ = ybg[:, k - 4, j * 128:(j + 1) * 128], ybg.b
                    b.mm(pz[:], ap_, wout[:, k, cb * 512:(cb + 1) * 512], k == 0, k == 7, [bf_, wout.b], [pz.b])
                b.tt("dve", x_[:, cb * 512:(cb + 1) * 512], pz[:], x_[:, cb * 512:(cb + 1) * 512], ALU.add,
                     [pz.b, x_.b], [x_.b])
            p.dma("sp", d_x1.ap()[t * 128:(t + 1) * 128, :], x_[:], reads=[x_.b])

        HA1(0)
        for j in range(4):
            HA1(j + 1)
            HA2(j)
        GA(0)
        LX(0)
        LX(1)
        VA(0)
        GaA(0)
        UA(0)
        for t in range(NT):
            m, j = t // 4, t % 4
            LX(t + 2)
            VA(t + 1)
            SA(t)
            GaA(t + 1)
            TyA(t)
            UA(t + 1)
            HA1(t + 5)
            HA2(t + 4)
            OA(t)
            if j == 3:
                GA(m + 1)

    def layer1(self, d_x1, d_ctx1, d_out, ar):
        b = self
        p = self.p
        pZ, pS, pX, pZ4 = self.pZ, self.pS, self.pX, self.pZ4
        d_win = self.din("w_in_c", [D, 2560])
        d_wout = self.din("w_out_c", [D, D])
        d_sink = self.din("sink_logit", [1, 16])
        d_fg = self.din("final_g", [1, D])
        d_cos = self.din("rope_cos", [128, NT * 64])
        d_sin = self.din("rope_sin", [128, NT * 64])
        d_mask = self.din("wmask", [128, 256])

        winc = b.sbt("winc", [128, 8, 2560], BF16)
        wo1 = b.sbt("wo1", [128, 8, D], BF16)
        cosT = b.sbt("cosT", [128, NT * 64])
        sinT = b.sbt("sinT", [128, NT * 64])
        maskb = b.sbt("maskb", [128, 256], BF16)
        esink = b.sbt("esink", [128, 16])
        fg = b.sbt("fg", [128, D])
        gx1 = b.sbt("gx1", [128, D])
        NR = 4
        kTd = b.sbt("kTd", [128, 4, NR, 128], BF16)
        Vp = b.sbt("Vp", [128, NR, 4, 65], BF16)
        kcT = b.sbt("kcT", [128, 4, 2, 128], BF16)
        Vc = b.sbt("Vc", [128, 2, 4, 65], BF16)
        qT = [b.sbt("qT%d" % i, [128, 8, 128], BF16) for i in range(3)]
        sg = [b.sbt("sg1_%d" % i, [128, D]) for i in range(3)]
        hT1 = [b.sbt("hT1_%d" % i, [128, 8, 128], BF16) for i in range(2)]
        t1 = [b.sbt("rt1_%d" % i, [128, 512]) for i in range(2)]
        t2 = [b.sbt("rt2_%d" % i, [128, 512]) for i in range(2)]
        qr = [b.sbt("qr%d" % i, [128, D], BF16) for i in range(2)]
        krd = [b.sbt("krd%d" % i, [128, 4, 2, 64], BF16) for i in range(2)]
        den = [b.sbt("den%d" % i, [128, 8]) for i in range(2)]
        on_t = [b.sbt("on%d" % i, [128, 256]) for i in range(2)]
        og = [b.sbt("og%d" % i, [128, D], BF16) for i in range(2)]
        ogT = [b.sbt("ogT%d" % i, [128, 8, 128], BF16) for i in range(2)]
        ss2 = [b.sbt("ss2_%d" % i, [128, 4]) for i in range(2)]
        junk1 = b.sbt("junk1", [128, D], BF16)
        pSt = [pZ[2], pZ[3], pS]
        pO = [T(self.pA.t.bitcast(F32), "pO0", self.pA.b)]
        self.pXs = [self.pX, self.pB]
        Arow1 = b.sbt("Arow1", [128, D])

        old = ar.reset() + list(getattr(self, "l1_old", []))
        l1_new = [winc, wo1, cosT, sinT, maskb, esink, fg, gx1, Arow1, kTd, Vp, kcT, Vc] + qT + sg + hT1 + t1 + t2 + qr + krd + den \
            + on_t + og + ogT + ss2 + [junk1]
        mod1 = ar.alloc("mod1", [128, 3 * D])
        nst = 4 if ar.cap >= 50 * 1024 else 2
        awst = [ar.alloc("awst1_%d" % i, [128, 8, 256]) for i in range(nst)]
        scd = ar.alloc("scdup1", [128, 8, 128])
        adab = [ar.alloc("adab1_%d" % i, [128, 256]) for i in range(2)]
        b.retarget(old, l1_new + [mod1, scd] + awst + adab)
        scdup = b.make_scdup(ar, scd)
        AB1 = b.adaln(1, mod1, scdup, awst, adab, pZ[0:2], pZ4[2:4])
        b.gate_bc(gx1, mod1, 0, pZ[0:2])
        gbc = T(awst[0].t, "gbc1", awst[0].b, awst[0].base, [128, D])
        b.load(gbc[:], bass.AP(self.d_ngrow, D, [[0, 128], [1, D]]), [gbc.b])
        b.arow_bc(Arow1, mod1, 1, pZ[0:2], gbc)
        engs3 = ["dve", "act"]
        stv = [T(a.t, "stv1", a.b, a.base, [128, 2048]) for a in awst]
        for k in range(8):
            for hh in range(2):
                st = stv[(k * 2 + hh) % len(stv)]
                b.loadw(st[:, 0:1280], d_win.ap()[k * 128:(k + 1) * 128, hh * 1280:(hh + 1) * 1280], [st.b])
                b.cp(b.pick("wcast", engs3), winc[:, k, hh * 1280:(hh + 1) * 1280], st[:, 0:1280], [st.b], [winc.b])
        for k in range(8):
            st = stv[k % len(stv)]
            b.loadw(st[:, 0:D], d_wout.ap()[k * 128:(k + 1) * 128, :], [st.b])
            b.tt(b.pick("wos", ["dve", "pool"]), wo1[:, k, :], st[:, 0:D], gx1[:], ALU.mult, [st.b, gx1.b], [wo1.b])
        b.load(cosT[:], d_cos.ap(), [cosT.b])
        b.load(sinT[:], d_sin.ap(), [sinT.b])
        st = stv[0]
        b.load(st[:, 0:256], d_mask.ap(), [st.b])
        b.cp("dve", maskb[:], st[:, 0:256], [st.b], [maskb.b])
        b.load(esink[:], bass.AP(d_sink, 0, [[0, 128], [1, 16]]), [esink.b])
        b.act(esink[:], esink[:], AF.Exp, [esink.b], [esink.b])
        b.ts("dve", esink[:], esink[:], 2.0, None, ALU.mult, None, [esink.b], [esink.b])
        b.load(fg[:], bass.AP(d_fg, 0, [[0, 128], [1, D]]), [fg.b])
        p.add("pool", lambda e: e.memset(Vp[:], 1.0), writes=[Vp.b])
        p.add("pool", lambda e: e.memset(Vc[:], 1.0), writes=[Vc.b])
        for kk in krd:
            p.add("pool", lambda e, kk=kk: e.memset(kk[:], 0.0), writes=[kk.b])
        oldp = ar.reset()
        x1t = [ar.alloc("x1t%d" % i, [128, D]) for i in range(5)]
        PT = [ar.alloc("PT%d" % i, [128, 512], BF16) for i in range(12)]
        b.retarget(oldp, x1t + PT)

        def hT_tile1(x_, xc, hTt):
            b.hT_tile(x_, AB1, xc, (hTt.ap(0, 128, 0, [[128, 8], [1, 128]]), hTt.b), Arow=(Arow1 if xc == 0 else None))

        def inproj(hTt, c0, pz, ncols=512):
            for k in range(8):
                b.mm(pz[:, 0:ncols], hTt[:, k, :], winc[:, k, c0:c0 + ncols], k == 0, k == 7, [hTt.b, winc.b], [pz.b])

        def rope(pz, col0, nh, t, outs):
            a1 = self.next("rt1", t1)
            a2 = self.next("rt2", t2)
            n = nh * 64
            cosb = cosT.ap(0, 128, t * 64, [[0, nh], [1, 64]])
            b.tt("dve", a1.ap(0, 128, 0, [[64, nh], [1, 64]]), pz.ap(0, 128, col0, [[64, nh], [1, 64]]), cosb, ALU.mult,
                 [pz.b, cosT.b], [a1.b])
            for hf in range(2):
                o_ = a2.ap(0, 128, hf * 16, [[64, nh], [32, 2], [1, 16]])
                i_ = pz.ap(0, 128, col0 + (1 - hf) * 16, [[64, nh], [32, 2], [1, 16]])
                s_ = sinT.ap(0, 128, t * 64 + hf * 16, [[0, nh], [32, 2], [1, 16]])
                b.tt("dve", o_, i_, s_, ALU.mult, [pz.b, sinT.b], [a2.b])
            for o_ in outs:
                oap, obuf = o_[0], o_[1]
                dims = o_[2] if len(o_) > 2 else [[64, nh], [1, 64]]
                b.tt("pool", oap, a1.ap(0, 128, 0, dims), a2.ap(0, 128, 0, dims), ALU.add, [a1.b, a2.b], [obuf])

        for j in range(2):
            x_ = self.next("x1t", x1t)
            b.load(x_[:], d_ctx1.ap()[j * 128:(j + 1) * 128, :], [x_.b])
            hTt = hT1[j % 2]
            hT_tile1(x_, 1, hTt)
            pz = pZ[j % 2]
            inproj(hTt, 1024, pz)
            kc = krd[j % 2]
            b.cp("dve", kc.ap(0, 128, 0, [[320, 2], [128, 2], [1, 64]]), pz.ap(0, 128, 0, [[128, 2], [64, 2], [1, 64]]),
                 [pz.b], [kc.b])
            b.cp("act", Vc.ap(0, 128, j * 260, [[65, 4], [1, 64]]), pz.ap(0, 128, 256, [[64, 4], [1, 64]]), [pz.b], [Vc.b])
            for kh in range(4):
                b.tr(pX[:, kh * 128:(kh + 1) * 128], kc.ap(0, 128, kh * 128, [[1, 128]]), self.identB[:], [kc.b, self.identB.b], [pX.b])
            b.cp("act", kcT.ap(0, 128, j * 128, [[256, 4], [1, 128]]), pX.ap(0, 128, 0, [[128, 4], [1, 128]]), [pX.b], [kcT.b])

        def stageA(t):
            x_ = self.next("x1t", x1t)
            xs[t] = x_
            b.load(x_[:], d_x1.ap()[t * 128:(t + 1) * 128, :], [x_.b])
            yield
            hTt = hT1[t % 2]
            n_ = self.next("xn", self.xn)
            s0 = self.next("ss", self.ss)
            b.sumsq(x_, s0, "act", n_)
            b.rstd_from_ss(s0)
            self.p.add("dve", lambda e: e.scalar_tensor_tensor(out=n_[:], in0=x_[:], scalar=s0[:, 3:4], in1=Arow1[:],
                                                              op0=ALU.mult, op1=ALU.mult), [x_.b, s0.b, Arow1.b], [n_.b])
            yield
            pX = self.next("pXr", self.pXs)
            for c in range(8):
                b.tr(pX[:, c * 128:(c + 1) * 128], n_[:, c * 128:(c + 1) * 128], self.identB[:], [n_.b, self.identB.b], [pX.b])
            b.tt("dve", hTt.ap(0, 128, 0, [[128, 8], [1, 128]]), pX.ap(0, 128, 0, [[128, 8], [1, 128]]),
                 AB1.ap(0, 128, 8, [[1, 8], [0, 128]]), ALU.add, [pX.b, AB1.b], [hTt.b])
            yield
            q_ = qr[t % 2]
            for qb in range(2):
                pz = self.next("pzin", pZ[0:2])
                inproj(hTt, qb * 512, pz)
                rope(pz, 0, 8, t, [(q_.ap(0, 128, qb * 512, [[64, 8], [1, 64]]), q_.b)])
                yield
            pz = self.next("pzin", pZ[0:2])
            inproj(hTt, 1024, pz)
            kc = krd[t % 2]
            slot = t % NR
            b.cp("act", Vp.ap(0, 128, slot * 260, [[65, 4], [1, 64]]), pz.ap(0, 128, 256, [[64, 4], [1, 64]]), [pz.b], [Vp.b])
            rope(pz, 0, 4, t, [(kc.ap(0, 128, 0, [[320, 2], [128, 2], [1, 64]]), kc.b, [[128, 2], [64, 2], [1, 64]])])
            yield
            s_ = sg[t % 3]
            for gb in range(2):
                pz = self.next("pzin", pZ[0:2])
                inproj(hTt, 1536 + gb * 512, pz)
                b.act(s_[:, gb * 512:(gb + 1) * 512], pz[:], AF.Tanh, [pz.b], [s_.b], scale=0.5)
                p.add("dve", lambda e, s_=s_, pz=pz, gb=gb: e.scalar_tensor_tensor(
                    out=s_[:, gb * 512:(gb + 1) * 512], in0=s_[:, gb * 512:(gb + 1) * 512], scalar=1.0, in1=pz[:],
                    op0=ALU.add, op1=ALU.mult), [s_.b, pz.b], [s_.b])
                yield
            pX = self.next("pXr", self.pXs)
            for h in range(16):
                r0 = 64 * (h // 8)
                b.tr(pX[r0:r0 + 64, (h % 8) * 128:(h % 8 + 1) * 128], q_[:, h * 64:(h + 1) * 64], self.identB[:],
                     [q_.b, self.identB.b], [pX.b])
            b.cp("act", qT[t % 3].ap(0, 128, 0, [[1, 1024]]), pX[:], [pX.b], [qT[t % 3].b])
            pX = self.next("pXr", self.pXs)
            for kh in range(4):
                b.tr(pX[:, kh * 128:(kh + 1) * 128], kc.ap(0, 128, kh * 128, [[1, 128]]), self.identB[:], [kc.b, self.identB.b], [pX.b])
            b.cp("dve", kTd.ap(0, 128, slot * 128, [[NR * 128, 4], [1, 128]]), pX.ap(0, 128, 0, [[128, 4], [1, 128]]), [pX.b], [kTd.b])
            yield

        def stageB(n):
            x_ = xs.pop(n)
            qTn = qT[n % 3]
            s_ = sg[n % 3]
            o_ = og[n % 2]
            blocks = [("c", 0, None), ("c", 1, None)]
            if n > 0:
                blocks.append(("w", (n - 1) % NR, 0))
            blocks.append(("w", n % NR, None))
            if n < NT - 1:
                blocks.append(("w", (n + 1) % NR, 1))
            rounds = [blocks[i:i + 2] for i in range(0, len(blocks), 2)]
            ptss = {}

            def QK(kh):
                pts = {}
                rt = 64 * (kh // 2)
                c4 = 4 * (kh % 2)
                for (kind, idx, mk) in blocks:
                    ps_ = self.next("pSt", pSt)
                    pt = self.next("PT", PT)
                    if kind == "c":
                        lhs = kcT[:, kh, idx, :]
                        lb = kcT.b
                    else:
                        lhs = kTd[:, kh, idx, :]
                        lb = kTd.b
                    b.mm(ps_[:], lhs, qTn[:, c4:c4 + 4, :], True, True, [lb, qTn.b], [ps_.b])
                    b.act(pt[:], ps_[:], AF.Exp, [ps_.b], [pt.b], scale=0.125)
                    if mk is not None:
                        b.tt(b.pick("mask_eng", ["dve", "pool"]), pt.ap(0, 128, 0, [[128, 4], [1, 128]]), pt.ap(0, 128, 0, [[128, 4], [1, 128]]),
                             maskb.ap(0, 128, mk * 128, [[0, 4], [1, 128]]), ALU.mult, [pt.b, maskb.b], [pt.b])
                    pts[(kind, idx)] = pt
                ptss[kh] = pts

            def PV(kh):
                pts = ptss[kh]
                po = pO[0]
                for hl in range(4):
                    for bi2, (kind, idx, mk) in enumerate(blocks):
                        pt = pts[(kind, idx)]
                        if kind == "c":
                            rhs = Vc[:, idx, kh, :]
                            rb = Vc.b
                        else:
                            rhs = Vp[:, idx, kh, :]
                            rb = Vp.b
                        b.mm(po[:, hl * 65:(hl + 1) * 65], pt[:, hl * 128:(hl + 1) * 128], rhs,
                             bi2 == 0, bi2 == len(blocks) - 1, [pt.b, rb], [po.b])
                dn = den[kh % 2]
                p.add("dve", lambda e, dn=dn, po=po, kh=kh: e.scalar_tensor_tensor(
                    out=dn[:, 0:4], in0=po.ap(0, 128, 64, [[65, 4]]), scalar=2.0, in1=esink[:, 4 * kh:4 * kh + 4],
                    op0=ALU.mult, op1=ALU.add), [po.b, esink.b], [dn.b])
                p.add("dve", lambda e, dn=dn: e.reciprocal(out=dn[:, 4:8], in_=dn[:, 0:4]), [dn.b], [dn.b])
                ot = on_t[kh % 2]
                b.tt("dve", ot.ap(0, 128, 0, [[64, 4], [1, 64]]), po.ap(0, 128, 0, [[65, 4], [1, 64]]),
                     dn.ap(0, 128, 4, [[1, 4], [0, 64]]), ALU.mult, [po.b, dn.b], [ot.b])
                b.tt("pool", o_[:, kh * 256:(kh + 1) * 256], ot[:], s_[:, kh * 256:(kh + 1) * 256], ALU.mult, [ot.b, s_.b], [o_.b])

            QK(0)
            yield
            QK(1)
            yield
            PV(0)
            yield
            QK(2)
            yield
            PV(1)
            yield
            QK(3)
            yield
            PV(2)
            yield
            PV(3)
            yield
            oT = ogT[n % 2]
            pX = self.next("pXr", self.pXs)
            for c in range(8):
                b.tr(pX[:, c * 128:(c + 1) * 128], o_[:, c * 128:(c + 1) * 128], self.identB[:], [o_.b, self.identB.b], [pX.b])
            b.cp("act", oT.ap(0, 128, 0, [[1, 1024]]), pX[:], [pX.b], [oT.b])
            yield
            for cb in range(2):
                pz = self.next("pzin", pZ[0:2])
                for k in range(8):
                    b.mm(pz[:], oT[:, k, :], wo1[:, k, cb * 512:(cb + 1) * 512], k == 0, k == 7, [oT.b, wo1.b], [pz.b])
                b.tt("dve", x_[:, cb * 512:(cb + 1) * 512], pz[:], x_[:, cb * 512:(cb + 1) * 512], ALU.add, [pz.b, x_.b], [x_.b])
            yield
            s2 = ss2[n % 2]
            b.sumsq(x_, s2, "act", junk1)
            b.rstd_from_ss(s2)
            p.add("dve", lambda e: e.scalar_tensor_tensor(out=x_[:], in0=x_[:], scalar=s2[:, 3:4], in1=fg[:],
                                                          op0=ALU.mult, op1=ALU.mult), [x_.b, s2.b, fg.b], [x_.b])
            p.dma("sp", d_out.ap()[n * 128:(n + 1) * 128, :], x_[:], reads=[x_.b])
            yield

        xs = {}
        nt_run = self.nt_l1 if self.nt_l1 is not None else NT
        nb_run = nt_run if nt_run == NT else nt_run - 1
        gA, gB = {}, {}

        def stepA(t):
            if 0 <= t < nt_run:
                if t not in gA:
                    gA[t] = stageA(t)
                next(gA[t], None)

        def stepB(n):
            if 0 <= n < nb_run:
                if n not in gB:
                    gB[n] = stageB(n)
                next(gB[n], None)

        order = "lbAbbAbbAbbAbaAobAnb"
        stepA(0)
        stepA(0)
        stepA(0)
        for t in range(nt_run + 3):
            for ch in order:
                if ch == "a" or ch == "l" or ch == "n":
                    stepA(t + 1)
                elif ch == "A":
                    stepA(t)
                elif ch == "o":
                    stepB(t - 3)
                else:
                    stepB(t - 2)


def build_l0(stop=None):
    B = Builder("l0")
    B.stop = stop
    d_x = B.din("x", [S, D])
    d_ctx = B.din("ctx", [LC, D])
    d_x1 = B.dout("x1", [S, D])
    d_ctx1 = B.dout("ctx1", [LC, D])
    B.setup_common()
    B.layer0(d_x, d_ctx, d_x1, d_ctx1)
    B.p.emit()
    print("l0 stats", B.p.stats)
    return B


_TB = None


L0_KEYS = ("x", "ctx", "cc", "ng", "ngrow", "ident", "ada_w", "ada_b", "w_in_ab", "w_out_ab", "v_norm_g", "spatial_w",
           "spatial_b", "tab1", "tab2", "fc", "tabc")
L1_KEYS = ("cc", "ng", "ngrow", "ident", "ada_w", "ada_b", "w_in_c", "w_out_c", "sink_logit", "final_g", "rope_cos",
           "rope_sin", "wmask")


def host_inputs(inputs):
    global _TB
    if _TB is None:
        _TB = _tables()
    f = lambda a: np.ascontiguousarray(np.asarray(a, dtype=np.float32))
    x, c, ctx, c_ctx = f(inputs["x"]), f(inputs["c"]), f(inputs["ctx"]), f(inputs["c_ctx"])
    norm_g = f(inputs["norm_g"])
    common = {
        "ident": _TB["ident"], "ada_w": f(inputs["ada_w"]), "ada_b": f(inputs["ada_b"]),
        "w_in_ab": f(inputs["w_in_ab"])[0], "w_out_ab": f(inputs["w_out_ab"])[0],
        "v_norm_g": f(inputs["v_norm_g"]).reshape(1, 512),
        "spatial_w": f(inputs["spatial_w"])[0], "spatial_b": f(inputs["spatial_b"]).reshape(1, 512),
        "tab1": _TB["tab1"], "tab2": _TB["tab2"], "fc": _TB["fc"], "tabc": _TB["tabc"],
        "w_in_c": f(inputs["w_in_c"])[0], "w_out_c": f(inputs["w_out_c"])[0],
        "sink_logit": f(inputs["sink_logit"]).reshape(1, 16), "final_g": f(inputs["final_g"]).reshape(1, D),
        "rope_cos": _TB["rope_cos"], "rope_sin": _TB["rope_sin"], "wmask": _TB["wmask"],
    }
    ng = np.concatenate([norm_g[0].reshape(8, 128).T, norm_g[1].reshape(8, 128).T], axis=1)
    maps = []
    for bi in range(x.shape[0]):
        cc = np.concatenate([c[bi].reshape(8, 128).T, c_ctx.reshape(8, 128).T], axis=1)
        m = dict(common)
        m.update({"x": x[bi], "ctx": ctx[bi], "cc": f(cc), "ng": f(ng), "ngrow": norm_g})
        maps.append(m)
    return maps


def build_l1(nt=None):
    B = Builder("l1")
    B.nt_l1 = nt
    d_x1 = B.din("x1", [S, D])
    d_ctx1 = B.din("ctx1", [LC, D])
    d_out = B.dout("out", [S, D])
    B.setup_common(l0=False)
    ar = Arena(B.p, "arL1", 36)
    B.layer1(d_x1, d_ctx1, d_out, ar)
    B.p.emit()
    print("l1 stats", B.p.stats)
    return B


def build_fused():
    B = Builder("fused")
    nc = B.nc
    d_x = B.din("x", [S, D])
    d_ctx = B.din("ctx", [LC, D])
    d_x1 = nc.dram_tensor("x1_scratch", [S, D], F32, kind="Internal")
    d_ctx1 = nc.dram_tensor("ctx1_scratch", [LC, D], F32, kind="Internal")
    d_out = B.dout("out", [S, D])
    B.setup_common(l0=True)
    main = Arena(B.p, "main", MAIN_KIB)
    B.cur_arena = main
    B.layer0(d_x, d_ctx, d_x1, d_ctx1)
    old = main.reset()
    B.l1_old = old
    ar = Arena(B.p, "arL1", 52, parent=main)
    B.layer1(d_x1, d_ctx1, d_out, ar)
    B.cur_arena = None
    B.p.emit()
    print("fused stats", B.p.stats)
    return B


MAIN_KIB = 196
ALL_KEYS = tuple(dict.fromkeys(L0_KEYS + L1_KEYS))
_PROGS = {}


def kernel(**inputs):
    maps = host_inputs(inputs)
    n = len(maps)
    if "fused" not in _PROGS:
        _PROGS["fused"] = build_fused()
    r = run_bass_kernel_spmd(_PROGS["fused"].nc, [{k: m[k] for k in ALL_KEYS} for m in maps], core_ids=list(range(n)))
    return np.stack([np.asarray(r.results[i]["out"], dtype=np.float32) for i in range(n)], axis=0)
```
